# Optimizing a Trainium2 kernel written in Bass

```python
import math
import jax, jax.numpy as jnp
from jax import lax
import numpy as np

D_MODEL = 1024
BATCH = 8
SEQ = 2048
DEPTH = 1
DEC_BATCH = 16
DEC_SEQ = 16
PAST_LEN = 2048

CHUNK = 64
EPS = 1e-6
LRU_WIDTH = 512
LRU_BLOCKS = 8
LRU_BLOCK_DIM = LRU_WIDTH // LRU_BLOCKS
CONV_W = 4
LRU_C = 8.0
N_HEADS = 8
N_KV_HEADS = 2
HEAD_DIM = 64
GQ = N_HEADS // N_KV_HEADS
ATTN_WIDTH = N_HEADS * HEAD_DIM
KV_WIDTH = N_KV_HEADS * HEAD_DIM
WINDOW = 128
WINDOW_CHUNKS = WINDOW // CHUNK
MIX_WIDTH = LRU_WIDTH + ATTN_WIDTH
IN_WIDTH = 2 * LRU_WIDTH + ATTN_WIDTH + 2 * KV_WIDTH
PEER_HEADS = 8
N_KEYS = 128
N_EXPERTS = N_KEYS * N_KEYS
PEER_TOPK = 16
D_QUERY = 256
D_HALF = D_QUERY // 2
PEER_BLOCK = 128

kernel_name = "hymba_rglru_swa_sink_peer_stream_step"


def rmsnorm(x, g):
    xf = x.astype(jnp.float32)
    xf = xf * lax.rsqrt(jnp.mean(xf * xf, axis=-1, keepdims=True) + EPS)
    return xf.astype(x.dtype) * g


def project_inputs(xn, w_in, q_g, k_g):
    B, T, _ = xn.shape
    z = xn @ w_in
    o1 = LRU_WIDTH
    o2 = o1 + LRU_WIDTH
    o3 = o2 + ATTN_WIDTH
    o4 = o3 + KV_WIDTH
    u = z[..., :o1]
    gate = z[..., o1:o2]
    q = rmsnorm(z[..., o2:o3].reshape(B, T, N_HEADS, HEAD_DIM), q_g)
    k = rmsnorm(z[..., o3:o4].reshape(B, T, N_KV_HEADS, HEAD_DIM), k_g)
    v = z[..., o4:].reshape(B, T, N_KV_HEADS, HEAD_DIM)
    return u, gate, q, k, v


def causal_conv(u, buf, w, b):
    T = u.shape[1]
    full = jnp.concatenate([buf, u], axis=1)
    out = b + sum(full[:, j:j + T] * w[j] for j in range(CONV_W))
    return out, full[:, -(CONV_W - 1):]


def rg_lru(xc, h0, wa, ba, wi, bi, lam):
    B, T, _ = xc.shape
    xb = xc.reshape(B, T, LRU_BLOCKS, LRU_BLOCK_DIM)
    r = jax.nn.sigmoid(jnp.einsum('btnd,nde->btne', xb, wa).reshape(B, T, LRU_WIDTH) + ba)
    ig = jax.nn.sigmoid(jnp.einsum('btnd,nde->btne', xb, wi).reshape(B, T, LRU_WIDTH) + bi)
    log_a = -LRU_C * r.astype(jnp.float32) * jax.nn.softplus(-lam.astype(jnp.float32))
    a = jnp.exp(log_a)
    drive = jnp.sqrt(-jnp.expm1(2.0 * log_a)) * (ig * xc).astype(jnp.float32)

    def combine(lhs, rhs):
        return (lhs[0] * rhs[0], rhs[0] * lhs[1] + rhs[1])

    a_cum, h = lax.associative_scan(combine, (a, drive), axis=1)
    h = h + a_cum * h0.astype(jnp.float32)[:, None, :]
    return h.astype(xc.dtype), h[:, -1].astype(h0.dtype)


def recurrent_group(u, gate, conv_buf, h0, conv_w, conv_b, wa, ba, wi, bi, lam):
    xc, new_buf = causal_conv(u, conv_buf, conv_w, conv_b)
    h, h_last = rg_lru(xc, h0, wa, ba, wi, bi, lam)
    return h * jax.nn.gelu(gate), new_buf, h_last


def sink_softmax(s, sink_b):
    m = jnp.maximum(jnp.max(s, axis=-1, keepdims=True), sink_b)
    p = jnp.exp(s - m)
    return p / (jnp.sum(p, axis=-1, keepdims=True) + jnp.exp(sink_b - m))


def attention_prompt(q, k, v, sinks):
    B, S = q.shape[:2]
    NC = S // CHUNK
    KB = (WINDOW_CHUNKS + 1) * CHUNK
    qc = q.reshape(B, NC, CHUNK, N_KV_HEADS, GQ, HEAD_DIM)

    def band(t):
        tc = t.reshape(B, NC, CHUNK, N_KV_HEADS, HEAD_DIM)
        tp = jnp.concatenate([jnp.zeros_like(tc[:, :WINDOW_CHUNKS]), tc], axis=1)
        return jnp.concatenate([tp[:, j:j + NC] for j in range(WINDOW_CHUNKS + 1)], axis=2)

    kb, vb = band(k), band(v)
    key_chunk = jnp.arange(NC)[:, None] - WINDOW_CHUNKS + jnp.arange(KB)[None, :] // CHUNK
    valid = (key_chunk >= 0)[None, :, None, None, None, :]
    s = jnp.einsum('bncgrd,bnkgd->bngrck', qc, kb).astype(jnp.float32) * (HEAD_DIM ** -0.5)
    s = jnp.where(valid, s, -jnp.inf)
    sink_b = sinks.astype(jnp.float32).reshape(N_KV_HEADS, GQ)[None, None, :, :, None, None]
    p = sink_softmax(s, sink_b).astype(v.dtype)
    o = jnp.einsum('bngrck,bnkgd->bncgrd', p, vb).reshape(B, S, ATTN_WIDTH)
    return o, k[:, -WINDOW:], v[:, -WINDOW:]


def attention_sample(q, k, v, cache_k, cache_v, sinks):
    B, T = q.shape[:2]
    kf = jnp.concatenate([cache_k, k], axis=1)
    vf = jnp.concatenate([cache_v, v], axis=1)
    qg = q.reshape(B, T, N_KV_HEADS, GQ, HEAD_DIM)
    s = jnp.einsum('btgrd,bkgd->bgrtk', qg, kf).astype(jnp.float32) * (HEAD_DIM ** -0.5)
    sink_b = sinks.astype(jnp.float32).reshape(N_KV_HEADS, GQ)[None, :, :, None, None]
    p = sink_softmax(s, sink_b).astype(v.dtype)
    o = jnp.einsum('bgrtk,bkgd->btgrd', p, vf).reshape(B, T, ATTN_WIDTH)
    return o, kf[:, -WINDOW:], vf[:, -WINDOW:]


def merge_groups(x, lru_out, attn_out, g_lru, g_attn, w_out):
    cat = jnp.concatenate([rmsnorm(lru_out, g_lru), rmsnorm(attn_out, g_attn)], axis=-1)
    return x + cat @ w_out


def peer(xn, w_query, sub_keys1, sub_keys2, expert_u, expert_v):
    T = xn.shape[0]
    q = (xn @ w_query).reshape(T, PEER_HEADS, 2, D_HALF)
    s1 = jnp.einsum('thd,kd->thk', q[:, :, 0], sub_keys1).astype(jnp.float32)
    s2 = jnp.einsum('thd,kd->thk', q[:, :, 1], sub_keys2).astype(jnp.float32)
    v1, i1 = lax.top_k(s1, PEER_TOPK)
    v2, i2 = lax.top_k(s2, PEER_TOPK)
    cand_s = (v1[..., :, None] + v2[..., None, :]).reshape(T, PEER_HEADS, PEER_TOPK * PEER_TOPK)
    cand_i = (i1[..., :, None] * N_KEYS + i2[..., None, :]).reshape(T, PEER_HEADS, PEER_TOPK * PEER_TOPK)
    top_s, top_pos = lax.top_k(cand_s, PEER_TOPK)
    idx = jnp.take_along_axis(cand_i, top_pos, axis=-1)
    gates = jax.nn.softmax(top_s, axis=-1)
    u = expert_u[idx]
    act = jax.nn.gelu(jnp.einsum('thkd,td->thk', u, xn).astype(jnp.float32))
    return jnp.einsum('thk,thkd->td', (gates * act).astype(xn.dtype), expert_v[idx])


def setup_inputs(seed: int = 0) -> dict:
    key = jax.random.key(seed)
    ks = jax.random.split(key, 32)
    f = jnp.float32
    nrm = lambda k, shape, sc: jax.random.normal(k, shape, f) * sc
    cache_win = min(WINDOW, PAST_LEN)
    a8 = jax.random.uniform(ks[12], (DEPTH, LRU_WIDTH), f, minval=0.9, maxval=0.999)
    a_base = a8 ** (1.0 / LRU_C)
    lam = jnp.log(a_base) - jnp.log1p(-a_base)
    return {
        "x_prompt": nrm(ks[0], (BATCH, SEQ, D_MODEL), 1.0),
        "x_sample": nrm(ks[1], (DEC_BATCH, DEC_SEQ, D_MODEL), 1.0),
        "cache_attn_k": nrm(ks[2], (DEPTH, DEC_BATCH, cache_win, N_KV_HEADS, HEAD_DIM), 1.0),
        "cache_attn_v": nrm(ks[3], (DEPTH, DEC_BATCH, cache_win, N_KV_HEADS, HEAD_DIM), 1.0),
        "state_conv": nrm(ks[4], (DEPTH, DEC_BATCH, CONV_W - 1, LRU_WIDTH), 1.0),
        "state_lru": nrm(ks[5], (DEPTH, DEC_BATCH, LRU_WIDTH), 0.5),
        "ln1_g": 1.0 + nrm(ks[6], (DEPTH, D_MODEL), 0.02),
        "w_in": nrm(ks[7], (DEPTH, D_MODEL, IN_WIDTH), D_MODEL ** -0.5),
        "conv_w": nrm(ks[8], (DEPTH, CONV_W, LRU_WIDTH), CONV_W ** -0.5),
        "conv_b": nrm(ks[9], (DEPTH, LRU_WIDTH), 0.01),
        "lru_wa": nrm(ks[10], (DEPTH, LRU_BLOCKS, LRU_BLOCK_DIM, LRU_BLOCK_DIM), LRU_BLOCK_DIM ** -0.5),
        "lru_ba": nrm(ks[11], (DEPTH, LRU_WIDTH), 0.01),
        "lru_wi": nrm(ks[13], (DEPTH, LRU_BLOCKS, LRU_BLOCK_DIM, LRU_BLOCK_DIM), LRU_BLOCK_DIM ** -0.5),
        "lru_bi": nrm(ks[14], (DEPTH, LRU_WIDTH), 0.01),
        "lru_lambda": lam,
        "q_norm_g": 1.0 + nrm(ks[15], (DEPTH, HEAD_DIM), 0.02),
        "k_norm_g": 1.0 + nrm(ks[16], (DEPTH, HEAD_DIM), 0.02),
        "attn_sinks": nrm(ks[17], (DEPTH, N_HEADS), 0.5),
        "g_lru_out": 1.0 + nrm(ks[18], (DEPTH, LRU_WIDTH), 0.02),
        "g_attn_out": 1.0 + nrm(ks[19], (DEPTH, ATTN_WIDTH), 0.02),
        "w_out": nrm(ks[20], (DEPTH, MIX_WIDTH, D_MODEL), MIX_WIDTH ** -0.5),
        "ln2_g": 1.0 + nrm(ks[21], (DEPTH, D_MODEL), 0.02),
        "peer_w_query": nrm(ks[22], (DEPTH, D_MODEL, PEER_HEADS * D_QUERY), D_MODEL ** -0.5),
        "peer_sub_keys1": nrm(ks[23], (DEPTH, N_KEYS, D_HALF), D_HALF ** -0.5),
        "peer_sub_keys2": nrm(ks[24], (DEPTH, N_KEYS, D_HALF), D_HALF ** -0.5),
        "peer_u": nrm(ks[25], (DEPTH, N_EXPERTS, D_MODEL), D_MODEL ** -0.5),
        "peer_v": nrm(ks[26], (DEPTH, N_EXPERTS, D_MODEL), 0.1),
    }


def reference(x_prompt, x_sample, cache_attn_k, cache_attn_v, state_conv, state_lru,
              ln1_g, w_in, conv_w, conv_b, lru_wa, lru_ba, lru_wi, lru_bi, lru_lambda,
              q_norm_g, k_norm_g, attn_sinks, g_lru_out, g_attn_out, w_out, ln2_g,
              peer_w_query, peer_sub_keys1, peer_sub_keys2, peer_u, peer_v):
    y_p, y_s = x_prompt, x_sample
    Bp, Sp = y_p.shape[:2]
    Bs, Ts = y_s.shape[:2]
    nk_p, nv_p, nc_p, nh_p = [], [], [], []
    nk_s, nv_s, nc_s, nh_s = [], [], [], []
    for l in range(DEPTH):
        lru_args = (conv_w[l], conv_b[l], lru_wa[l], lru_ba[l], lru_wi[l], lru_bi[l], lru_lambda[l])
        peer_args = (peer_w_query[l], peer_sub_keys1[l], peer_sub_keys2[l], peer_u[l], peer_v[l])

        u, gate, q, k, v = project_inputs(rmsnorm(y_p, ln1_g[l]), w_in[l], q_norm_g[l], k_norm_g[l])
        zero_buf = jnp.zeros((Bp, CONV_W - 1, LRU_WIDTH), y_p.dtype)
        zero_h = jnp.zeros((Bp, LRU_WIDTH), y_p.dtype)
        lru_out, cbuf, h_last = recurrent_group(u, gate, zero_buf, zero_h, *lru_args)
        attn_out, kbuf, vbuf = attention_prompt(q, k, v, attn_sinks[l])
        y_p = merge_groups(y_p, lru_out, attn_out, g_lru_out[l], g_attn_out[l], w_out[l])
        xn2 = rmsnorm(y_p, ln2_g[l]).reshape(-1, PEER_BLOCK, D_MODEL)
        ffn = lax.map(lambda blk: peer(blk, *peer_args), xn2)
        y_p = y_p + ffn.reshape(Bp, Sp, D_MODEL)
        nk_p.append(kbuf); nv_p.append(vbuf); nc_p.append(cbuf); nh_p.append(h_last)

        u, gate, q, k, v = project_inputs(rmsnorm(y_s, ln1_g[l]), w_in[l], q_norm_g[l], k_norm_g[l])
        lru_out, cbuf, h_last = recurrent_group(u, gate, state_conv[l], state_lru[l], *lru_args)
        attn_out, kbuf, vbuf = attention_sample(q, k, v, cache_attn_k[l], cache_attn_v[l], attn_sinks[l])
        y_s = merge_groups(y_s, lru_out, attn_out, g_lru_out[l], g_attn_out[l], w_out[l])
        xn2 = rmsnorm(y_s, ln2_g[l]).reshape(Bs * Ts, D_MODEL)
        y_s = y_s + peer(xn2, *peer_args).reshape(Bs, Ts, D_MODEL)
        nk_s.append(kbuf); nv_s.append(vbuf); nc_s.append(cbuf); nh_s.append(h_last)

    return (y_p, y_s,
            jnp.stack(nk_p), jnp.stack(nv_p), jnp.stack(nc_p), jnp.stack(nh_p),
            jnp.stack(nk_s), jnp.stack(nv_s), jnp.stack(nc_s), jnp.stack(nh_s))
```

```python
import numpy as np
import concourse.bass as bass
import concourse.mybir as mybir
from concourse.bass_utils import run_bass_kernel_spmd

F32 = mybir.dt.float32
BF16 = mybir.dt.bfloat16
U32 = mybir.dt.uint32
I32 = mybir.dt.int32
AF = mybir.ActivationFunctionType
ALU = mybir.AluOpType
AX = mybir.AxisListType

SEM_LIMIT = 30000
SKIP_SAME_RAW = ()
STRICT = 1


class Res:
    __slots__ = ("name", "last_w", "reads", "dsem_w", "dsem_r", "dcnt_w", "dcnt_r")

    def __init__(self, name):
        self.name = name
        self.last_w = None
        self.reads = []
        self.dsem_w = None
        self.dsem_r = None
        self.dcnt_w = 0
        self.dcnt_r = 0


class Buf:
    def __init__(self, C, name, shape, dtype, psum=False, stack=None):
        self.name = name
        if psum:
            self.t = C.nc.alloc_psum_tensor(name, list(shape), dtype)
        elif stack is not None:
            self.t = stack.enter_context(C.nc.sbuf_tensor(name, list(shape), dtype))
        else:
            self.t = C.nc.alloc_sbuf_tensor(name, list(shape), dtype)
        self.res = Res(name)
        if stack is not None:
            stack.callback(C.release_res, self.res)
        self.shape = list(shape)
        self.dtype = dtype

    def __getitem__(self, k):
        return self.t[k]

    def ap(self, offset, dims):
        fs = 1
        for s in self.shape[1:]:
            fs *= s
        return bass.AP(self.t, offset, [[fs, self.shape[0]]] + [list(d) for d in dims])

    def pap(self, p0, pn, offset, dims):
        fs = 1
        for s in self.shape[1:]:
            fs *= s
        return bass.AP(self.t, p0 * fs + offset, [[fs, pn]] + [list(d) for d in dims])


def _ap_n(ap):
    try:
        sh = list(ap.shape)
        n = 1
        for s_ in sh[1:]:
            n *= int(s_)
        return int(sh[0]), n
    except Exception:
        return 128, 128


_DT_SIZE = {}


def _dsize(ap):
    try:
        d = ap.dtype
        if d == BF16:
            return 2
        return 4
    except Exception:
        return 4


class EngProxy:
    def __init__(self, eng):
        self._eng = eng
        self.name = None
        self.n = 128
        self.passes = 1
        self.nbytes = 0

    def __getattr__(self, name):
        real = getattr(self._eng, name)

        def call(*a, **kw):
            self.name = name
            out = kw.get("out", a[0] if a else None)
            if name in ("matmul", "transpose"):
                rhs = kw.get("rhs", a[2] if len(a) > 2 else None)
                if name == "transpose":
                    src = kw.get("in_", a[1] if len(a) > 1 else None)
                    p_, n_ = _ap_n(src)
                    self.n = p_
                else:
                    p_, n_ = _ap_n(rhs)
                    self.n = n_
                    self.passes = 4 if _dsize(rhs) == 4 else 1
            elif name in ("max", "max_index", "match_replace"):
                src = kw.get("in_", kw.get("in_values", None))
                p_, n_ = _ap_n(src)
                self.n = n_
            elif name == "tensor_reduce":
                p_, n_ = _ap_n(kw.get("in_"))
                self.n = n_
            elif out is not None:
                p_, n_ = _ap_n(out)
                self.n = n_
                self.nbytes = p_ * n_ * _dsize(out)
            return real(*a, **kw)

        return call


def op_cost(e, name, n, passes):
    if e == "pe":
        return 0.11 + n * passes * 0.00032
    if e == "dve":
        return 0.07 + n * 0.00105
    if e == "act":
        return 0.2 + n * 0.00088
    if e == "pool":
        return 0.3 + n * 0.0023
    return 0.1


class Ctx:
    def __init__(self, nc):
        self.nc = nc
        self.eng = {"pe": nc.tensor, "act": nc.scalar, "dve": nc.vector,
                    "pool": nc.gpsimd, "sp": nc.sync}
        self.esem = {}
        self.ecnt = {}
        self.waited = {e: {} for e in self.eng}
        self.semn = 0
        self.ninst = 0
        self.final_tokens = []
        self.sem_pool = []
        self.pending = []
        self.hook_thread = None
        self.pump = None
        self.in_pump = False
        self.eng_time = {e: 0.0 for e in self.eng}
        self.tok_time = {}
        self.dma_free = 0.0
        self.front_time = 0.0
        self.log = None
        self.ninst_f = 0

    def release_res(self, res):
        if res.dsem_w is not None and res.dcnt_w < SEM_LIMIT:
            self.sem_pool.append((res.dsem_w, res.dcnt_w))
        if res.dsem_r is not None and res.dcnt_r < SEM_LIMIT:
            self.sem_pool.append((res.dsem_r, res.dcnt_r))
        res.dsem_w = res.dsem_r = None

    def get_dsem(self, name):
        if self.sem_pool:
            return self.sem_pool.pop()
        return (self.newsem(name), 0)

    def barrier(self):
        engs = list(self.eng)
        for e in engs:
            for e2 in list(self.esem):
                if e2 != e:
                    self._wait(e, (self.esem[e2], self.ecnt[e2], e2))
            d = {}
            for t in self.pending:
                k = id(t[0])
                if k not in d or d[k][1] < t[1]:
                    d[k] = t
            for t in d.values():
                self._wait(e, t)
        self.pending = []

    def newsem(self, name):
        self.semn += 1
        return self.nc.alloc_semaphore(name=f"{name}_{self.semn}")

    def buf(self, name, shape, dtype, psum=False):
        return Buf(self, name, shape, dtype, psum)

    def _next_tok(self, e):
        if e not in self.esem or self.ecnt[e] >= SEM_LIMIT:
            self.esem[e] = self.newsem("e" + e)
            self.ecnt[e] = 0
        self.ecnt[e] += 1
        return (self.esem[e], self.ecnt[e], e)

    def _wait(self, e, tok):
        sem, val, src = tok
        key = id(sem)
        w = self.waited[e]
        if w.get(key, (None, 0))[1] >= val:
            return False
        self.eng[e].wait_ge(sem, val)
        w[key] = (sem, val)
        return True

    def _dep_tokens(self, e, r, w, same_engine_raw=True):
        toks = []
        for b in r:
            res = b.res if hasattr(b, 'res') else b
            if res.last_w is not None:
                t = res.last_w
                if t[2] == e and (e == "pe" or not same_engine_raw or e in SKIP_SAME_RAW):
                    continue
                toks.append(t)
        strict = STRICT and e != "pe"
        for b in w:
            res = b.res if hasattr(b, 'res') else b
            if res.last_w is not None:
                t = res.last_w
                if strict or not (t[2] == e):
                    toks.append(t)
            for t in res.reads:
                if t[2] == e and not strict:
                    continue
                toks.append(t)
        return toks

    def pred_ready(self, e, r, w):
        t_ = 0.0
        for tk in self._dep_tokens(e, r, w):
            t_ = max(t_, self.tok_time.get((id(tk[0]), tk[1]), 0.0))
        return t_

    def _deps(self, e, r, w, same_engine_raw=True):
        toks = self._dep_tokens(e, r, w, same_engine_raw)
        t_ = 0.0
        nw = 0
        for t in toks:
            t_ = max(t_, self.tok_time.get((id(t[0]), t[1]), 0.0))
            if self._wait(e, t):
                nw += 1
        return t_, nw

    def _commit(self, tok, r, w):
        for b in w:
            res = b.res if hasattr(b, 'res') else b
            res.last_w = tok
            res.reads = []
        for b in r:
            res = b.res if hasattr(b, 'res') else b
            if res.last_w is tok:
                continue
            res.reads.append(tok)
            if len(res.reads) > 64:
                d = {}
                for t in res.reads:
                    k = id(t[0])
                    if k not in d or d[k][1] < t[1]:
                        d[k] = t
                res.reads = list(d.values())

    def _hook(self, e=None, r=(), w=()):
        ht = self.hook_thread
        import threading as _th
        if ht is not None and _th.current_thread() is ht[0]:
            ht[1](e, r, w)
        elif self.pump is not None and not self.in_pump:
            self.in_pump = True
            try:
                self.pump()
            finally:
                self.in_pump = False

    def mark(self, name="MARK"):
        self._hook(name)

    def _is_main(self):
        ht = self.hook_thread
        if ht is None:
            return True
        import threading as _th
        return _th.current_thread() is not ht[0]

    def op(self, e, fn, r=(), w=()):
        self._hook(e, r, w)
        t_ready, nw = self._deps(e, r, w)
        tok = self._next_tok(e)
        px = EngProxy(self.eng[e])
        ins = fn(px)
        ins.then_inc(tok[0], 1)
        self._commit(tok, r, w)
        self.ninst += 1
        start = max(self.eng_time[e], t_ready) + (0.08 if nw else 0.0)
        end = start + op_cost(e, px.name, px.n, px.passes)
        self.eng_time[e] = end
        self.tok_time[(id(tok[0]), tok[1])] = end + 0.06
        if self.log is not None:
            self.log.append(("M" if self._is_main() else "F", e, px.name, px.n, round(t_ready, 2), round(start, 2), round(end, 2)))
        if self._is_main():
            self.front_time = max(self.front_time, start)
        else:
            self.ninst_f += 1
        return tok

    def dma(self, q, out, in_, r=(), w=(), sbuf_side=None, is_store=False, fn=None, final=False, temp_store=False):
        self._hook(q, r, w)
        t_ready, nw = self._deps(q, r, w, same_engine_raw=True)
        res = sbuf_side.res if hasattr(sbuf_side, 'res') else sbuf_side
        if is_store:
            if res.dsem_r is None:
                res.dsem_r, res.dcnt_r = self.get_dsem("dr")
            res.dcnt_r += 16
            tok = (res.dsem_r, res.dcnt_r, "dma")
        else:
            if res.dsem_w is None:
                if q == "pool":
                    res.dsem_w, res.dcnt_w = self.newsem("dg"), 0
                else:
                    res.dsem_w, res.dcnt_w = self.get_dsem("dw")
            res.dcnt_w += 16
            tok = (res.dsem_w, res.dcnt_w, "dma")
        px = EngProxy(self.eng[q])
        if fn is None:
            ins = px.dma_start(out=out, in_=in_)
        else:
            ins = fn(px)
        ins.then_inc(tok[0], 16)
        self._commit(tok, r, w)
        self.ninst += 1
        if final:
            self.final_tokens.append(tok)
        if temp_store:
            self.pending.append(tok)
        start = max(self.eng_time[q], t_ready) + (0.08 if nw else 0.0)
        issue = 1.1 if px.name == "indirect_dma_start" else 0.1
        self.eng_time[q] = start + issue
        xfer = px.nbytes / 3.0e5
        s2 = max(start + issue, self.dma_free)
        self.dma_free = s2 + xfer
        self.tok_time[(id(tok[0]), tok[1])] = s2 + xfer + 2.0
        if self._is_main():
            self.front_time = max(self.front_time, start)
        return tok

    def finish(self):
        for t in self.final_tokens:
            self._wait("sp", t)
        for e in self.esem:
            self._wait("sp", (self.esem[e], self.ecnt[e], e))


import contextlib
import threading

EPS = 1e-6
NEG = -1.0e30
R_CW, R_CB, R_BA, R_BI, R_LAM, R_GL, R_SC, R_SL, R_GA, NROW = 0, 4, 5, 6, 7, 8, 9, 15, 17, 18


class PS:
    def __init__(self, C, name):
        self.b = C.buf(name, [128, 1024], F32, psum=True)
        self.t = self.b.t
        self.ra = Res(name + "a")
        self.rb = Res(name + "b")
        self.bt = self.t[:, :].bitcast(BF16)

    def f(self, p0, pn, off, dims):
        return bass.AP(self.t, p0 * 1024 + off, [[1024, pn]] + [list(d) for d in dims])

    def bf(self, p0, pn, off, dims):
        return bass.AP(self.bt.tensor, self.bt.offset + p0 * 2048 + off, [[2048, pn]] + [list(d) for d in dims])

    def r(self, half):
        return self.ra if half == 0 else self.rb

    @property
    def both(self):
        return [self.ra, self.rb]


class VBuf:
    def __init__(self, arena, f32_off, shape, dtype, name):
        esz = 2 if dtype == BF16 else 4
        n = 1
        for s_ in shape[1:]:
            n *= s_
        nf32 = (n * esz + 3) // 4
        self.nf32 = nf32
        base = arena.t[:, f32_off:f32_off + nf32]
        if dtype != F32:
            base = base.bitcast(dtype)
        self.base = base
        self.pstep = base.ap[0][0]
        self.off0 = base.offset
        self.tensor = base.tensor
        self.shape = list(shape)
        self.n = n
        self.res = Res(name)
        self.name = name
        if len(shape) == 2:
            self.v = base[:, 0:n] if n != base.shape[1] else base
        elif len(shape) == 3:
            self.v = base[:, 0:n].rearrange("p (a b) -> p a b", a=shape[1], b=shape[2])
        else:
            raise ValueError

    def __getitem__(self, k):
        return self.v[k]

    def pap(self, p0, pn, off, dims):
        return bass.AP(self.tensor, self.off0 + p0 * self.pstep + off, [[self.pstep, pn]] + [list(d) for d in dims])

    def ap(self, off, dims):
        return self.pap(0, self.shape[0], off, dims)


class Stepper:
    def __init__(self, C, fn):
        self.C = C
        self.go = threading.Semaphore(0)
        self.back = threading.Semaphore(0)
        self.finished = False
        self.err = None
        self.next_e = None
        self.at_mark = False

        def run():
            self.go.acquire()
            try:
                fn()
            except BaseException as ex:
                self.err = ex
            self.finished = True
            self.C.hook_thread = None
            self.back.release()

        self.th = threading.Thread(target=run)
        self.th.start()

    def hook(self, e=None, r=(), w=()):
        self.next_e = e
        self.next_rw = (r, w)
        if e == "MARK":
            self.at_mark = True
        self.back.release()
        self.go.acquire()

    def step(self, n=1):
        for _ in range(n):
            if self.finished:
                break
            self.C.hook_thread = (self.th, self.hook)
            self.go.release()
            self.back.acquire()
            self.C.hook_thread = None
        if self.err is not None:
            raise self.err

    def run_to_mark(self):
        while not self.finished and not self.at_mark:
            self.step(1)

    def drain(self):
        while not self.finished:
            self.step(64)
        self.th.join()
        if self.err is not None:
            raise self.err


def build_program():
    global SKIP_SAME_RAW
    SKIP_SAME_RAW = {0: (), 1: ("dve",), 2: ("dve", "act"), 3: ("dve", "act", "pool")}[SKIPRAW]
    nc = bass.Bass("TRN2", target_bir_lowering=False)
    C = Ctx(nc)
    if DEBUG:
        C.log = []
        DBG["C"] = C
    cnt = [0]

    def DI(name, shape, dt=F32):
        return nc.dram_tensor(name, list(shape), dt, kind="ExternalInput")

    def DO(name, shape, dt=F32):
        return nc.dram_tensor(name, list(shape), dt, kind="ExternalOutput")

    xp = DI("xp", [2048, 1024]); xs = DI("xs", [32, 1024])
    ck = DI("ck", [2, 128, 128]); cv = DI("cv", [2, 128, 128])
    v512 = DI("v512", [NROW, 512])
    ln1 = DI("ln1", [1, 1024]); ln2 = DI("ln2", [1, 1024])
    qg = DI("qg", [1, 64]); kg = DI("kg", [1, 64]); snk = DI("snk", [1, 8])
    w_in = DI("w_in", [1024, 1792]); w_out = DI("w_out", [1024, 1024]); w_q = DI("w_q", [1024, 2048])
    wa = DI("wa", [8, 64, 64]); wi = DI("wi", [8, 64, 64])
    sk1 = DI("sk1", [128, 128]); sk2 = DI("sk2", [128, 128])
    pu = DI("pu", [16384, 1024]); pv = DI("pv", [16384, 1024])
    yp = DO("yp", [2048, 1024]); ys = DO("ys", [32, 1024])
    nkp = DO("nkp", [128, 128]); nvp = DO("nvp", [128, 128]); ncp = DO("ncp", [3, 512]); nhp = DO("nhp", [1, 512])
    nks = DO("nks", [2, 128, 128]); nvs = DO("nvs", [2, 128, 128]); ncs = DO("ncs", [2, 3, 512]); nhs = DO("nhs", [2, 512])
    tab = nc.dram_tensor("tab", [16384, 2048], BF16, kind="Internal")

    def dap(t, off, dims):
        return bass.AP(t, off, [list(d) for d in dims])

    def B(name, shape, dt, stack=None):
        cnt[0] += 1
        return Buf(C, f"{name}_{cnt[0]}", shape, dt, stack=stack)

    def barrier():
        C.barrier()

    P = [PS(C, f"P{i}") for i in range(4)]
    identf = B("identf", [128, 128], F32); identb = B("identb", [128, 128], BF16); ones_f = B("ones_f", [128, 128], F32)
    wi_bf = B("wi_bf", [128, 8, 1792], BF16); wo_bf = B("wo_bf", [128, 8, 1024], BF16); wq_bf = B("wq_bf", [128, 8, 2048], BF16)
    wi_r = [Res(f"wi{k}") for k in range(8)]; wo_r = [Res(f"wo{k}") for k in range(8)]; wq_r = [Res(f"wq{k}") for k in range(8)]
    wa_bd = B("wa_bd", [128, 4, 128], BF16); wi_bd = B("wi_bd", [128, 4, 128], BF16)
    skT = [B("skT0", [128, 128], BF16), B("skT1", [128, 128], BF16)]
    vecT = B("vecT", [128, 4, NROW], F32); gT8 = B("gT8", [128, 2, 8], F32)
    clam = B("clam", [128, 4], F32); nclam = B("nclam", [128, 4], F32)
    qgrep = B("qgrep", [128, 64], F32); kgrep = B("kgrep", [128, 64], F32); esink = B("esink", [128, 8], F32)
    uext = B("uext", [128, 4, 131], F32); hstate = B("hstate", [128, 4], F32)
    kTb = [B("kT0", [64, 2, 128], BF16), B("kT1", [64, 2, 128], BF16)]
    Vaug = [B("Va0", [128, 2, 65], BF16), B("Va1", [128, 2, 65], BF16)]
    PT = {(g, nm): B(f"PT{g}{nm}", [128, 4, 128], BF16) for g in range(2) for nm in ("prev", "own")}
    Zc = B("Zc", [128, 256], BF16)
    iota16 = B("iota16", [128, 16], F32)

    C.op("pool", lambda e: e.memset(ones_f[:], 1.0), w=[ones_f])
    C.op("pool", lambda e: e.affine_select(out=identf[:], in_=ones_f[:], pattern=[[-1, 128]], compare_op=ALU.is_equal,
                                           fill=0.0, base=0, channel_multiplier=1), r=[ones_f], w=[identf])
    C.op("dve", lambda e: e.tensor_copy(out=identb[:], in_=identf[:]), r=[identf], w=[identb])
    for b_ in Vaug:
        C.op("pool", lambda e: e.memset(b_[:], 1.0), w=[b_])
    for b_ in PT.values():
        C.op("pool", lambda e: e.memset(b_[:], 0.0), w=[b_])
    C.op("pool", lambda e: e.memset(uext[:], 0.0), w=[uext])
    C.op("pool", lambda e: e.memset(hstate[:], 0.0), w=[hstate])
    C.op("pool", lambda e: e.memset(Zc[:], 0.0), w=[Zc])
    C.op("pool", lambda e: e.memset(Zc[:, 127:128], 1.0), w=[Zc])
    C.op("pool", lambda e: e.iota(iota16[:], pattern=[[1, 16]], base=0, channel_multiplier=0, allow_small_or_imprecise_dtypes=True), w=[iota16])

    with contextlib.ExitStack() as st:
        v_sb = B("v_sb", [32, 512], F32, st)
        C.dma("sp", v_sb[0:NROW, :], v512.ap(), w=[v_sb], sbuf_side=v_sb)
        for ct in range(4):
            C.op("pe", lambda e: e.transpose(P[0].f(0, 128, 512 + ct * 32, [[1, NROW]]), v_sb[0:NROW, ct * 128:(ct + 1) * 128], identf[0:NROW, 0:NROW]),
                 r=[v_sb, identf], w=[P[0].rb])
        C.op("act", lambda e: e.activation(out=vecT[:], in_=P[0].f(0, 128, 512, [[32, 4], [1, NROW]]), func=AF.Copy), r=[P[0].rb], w=[vecT])
        g_sb = B("g_sb", [16, 128], F32, st)
        C.dma("sp", g_sb[0:8, :], dap(ln1, 0, [[128, 8], [1, 128]]), w=[g_sb], sbuf_side=g_sb)
        C.dma("sp", g_sb[8:16, :], dap(ln2, 0, [[128, 8], [1, 128]]), w=[g_sb], sbuf_side=g_sb)
        C.op("pe", lambda e: e.transpose(P[0].f(0, 128, 0, [[1, 16]]), g_sb[0:16, :], identf[0:16, 0:16]), r=[g_sb, identf], w=[P[0].ra])
        C.op("act", lambda e: e.activation(out=gT8[:], in_=P[0].f(0, 128, 0, [[8, 2], [1, 8]]), func=AF.Copy), r=[P[0].ra], w=[gT8])
        stg = [B("stg0", [128, 2048], F32, st), B("stg1", [128, 2048], F32, st), B("stg2", [128, 2048], F32, st)]
        jobs = []
        for kc in range(8):
            jobs.append((dap(w_in, kc * 128 * 1792, [[1792, 128], [1, 1792]]), wi_bf[:, kc, :], 1792, wi_r[kc], gT8[:, 0, kc:kc + 1], [gT8]))
        for kc in range(8):
            sc_ = vecT[:, kc, R_GL:R_GL + 1] if kc < 4 else vecT[:, kc - 4, R_GA:R_GA + 1]
            jobs.append((dap(w_out, kc * 128 * 1024, [[1024, 128], [1, 1024]]), wo_bf[:, kc, :], 1024, wo_r[kc], sc_, [vecT]))
        for kc in range(8):
            jobs.append((dap(w_q, kc * 128 * 2048, [[2048, 128], [1, 2048]]), wq_bf[:, kc, :], 2048, wq_r[kc], gT8[:, 1, kc:kc + 1], [gT8]))
        for j, (src, dst, n, rr, scl, sr) in enumerate(jobs):
            s_ = stg[j % 3]
            C.dma("sp", s_[:, 0:n], src, w=[s_], sbuf_side=s_)
            en = ("act", "dve", "pool")[j % 3]
            if en == "act":
                C.op("act", lambda e: e.activation(out=dst, in_=s_[:, 0:n], func=AF.Copy, scale=scl), r=[s_] + sr, w=[rr])
            else:
                C.op(en, lambda e: e.tensor_scalar(out=dst, in0=s_[:, 0:n], scalar1=scl, scalar2=None, op0=ALU.mult), r=[s_] + sr, w=[rr])
        for (src_t, dstb) in ((wa, wa_bd), (wi, wi_bd)):
            sw = B("stgw", [128, 4, 128], F32, st)
            C.op("pool", lambda e: e.memset(sw[:], 0.0), w=[sw])
            C.dma("sp", sw.pap(0, 64, 0, [[128, 4], [1, 64]]), dap(src_t, 0, [[64, 64], [8192, 4], [1, 64]]), w=[sw], sbuf_side=sw)
            C.dma("sp", sw.pap(64, 64, 64, [[128, 4], [1, 64]]), dap(src_t, 4096, [[64, 64], [8192, 4], [1, 64]]), w=[sw], sbuf_side=sw)
            C.op("dve", lambda e: e.tensor_copy(out=dstb[:], in_=sw[:]), r=[sw], w=[dstb])
        for i_, src_t in enumerate((sk1, sk2)):
            sk_sb = B("sk_sb", [128, 128], F32, st)
            C.dma("sp", sk_sb[:], src_t.ap(), w=[sk_sb], sbuf_side=sk_sb)
            C.op("pe", lambda e: e.transpose(P[0].f(0, 128, i_ * 128, [[1, 128]]), sk_sb[:], identf[:]), r=[sk_sb, identf], w=[P[0].ra])
            C.op("act", lambda e: e.activation(out=skT[i_][:], in_=P[0].f(0, 128, i_ * 128, [[1, 128]]), func=AF.Copy), r=[P[0].ra], w=[skT[i_]])
        e1 = B("e1", [128, 4], F32, st)
        C.op("act", lambda e: e.activation(out=e1[:], in_=vecT[:, :, R_LAM], func=AF.Exp, scale=-1.0), r=[vecT], w=[e1])
        C.op("act", lambda e: e.activation(out=e1[:], in_=e1[:], func=AF.Ln, bias=1.0), r=[e1], w=[e1])
        C.op("dve", lambda e: e.tensor_scalar(out=clam[:], in0=e1[:], scalar1=-8.0, scalar2=None, op0=ALU.mult), r=[e1], w=[clam])
        C.op("dve", lambda e: e.tensor_scalar(out=nclam[:], in0=e1[:], scalar1=8.0, scalar2=None, op0=ALU.mult), r=[e1], w=[nclam])
        for (dt_, buf_, n_) in ((qg, qgrep, 64), (kg, kgrep, 64), (snk, esink, 8)):
            C.dma("sp", buf_[:], dap(dt_, 0, [[0, 128], [1, n_]]), w=[buf_], sbuf_side=buf_)
        C.op("act", lambda e: e.activation(out=esink[:], in_=esink[:], func=AF.Exp), r=[esink], w=[esink])
        barrier()
    with contextlib.ExitStack() as st:
        NSL = 4
        g2rep = B("g2rep", [128, 1024], F32, st)
        C.dma("sp", g2rep[:], dap(ln2, 0, [[0, 128], [1, 1024]]), w=[g2rep], sbuf_side=g2rep)
        su = [B("su", [128, 1024], F32, st) for _ in range(NSL)]
        sv = [B("sv", [128, 1024], F32, st) for _ in range(NSL)]
        tb = [B("tb", [128, 2048], BF16, st) for _ in range(NSL)]
        tbu = [Res("tbu") for _ in range(NSL)]; tbv = [Res("tbv") for _ in range(NSL)]
        for c in range(128 + 2):
            if c < 128:
                s_ = c % NSL
                C.dma("sp", su[s_][:], dap(pu, c * 128 * 1024, [[1024, 128], [1, 1024]]), w=[su[s_]], sbuf_side=su[s_])
                C.dma("sp", sv[s_][:], dap(pv, c * 128 * 1024, [[1024, 128], [1, 1024]]), w=[sv[s_]], sbuf_side=sv[s_])
                C.op("act", lambda e: e.activation(out=tb[s_][:, 1024:2048], in_=sv[s_][:], func=AF.Copy), r=[sv[s_]], w=[tbv[s_]])
                en = "dve" if c % 2 == 0 else "pool"
                C.op(en, lambda e: e.tensor_tensor(out=tb[s_][:, 0:1024], in0=su[s_][:], in1=g2rep[:], op=ALU.mult), r=[su[s_], g2rep], w=[tbu[s_]])
            cs = c - 2
            if cs >= 0:
                s2 = cs % NSL
                C.dma("sp", dap(tab, cs * 128 * 2048, [[2048, 128], [1, 2048]]), tb[s2][:], r=[tbu[s2], tbv[s2]], sbuf_side=tb[s2], is_store=True, temp_store=True)
        barrier()

    xtb = [B("xt0", [128, 1024], F32), B("xt1", [128, 1024], F32)]
    h_sb = [B("h_sb0", [128, 1024], F32), B("h_sb1", [128, 1024], F32)]
    xn2_bf = [B("xn2_0", [128, 1024], BF16), B("xn2_1", [128, 1024], BF16)]
    idxT = [B("idxT0", [128, 128], U32), B("idxT1", [128, 128], U32)]
    gT = [B("gT0", [128, 128], F32), B("gT1", [128, 128], F32)]
    junkf = B("junkf", [128, 1024], BF16)
    junkc = B("junkc", [128, 1024], BF16)
    NS, DPF, LAG, NW = NSLOT, NDPF, NLAG, NLAG + 3
    assert NS >= DPF + LAG + 1
    UV = [B(f"UV{i}", [128, 2048], BF16) for i in range(NS)]
    WDr = [B(f"WD{i}", [128, 128], BF16) for i in range(NW)]
    apre = B("apre", [128, 128], F32); gel = B("gel", [128, 128], F32)
    a_r = [Res(f"ap{i}") for i in range(8)]; g_r = [Res(f"gl{i}") for i in range(8)]

    ARENA = 11008
    arena = B("arena", [128, ARENA], F32)
    A_res, B_res = [], []

    class Alloc:
        def __init__(self, lst):
            self.off = 0
            self.lst = lst

        def __call__(self, name, shape, dt):
            v = VBuf(arena, self.off, shape, dt, name)
            self.off += v.nf32
            assert self.off <= ARENA, (name, self.off)
            self.lst.append(v.res)
            return v

        def res(self, name):
            r_ = Res(name)
            self.lst.append(r_)
            return r_

    VA = Alloc(A_res); VB = Alloc(B_res)
    ss1 = VA("ss1", [128, 1], F32); rs1 = VA("rs1", [128, 1], F32)
    xn_bf = VA("xn_bf", [128, 1024], BF16); xnT = VA("xnT", [128, 8, 128], BF16)
    gate = VA("gate", [128, 4, 128], F32); xc = VA("xc", [128, 4, 128], F32); xc_bf = VA("xc_bf", [128, 4, 128], BF16)
    rr = VA("rr", [128, 4, 128], F32); ig = VA("ig", [128, 4, 128], F32); aa = VA("aa", [128, 4, 128], F32)
    t1 = VA("t1", [128, 4, 128], F32); t2 = VA("t2", [128, 4, 128], F32); hh = VA("hh", [128, 4, 128], F32)
    lo = VA("lo", [128, 4, 128], F32); ril = VA("ril", [128, 128], F32)
    catT = VA("catT", [128, 8, 128], BF16); cat_l = VA.res("cat_l"); cat_a = VA.res("cat_a")
    qkv = VA("qkv", [128, 768], F32); sqq = VA("sqq", [128, 640], F32); tmpq = VA("tmpq", [128, 640], F32)
    ssq = VA("ssq", [128, 10], F32); riq = VA("riq", [128, 10], F32)
    qn_bf = VA("qn_bf", [128, 512], BF16); kn = VA("kn", [128, 128], F32); kn_bf = VA("kn_bf", [128, 128], BF16)
    qT = VA("qT", [64, 8, 128], BF16)
    den = VA("den", [128, 8], F32); o_sb = VA("o_sb", [128, 512], F32)
    ssa = VA("ssa", [128, 1], F32); rsa = VA("rsa", [128, 1], F32); an_bf = VA("an_bf", [128, 512], BF16)
    ck_sb = VA("ck_sb", [128, 128], F32); cv_sb = VA("cv_sb", [128, 128], F32); ck_bf = VA("ck_bf", [128, 128], BF16)
    ss2 = VB("ss2", [128, 1], F32)
    xn2T = VB("xn2T", [128, 8, 128], BF16); pq_bf = VB("pq_bf", [128, 16, 128], BF16)
    S = VB("S", [128, 16, 128], F32); v16 = VB("v16", [128, 16, 16], F32); i16 = VB("i16", [128, 16, 16], U32)
    i16f = VB("i16f", [128, 16, 16], F32); big = VB("big", [128, 2048], F32)
    tmp = [VB("tmpa", [128, 128], F32), VB("tmpb", [128, 128], F32)]
    tmp2 = [VB("tmp2a", [128, 256], F32), VB("tmp2b", [128, 256], F32)]
    ts = VB("ts", [128, 8, 16], F32); tp = VB("tp", [128, 8, 16], U32)
    pA = VB("pA", [128, 128], U32); pB = VB("pB", [128, 128], U32); pAf = VB("pAf", [128, 128], F32); pBf = VB("pBf", [128, 128], F32)
    i1s = VB("i1s", [128, 128], F32); i2s = VB("i2s", [128, 128], F32); idxf = VB("idxf", [128, 128], F32)
    ge = VB("ge", [128, 128], F32); gs = VB("gs", [128, 8], F32)
    pq_r = [VB.res(f"pq{i}") for i in range(4)]; S_r = [VB.res(f"S{i}") for i in range(4)]
    v_r = [VB.res(f"v{i}") for i in range(16)]; i_r = [VB.res(f"i{i}") for i in range(16)]
    t_r = [VB.res(f"t{i}") for i in range(8)]; p_r = [VB.res(f"p{i}") for i in range(8)]

    def inherit(news, olds):
        d = {}
        for o in olds:
            toks = list(o.reads)
            if o.last_w is not None:
                toks.append(o.last_w)
            for t in toks:
                k = id(t[0])
                if k not in d or d[k][1] < t[1]:
                    d[k] = t
        for n_ in news:
            n_.reads = list(n_.reads) + list(d.values())

    tiles = []
    for n in range(16):
        tiles.append(dict(kind="p", n=n, nt=128, x=dap(xp, n * 128 * 1024, [[1024, 128], [1, 1024]]),
                          y=dap(yp, n * 128 * 1024, [[1024, 128], [1, 1024]]), first=(n == 0), last=(n == 15)))
    for s in range(2):
        tiles.append(dict(kind="s", s=s, nt=16, x=dap(xs, s * 16 * 1024, [[1024, 16], [1, 1024]]),
                          y=dap(ys, s * 16 * 1024, [[1024, 16], [1, 1024]]), first=True, last=True))

    F0 = P[0]

    def front(gi):
        T = tiles[gi]
        nt = T["nt"]
        xt = xtb[gi % 2]
        hb = h_sb[gi % 2]; x2 = xn2_bf[gi % 2]; ixT = idxT[gi % 2]; gtT = gT[gi % 2]
        C.dma("sp", xt[0:nt, :], T["x"], w=[xt], sbuf_side=xt)
        DBG["fstart"] = C.eng_time["sp"]
        samp = T["kind"] == "s"
        if samp:
            sp_, so_ = 0, 1
        else:
            so_ = T["n"] % 2
            sp_ = 1 - so_
        inherit(A_res, B_res)
        C.op("dve", lambda e: e.scalar_tensor_tensor(out=junkf[0:nt, :], in0=xt[0:nt, :], scalar=1.0, in1=xt[0:nt, :], op0=ALU.mult,
                                                     op1=ALU.mult, accum_out=ss1[0:nt, :]), r=[xt], w=[junkf, ss1])
        C.op("act", lambda e: e.activation(out=rs1[0:nt, :], in_=ss1[0:nt, :], func=AF.Sqrt, scale=1.0 / 1024, bias=EPS), r=[ss1], w=[rs1])
        C.op("dve", lambda e: e.reciprocal(out=rs1[0:nt, :], in_=rs1[0:nt, :]), r=[rs1], w=[rs1])
        C.op("act", lambda e: e.activation(out=xn_bf[0:nt, :], in_=xt[0:nt, :], func=AF.Copy, scale=rs1[0:nt, 0:1]),
             r=[xt, rs1], w=[xn_bf])
        for kc in range(8):
            C.op("pe", lambda e: e.transpose(F0.bf(0, 128, kc * 128, [[1, nt]]), xn_bf[0:nt, kc * 128:(kc + 1) * 128], identb[0:nt, 0:nt]),
                 r=[xn_bf, identb], w=[F0.ra])
        C.op("act", lambda e: e.activation(out=xnT[:, :, 0:nt], in_=F0.bf(0, 128, 0, [[128, 8], [1, nt]]), func=AF.Copy), r=[F0.ra], w=[xnT])
        for ct in range(8):
            hf = ct // 4
            for kc in range(8):
                C.op("pe", lambda e: e.matmul(F0.f(0, 128, hf * 512 + (ct % 4) * 128, [[1, nt]]), wi_bf[:, kc, ct * 128:(ct + 1) * 128],
                                              xnT[:, kc, 0:nt], start=(kc == 0), stop=(kc == 7)), r=[wi_r[kc], xnT], w=[F0.r(hf)])
        if samp:
            s = T["s"]
            C.op("pool", lambda e: e.tensor_copy(out=uext[:, :, 0:3], in_=vecT[:, :, R_SC + 3 * s:R_SC + 3 * s + 3]), r=[vecT], w=[uext])
            C.op("pool", lambda e: e.tensor_copy(out=hstate[:], in_=vecT[:, :, R_SL + s]), r=[vecT], w=[hstate])
        C.op("act", lambda e: e.activation(out=uext[:, :, 3:3 + nt], in_=F0.f(0, 128, 0, [[128, 4], [1, nt]]), func=AF.Copy), r=[F0.ra], w=[uext])
        C.op("act", lambda e: e.activation(out=gate[:, :, 0:nt], in_=F0.f(0, 128, 512, [[128, 4], [1, nt]]), func=AF.Copy), r=[F0.rb], w=[gate])
        for hf, (c0, c1) in enumerate(((1024, 1536), (1536, 1792))):
            for kc in range(8):
                C.op("pe", lambda e: e.matmul(F0.f(0, nt, hf * 512, [[1, c1 - c0]]), xnT[:, kc, 0:nt], wi_bf[:, kc, c0:c1],
                                              start=(kc == 0), stop=(kc == 7)), r=[wi_r[kc], xnT], w=[F0.r(hf)])
        C.op("act", lambda e: e.activation(out=qkv[0:nt, :], in_=F0.f(0, nt, 0, [[1, 768]]), func=AF.Copy), r=F0.both, w=[qkv])
        for ct in range(4):
            C.op("dve", lambda e: e.tensor_scalar(out=xc[:, ct, 0:nt], in0=uext[:, ct, 0:nt], scalar1=vecT[:, ct, R_CW:R_CW + 1],
                                                  scalar2=vecT[:, ct, R_CB:R_CB + 1], op0=ALU.mult, op1=ALU.add), r=[uext, vecT], w=[xc])
            for j in range(1, 4):
                C.op("dve", lambda e: e.scalar_tensor_tensor(out=xc[:, ct, 0:nt], in0=uext[:, ct, j:j + nt], scalar=vecT[:, ct, R_CW + j:R_CW + j + 1],
                                                             in1=xc[:, ct, 0:nt], op0=ALU.mult, op1=ALU.add), r=[uext, vecT, xc], w=[xc])
        C.op("act", lambda e: e.activation(out=xc_bf[:, :, 0:nt], in_=xc[:, :, 0:nt], func=AF.Copy), r=[xc], w=[xc_bf])
        if T["last"]:
            dstt, base = (ncp, 0) if not samp else (ncs, T["s"] * 1536)
            for ct in range(4):
                C.dma("sp", None, None, r=[uext], sbuf_side=uext, is_store=True, final=True,
                      fn=lambda e: e.dma_start(out=dap(dstt, base + ct * 128, [[1, 128], [512, 3]]), in_=uext[:, ct, nt:nt + 3],
                                               allow_slow_non_contiguous=True))
        else:
            C.op("pool", lambda e: e.tensor_copy(out=uext[:, :, 0:3], in_=uext[:, :, nt:nt + 3]), r=[uext], w=[uext])
        for ct in range(4):
            C.op("pe", lambda e: e.matmul(F0.f(0, 128, ct * 128, [[1, nt]]), wa_bd[:, ct, :], xc_bf[:, ct, 0:nt], start=True, stop=True),
                 r=[wa_bd, xc_bf], w=[F0.ra])
            C.op("pe", lambda e: e.matmul(F0.f(0, 128, 512 + ct * 128, [[1, nt]]), wi_bd[:, ct, :], xc_bf[:, ct, 0:nt], start=True, stop=True),
                 r=[wi_bd, xc_bf], w=[F0.rb])
        for ct in range(4):
            C.op("act", lambda e: e.activation(out=rr[:, ct, 0:nt], in_=F0.f(0, 128, ct * 128, [[1, nt]]), func=AF.Sigmoid,
                                               bias=vecT[:, ct, R_BA:R_BA + 1]), r=[F0.ra, vecT], w=[rr])
        for ct in range(4):
            C.op("act", lambda e: e.activation(out=ig[:, ct, 0:nt], in_=F0.f(0, 128, 512 + ct * 128, [[1, nt]]), func=AF.Sigmoid,
                                               bias=vecT[:, ct, R_BI:R_BI + 1]), r=[F0.rb, vecT], w=[ig])
        for ct in range(4):
            C.op("act", lambda e: e.activation(out=aa[:, ct, 0:nt], in_=rr[:, ct, 0:nt], func=AF.Exp, scale=clam[:, ct:ct + 1]), r=[rr, clam], w=[aa])
        for ct in range(4):
            C.op("act", lambda e: e.activation(out=t1[:, ct, 0:nt], in_=rr[:, ct, 0:nt], func=AF.Tanh, scale=nclam[:, ct:ct + 1]), r=[rr, nclam], w=[t1])
        C.op("pool", lambda e: e.tensor_tensor(out=t2[:, :, 0:nt], in0=aa[:, :, 0:nt], in1=aa[:, :, 0:nt], op=ALU.mult), r=[aa], w=[t2])
        C.op("dve", lambda e: e.scalar_tensor_tensor(out=t2[:, :, 0:nt], in0=t2[:, :, 0:nt], scalar=1.0, in1=t1[:, :, 0:nt], op0=ALU.add, op1=ALU.mult),
             r=[t2, t1], w=[t2])
        C.op("act", lambda e: e.activation(out=t2[:, :, 0:nt], in_=t2[:, :, 0:nt], func=AF.Sqrt), r=[t2], w=[t2])
        C.op("pool", lambda e: e.tensor_tensor(out=ig[:, :, 0:nt], in0=ig[:, :, 0:nt], in1=xc[:, :, 0:nt], op=ALU.mult), r=[ig, xc], w=[ig])
        C.op("pool", lambda e: e.tensor_tensor(out=ig[:, :, 0:nt], in0=ig[:, :, 0:nt], in1=t2[:, :, 0:nt], op=ALU.mult), r=[ig, t2], w=[ig])
        for ct in range(4):
            C.op("dve", lambda e: e.tensor_tensor_scan(out=hh[:, ct, 0:nt], data0=aa[:, ct, 0:nt], data1=ig[:, ct, 0:nt], initial=hstate[:, ct:ct + 1],
                                                       op0=ALU.mult, op1=ALU.add), r=[aa, ig, hstate], w=[hh])
        C.op("dve", lambda e: e.tensor_copy(out=hstate[:], in_=hh[:, :, nt - 1]), r=[hh], w=[hstate])
        if T["last"]:
            dstt, base = (nhp, 0) if not samp else (nhs, T["s"] * 512)
            C.dma("sp", None, None, r=[hstate], sbuf_side=hstate, is_store=True, final=True,
                  fn=lambda e: e.dma_start(out=dap(dstt, base, [[1, 128], [128, 4]]), in_=hstate[:], allow_slow_non_contiguous=True))
        C.op("act", lambda e: e.activation(out=t1[:, :, 0:nt], in_=gate[:, :, 0:nt], func=AF.Gelu_apprx_tanh), r=[gate], w=[t1])
        C.op("pool", lambda e: e.tensor_tensor(out=lo[:, :, 0:nt], in0=hh[:, :, 0:nt], in1=t1[:, :, 0:nt], op=ALU.mult), r=[hh, t1], w=[lo])
        C.op("pool", lambda e: e.tensor_tensor(out=t1[:, :, 0:nt], in0=lo[:, :, 0:nt], in1=lo[:, :, 0:nt], op=ALU.mult), r=[lo], w=[t1])
        for ct in range(4):
            C.op("pe", lambda e: e.matmul(F0.f(0, 128, 0, [[1, nt]]), ones_f[:], t1[:, ct, 0:nt], start=(ct == 0), stop=(ct == 3)),
                 r=[ones_f, t1], w=[F0.ra])
        C.op("act", lambda e: e.activation(out=ril[:, 0:nt], in_=F0.f(0, 128, 0, [[1, nt]]), func=AF.Sqrt, scale=1.0 / 512, bias=EPS), r=[F0.ra], w=[ril])
        C.op("dve", lambda e: e.reciprocal(out=ril[:, 0:nt], in_=ril[:, 0:nt]), r=[ril], w=[ril])
        C.op("pool", lambda e: e.tensor_tensor(out=catT[:, 0:4, 0:nt], in0=lo[:, :, 0:nt], in1=ril.pap(0, 128, 0, [[0, 4], [1, nt]]), op=ALU.mult),
             r=[lo, ril], w=[cat_l])
        C.op("pool", lambda e: e.tensor_tensor(out=sqq[0:nt, :], in0=qkv[0:nt, 0:640], in1=qkv[0:nt, 0:640], op=ALU.mult), r=[qkv], w=[sqq])
        C.op("dve", lambda e: e.tensor_reduce(out=ssq[0:nt, :], in_=sqq.pap(0, nt, 0, [[64, 10], [1, 64]]), axis=AX.X, op=ALU.add), r=[sqq], w=[ssq])
        C.op("act", lambda e: e.activation(out=riq[0:nt, :], in_=ssq[0:nt, :], func=AF.Sqrt, scale=1.0 / 64, bias=EPS), r=[ssq], w=[riq])
        C.op("dve", lambda e: e.reciprocal(out=riq[0:nt, :], in_=riq[0:nt, :]), r=[riq], w=[riq])
        C.op("pool", lambda e: e.tensor_tensor(out=tmpq.pap(0, nt, 0, [[64, 10], [1, 64]]), in0=qkv.pap(0, nt, 0, [[64, 10], [1, 64]]),
                                               in1=riq.pap(0, nt, 0, [[1, 10], [0, 64]]), op=ALU.mult), r=[qkv, riq], w=[tmpq])
        C.op("pool", lambda e: e.tensor_tensor(out=qn_bf.pap(0, nt, 0, [[64, 8], [1, 64]]), in0=tmpq.pap(0, nt, 0, [[64, 8], [1, 64]]),
                                               in1=qgrep.pap(0, nt, 0, [[0, 8], [1, 64]]), op=ALU.mult), r=[tmpq, qgrep], w=[qn_bf])
        C.op("dve", lambda e: e.tensor_tensor(out=kn.pap(0, nt, 0, [[64, 2], [1, 64]]), in0=tmpq.pap(0, nt, 512, [[64, 2], [1, 64]]),
                                              in1=kgrep.pap(0, nt, 0, [[0, 2], [1, 64]]), op=ALU.mult), r=[tmpq, kgrep], w=[kn])
        C.op("act", lambda e: e.activation(out=kn_bf[0:nt, :], in_=kn[0:nt, :], func=AF.Copy), r=[kn], w=[kn_bf])
        C.op("act", lambda e: e.activation(out=Vaug[so_].pap(0, nt, 0, [[65, 2], [1, 64]]), in_=qkv.pap(0, nt, 640, [[64, 2], [1, 64]]), func=AF.Copy),
             r=[qkv], w=[Vaug[so_]])
        if samp:
            s = T["s"]
            C.dma("sp", ck_sb[:], dap(ck, s * 16384, [[128, 128], [1, 128]]), w=[ck_sb], sbuf_side=ck_sb)
            C.dma("sp", cv_sb[:], dap(cv, s * 16384, [[128, 128], [1, 128]]), w=[cv_sb], sbuf_side=cv_sb)
            C.op("pool", lambda e: e.tensor_copy(out=ck_bf[:], in_=ck_sb[:]), r=[ck_sb], w=[ck_bf])
            C.op("pool", lambda e: e.tensor_copy(out=Vaug[sp_].pap(0, 128, 0, [[65, 2], [1, 64]]), in_=cv_sb.pap(0, 128, 0, [[64, 2], [1, 64]])),
                 r=[cv_sb], w=[Vaug[sp_]])
            for g in range(2):
                C.op("pe", lambda e: e.transpose(F0.bf(0, 64, 1024 + 256 + g * 128, [[1, 128]]), ck_bf[:, g * 64:(g + 1) * 64], identb[:]),
                     r=[ck_bf, identb], w=[F0.rb])
            C.op("dve", lambda e: e.tensor_copy(out=kTb[sp_][:], in_=F0.bf(0, 64, 1024 + 256, [[128, 2], [1, 128]])), r=[F0.rb], w=[kTb[sp_]])
            C.dma("sp", dap(nks, s * 16384, [[128, 112], [1, 128]]), ck_sb[16:128, :], r=[ck_sb], sbuf_side=ck_sb, is_store=True, final=True)
            C.dma("sp", dap(nvs, s * 16384, [[128, 112], [1, 128]]), cv_sb[16:128, :], r=[cv_sb], sbuf_side=cv_sb, is_store=True, final=True)
            C.dma("sp", dap(nks, s * 16384 + 112 * 128, [[128, 16], [1, 128]]), kn[0:16, :], r=[kn], sbuf_side=kn, is_store=True, final=True)
            C.dma("sp", dap(nvs, s * 16384 + 112 * 128, [[128, 16], [1, 128]]), qkv[0:16, 640:768], r=[qkv], sbuf_side=qkv, is_store=True, final=True)
        elif T["last"]:
            C.dma("sp", nkp.ap(), kn[:, :], r=[kn], sbuf_side=kn, is_store=True, final=True)
            C.dma("sp", nvp.ap(), qkv[:, 640:768], r=[qkv], sbuf_side=qkv, is_store=True, final=True)
        for h in range(8):
            C.op("pe", lambda e: e.transpose(F0.bf(0, 64, h * 128, [[1, nt]]), qn_bf[0:nt, h * 64:(h + 1) * 64], identb[0:nt, 0:nt]),
                 r=[qn_bf, identb], w=[F0.ra])
        for g in range(2):
            C.op("pe", lambda e: e.transpose(F0.bf(0, 64, 1024 + g * 128, [[1, nt]]), kn_bf[0:nt, g * 64:(g + 1) * 64], identb[0:nt, 0:nt]),
                 r=[kn_bf, identb], w=[F0.rb])
        C.op("act", lambda e: e.activation(out=qT[0:64, :, 0:nt], in_=F0.bf(0, 64, 0, [[128, 8], [1, nt]]), func=AF.Copy), r=[F0.ra], w=[qT])
        C.op("dve", lambda e: e.tensor_copy(out=kTb[so_][:, :, 0:nt], in_=F0.bf(0, 64, 1024, [[128, 2], [1, nt]])), r=[F0.rb], w=[kTb[so_]])
        if samp:
            blocks = [("prev", sp_, 128, [(0, 128, 0, 16)]), ("own", so_, 16, [(0, 16, 0, 16)])]
        else:
            blocks = []
            if not T["first"]:
                blocks.append(("prev", sp_, 128, [(0, 128, 0, 64), (64, 128, 64, 128)]))
            blocks.append(("own", so_, 128, [(0, 64, 0, 64), (0, 128, 64, 128)]))
        k_ = 0
        for g in range(2):
            for bi, (nm, slot, nk, regions) in enumerate(blocks):
                hf = k_ % 2
                k_ += 1
                C.op("pe", lambda e: e.matmul(F0.f(0, nk, hf * 512, [[128, 4], [1, nt]]), kTb[slot][:, g, 0:nk], qT[0:64, 4 * g:4 * g + 4, 0:nt],
                                              start=True, stop=True), r=[kTb[slot], qT], w=[F0.r(hf)])
                for (k0, k1, q0, q1) in regions:
                    C.op("act", lambda e: e.activation(out=PT[(g, nm)].pap(k0, k1 - k0, q0, [[128, 4], [1, q1 - q0]]),
                                                       in_=F0.f(k0, k1 - k0, hf * 512 + q0, [[128, 4], [1, q1 - q0]]), func=AF.Exp, scale=0.125),
                         r=[F0.r(hf)], w=[PT[(g, nm)]])
        for h in range(8):
            g, h4 = h // 4, h % 4
            for bi, (nm, slot, nk, regions) in enumerate(blocks):
                C.op("pe", lambda e: e.matmul(F0.f(0, nt, g * 512 + h4 * 65, [[1, 65]]), PT[(g, nm)].pap(0, nk, h4 * 128, [[1, nt]]),
                                              Vaug[slot].pap(0, nk, g * 65, [[1, 65]]), start=(bi == 0), stop=(bi == len(blocks) - 1)),
                     r=[PT[(g, nm)], Vaug[slot]], w=[F0.r(g)])
        C.op("dve", lambda e: e.tensor_tensor(out=den.pap(0, nt, 0, [[4, 2], [1, 4]]), in0=F0.f(0, nt, 64, [[512, 2], [65, 4]]),
                                              in1=esink.pap(0, nt, 0, [[4, 2], [1, 4]]), op=ALU.add), r=F0.both + [esink], w=[den])
        C.op("dve", lambda e: e.reciprocal(out=den[0:nt, :], in_=den[0:nt, :]), r=[den], w=[den])
        C.op("dve", lambda e: e.tensor_tensor(out=o_sb.pap(0, nt, 0, [[256, 2], [64, 4], [1, 64]]), in0=F0.f(0, nt, 0, [[512, 2], [65, 4], [1, 64]]),
                                              in1=den.pap(0, nt, 0, [[4, 2], [1, 4], [0, 64]]), op=ALU.mult), r=F0.both + [den], w=[o_sb])
        C.op("dve", lambda e: e.scalar_tensor_tensor(out=junkf[0:nt, 0:512], in0=o_sb[0:nt, :], scalar=1.0, in1=o_sb[0:nt, :], op0=ALU.mult, op1=ALU.mult,
                                                     accum_out=ssa[0:nt, :]), r=[o_sb], w=[junkf, ssa])
        C.op("act", lambda e: e.activation(out=rsa[0:nt, :], in_=ssa[0:nt, :], func=AF.Sqrt, scale=1.0 / 512, bias=EPS), r=[ssa], w=[rsa])
        C.op("dve", lambda e: e.reciprocal(out=rsa[0:nt, :], in_=rsa[0:nt, :]), r=[rsa], w=[rsa])
        C.op("act", lambda e: e.activation(out=an_bf[0:nt, :], in_=o_sb[0:nt, :], func=AF.Copy, scale=rsa[0:nt, 0:1]),
             r=[o_sb, rsa], w=[an_bf])
        for j in range(4):
            C.op("pe", lambda e: e.transpose(F0.bf(0, 128, j * 128, [[1, nt]]), an_bf[0:nt, j * 128:(j + 1) * 128], identb[0:nt, 0:nt]),
                 r=[an_bf, identb], w=[F0.ra])
        C.op("act", lambda e: e.activation(out=catT[:, 4:8, 0:nt], in_=F0.bf(0, 128, 0, [[128, 4], [1, nt]]), func=AF.Copy), r=[F0.ra], w=[cat_a])
        for hf in range(2):
            for c in range(8):
                C.op("pe", lambda e: e.matmul(F0.f(0, nt, hf * 512, [[1, 512]]), catT[:, c, 0:nt], wo_bf[:, c, hf * 512:(hf + 1) * 512],
                                              start=(c == 0), stop=(c == 7)), r=[cat_l if c < 4 else cat_a, wo_r[c]], w=[F0.r(hf)])
        C.op("dve", lambda e: e.tensor_tensor(out=hb[0:nt, :], in0=xt[0:nt, :], in1=F0.f(0, nt, 0, [[1, 1024]]), op=ALU.add), r=[xt] + F0.both, w=[hb])
        inherit(B_res, A_res)
        C.op("dve", lambda e: e.scalar_tensor_tensor(out=junkf[0:nt, :], in0=hb[0:nt, :], scalar=1.0, in1=hb[0:nt, :], op0=ALU.mult, op1=ALU.mult,
                                                     accum_out=ss2[0:nt, :]), r=[hb], w=[junkf, ss2])
        C.op("act", lambda e: e.activation(out=ss2[0:nt, :], in_=ss2[0:nt, :], func=AF.Sqrt, scale=1.0 / 1024, bias=EPS), r=[ss2], w=[ss2])
        C.op("dve", lambda e: e.reciprocal(out=ss2[0:nt, :], in_=ss2[0:nt, :]), r=[ss2], w=[ss2])
        C.op("act", lambda e: e.activation(out=x2[0:nt, :], in_=hb[0:nt, :], func=AF.Copy, scale=ss2[0:nt, 0:1]),
             r=[hb, ss2], w=[x2])
        for kc in range(8):
            C.op("pe", lambda e: e.transpose(F0.bf(0, 128, kc * 128, [[1, nt]]), x2[0:nt, kc * 128:(kc + 1) * 128], identb[0:nt, 0:nt]),
                 r=[x2, identb], w=[F0.ra])
        C.op("act", lambda e: e.activation(out=xn2T[:, :, 0:nt], in_=F0.bf(0, 128, 0, [[128, 8], [1, nt]]), func=AF.Copy), r=[F0.ra], w=[xn2T])
        for b4 in range(4):
            hf = b4 % 2
            for g4 in range(4):
                grp = b4 * 4 + g4
                for kc in range(8):
                    C.op("pe", lambda e: e.matmul(F0.f(0, 128, hf * 512 + g4 * 128, [[1, nt]]), wq_bf[:, kc, grp * 128:(grp + 1) * 128], xn2T[:, kc, 0:nt],
                                                  start=(kc == 0), stop=(kc == 7)), r=[wq_r[kc], xn2T], w=[F0.r(hf)])
            if b4 % 2 == 0:
                C.op("act", lambda e: e.activation(out=pq_bf[:, 4 * b4:4 * b4 + 4, 0:nt], in_=F0.f(0, 128, hf * 512, [[128, 4], [1, nt]]), func=AF.Copy),
                     r=[F0.r(hf)], w=[pq_r[b4]])
            else:
                C.op("dve", lambda e: e.tensor_copy(out=pq_bf[:, 4 * b4:4 * b4 + 4, 0:nt], in_=F0.f(0, 128, hf * 512, [[128, 4], [1, nt]])),
                     r=[F0.r(hf)], w=[pq_r[b4]])
        for b4 in range(4):
            hf = b4 % 2
            for g4 in range(4):
                grp = b4 * 4 + g4
                C.op("pe", lambda e: e.matmul(F0.f(0, nt, hf * 512 + g4 * 128, [[1, 128]]), pq_bf[:, grp, 0:nt], skT[grp % 2][:], start=True, stop=True),
                     r=[pq_r[b4], skT[grp % 2]], w=[F0.r(hf)])
            if b4 % 2 == 0:
                C.op("act", lambda e: e.activation(out=S.pap(0, nt, b4 * 512, [[1, 512]]), in_=F0.f(0, nt, hf * 512, [[1, 512]]), func=AF.Copy), r=[F0.r(hf)], w=[S_r[b4]])
            else:
                C.op("dve", lambda e: e.tensor_copy(out=S.pap(0, nt, b4 * 512, [[1, 512]]), in_=F0.f(0, nt, hf * 512, [[1, 512]])), r=[F0.r(hf)], w=[S_r[b4]])
        C.mark()
        for grp in range(16):
            tb_ = tmp[grp % 2]; sr = S_r[grp // 4]
            C.op("dve", lambda e: e.max(out=v16[0:nt, grp, 0:8], in_=S[0:nt, grp, :]), r=[sr], w=[v_r[grp]])
            C.op("dve", lambda e: e.max_index(out=i16[0:nt, grp, 0:8], in_max=v16[0:nt, grp, 0:8], in_values=S[0:nt, grp, :]), r=[sr, v_r[grp]], w=[i_r[grp]])
            C.op("dve", lambda e: e.match_replace(out=tb_[0:nt, :], in_to_replace=v16[0:nt, grp, 0:8], in_values=S[0:nt, grp, :], imm_value=NEG), r=[sr, v_r[grp]], w=[tb_])
            C.op("dve", lambda e: e.max(out=v16[0:nt, grp, 8:16], in_=tb_[0:nt, :]), r=[tb_], w=[v_r[grp]])
            C.op("dve", lambda e: e.max_index(out=i16[0:nt, grp, 8:16], in_max=v16[0:nt, grp, 8:16], in_values=tb_[0:nt, :]), r=[tb_, v_r[grp]], w=[i_r[grp]])
        C.op("act", lambda e: e.activation(out=i16f[0:nt, :, :], in_=i16[0:nt, :, :], func=AF.Copy), r=i_r, w=[i16f])
        C.op("dve", lambda e: e.tensor_tensor(out=big.pap(0, nt, 0, [[256, 8], [16, 16], [1, 16]]), in0=v16.pap(0, nt, 0, [[32, 8], [1, 16], [0, 16]]),
                                               in1=v16.pap(0, nt, 16, [[32, 8], [0, 16], [1, 16]]), op=ALU.add), r=v_r, w=[big])
        for h in range(8):
            tb_ = tmp2[h % 2]
            cand = big[0:nt, h * 256:(h + 1) * 256]
            C.op("dve", lambda e: e.max(out=ts[0:nt, h, 0:8], in_=cand), r=[big], w=[t_r[h]])
            C.op("dve", lambda e: e.max_index(out=tp[0:nt, h, 0:8], in_max=ts[0:nt, h, 0:8], in_values=cand), r=[big, t_r[h]], w=[p_r[h]])
            C.op("dve", lambda e: e.match_replace(out=tb_[0:nt, :], in_to_replace=ts[0:nt, h, 0:8], in_values=cand, imm_value=NEG), r=[big, t_r[h]], w=[tb_])
            C.op("dve", lambda e: e.max(out=ts[0:nt, h, 8:16], in_=tb_[0:nt, :]), r=[tb_], w=[t_r[h]])
            C.op("dve", lambda e: e.max_index(out=tp[0:nt, h, 8:16], in_max=ts[0:nt, h, 8:16], in_values=tb_[0:nt, :]), r=[tb_, t_r[h]], w=[p_r[h]])
        C.op("dve", lambda e: e.tensor_scalar(out=pA[0:nt, :], in0=tp.pap(0, nt, 0, [[1, 128]]), scalar1=4, scalar2=None, op0=ALU.logical_shift_right), r=p_r, w=[pA])
        C.op("dve", lambda e: e.tensor_scalar(out=pB[0:nt, :], in0=tp.pap(0, nt, 0, [[1, 128]]), scalar1=15, scalar2=None, op0=ALU.bitwise_and), r=p_r, w=[pB])
        C.op("act", lambda e: e.activation(out=pAf[0:nt, :], in_=pA[0:nt, :], func=AF.Copy), r=[pA], w=[pAf])
        C.op("act", lambda e: e.activation(out=pBf[0:nt, :], in_=pB[0:nt, :], func=AF.Copy), r=[pB], w=[pBf])
        for (pf, ioff, dst) in ((pAf, 0, i1s), (pBf, 16, i2s)):
            C.op("dve", lambda e: e.tensor_tensor(out=big.pap(0, nt, 0, [[256, 8], [16, 16], [1, 16]]), in0=iota16.pap(0, nt, 0, [[0, 8], [0, 16], [1, 16]]),
                                                  in1=pf.pap(0, nt, 0, [[16, 8], [1, 16], [0, 16]]), op=ALU.is_equal), r=[iota16, pf], w=[big])
            C.op("dve", lambda e: e.tensor_tensor(out=big.pap(0, nt, 0, [[256, 8], [16, 16], [1, 16]]), in0=big.pap(0, nt, 0, [[256, 8], [16, 16], [1, 16]]),
                                                   in1=i16f.pap(0, nt, ioff, [[32, 8], [0, 16], [1, 16]]), op=ALU.mult), r=[big, i16f], w=[big])
            C.op("dve", lambda e: e.tensor_reduce(out=dst[0:nt, :], in_=big.pap(0, nt, 0, [[16, 128], [1, 16]]), axis=AX.X, op=ALU.add), r=[big], w=[dst])
        C.op("dve", lambda e: e.scalar_tensor_tensor(out=idxf[0:nt, :], in0=i1s[0:nt, :], scalar=128.0, in1=i2s[0:nt, :], op0=ALU.mult, op1=ALU.add),
             r=[i1s, i2s], w=[idxf])
        C.op("dve", lambda e: e.tensor_tensor(out=ge.pap(0, nt, 0, [[16, 8], [1, 16]]), in0=ts.pap(0, nt, 0, [[16, 8], [1, 16]]),
                                               in1=ts.pap(0, nt, 0, [[16, 8], [0, 16]]), op=ALU.subtract), r=t_r, w=[ge])
        C.op("act", lambda e: e.activation(out=ge[0:nt, :], in_=ge[0:nt, :], func=AF.Exp), r=[ge], w=[ge])
        C.op("dve", lambda e: e.tensor_reduce(out=gs[0:nt, :], in_=ge.pap(0, nt, 0, [[16, 8], [1, 16]]), axis=AX.X, op=ALU.add), r=[ge], w=[gs])
        C.op("dve", lambda e: e.reciprocal(out=gs[0:nt, :], in_=gs[0:nt, :]), r=[gs], w=[gs])
        C.op("dve", lambda e: e.tensor_tensor(out=ge.pap(0, nt, 0, [[16, 8], [1, 16]]), in0=ge.pap(0, nt, 0, [[16, 8], [1, 16]]),
                                               in1=gs.pap(0, nt, 0, [[1, 8], [0, 16]]), op=ALU.mult), r=[ge, gs], w=[ge])
        C.op("pe", lambda e: e.transpose(F0.f(0, 128, 0, [[1, nt]]), idxf[0:nt, :], identf[0:nt, 0:nt]), r=[idxf, identf], w=[F0.ra])
        C.op("pe", lambda e: e.transpose(F0.f(0, 128, 512, [[1, nt]]), ge[0:nt, :], identf[0:nt, 0:nt]), r=[ge, identf], w=[F0.rb])
        C.op("dve", lambda e: e.tensor_copy(out=ixT[:, 0:nt], in_=F0.f(0, 128, 0, [[1, nt]])), r=[F0.ra], w=[ixT])
        C.op("act", lambda e: e.activation(out=gtT[:, 0:nt], in_=F0.f(0, 128, 512, [[1, nt]]), func=AF.Copy), r=[F0.rb], w=[gtT])
        DBG["fend"] = C.eng_time["act"]

    bcP = [P[1], P[2]]

    def back(gi, stepper, kstep):
        T = tiles[gi]
        nt = T["nt"]
        xt = xtb[gi % 2]
        hb = h_sb[gi % 2]; x2 = xn2_bf[gi % 2]; ixT = idxT[gi % 2]; gtT = gT[gi % 2]

        def gather(t):
            uv = UV[t % NS]
            C.dma("pool", None, None, r=[ixT], w=[uv], sbuf_side=uv,
                  fn=lambda e: e.indirect_dma_start(out=uv[:], out_offset=None, in_=tab.ap(),
                                                    in_offset=bass.IndirectOffsetOnAxis(ap=ixT[:, t:t + 1], axis=0)))

        credit = {k_: 0.0 for k_ in BUDGET}
        tcur = [0]

        def pump():
            if stepper is None:
                return
            while not stepper.finished:
                ne = stepper.next_e
                if ne in C.eng_time:
                    if credit.get(ne, 1.0) <= 0.0:
                        break
                    r_, w_ = stepper.next_rw
                    ready = C.pred_ready(ne, r_, w_)
                    if ready > max(C.eng_time[ne], C.front_time) + SLACK:
                        break
                    t_before = max(C.eng_time[ne], ready)
                    stepper.step(1)
                    if ne in credit:
                        credit[ne] -= max(0.05, C.eng_time[ne] - t_before)
                else:
                    stepper.step(1)

        f_base = C.ninst_f

        def refill():
            mult = 1.0
            if stepper is not None and NF[0] > 0:
                exp_frac = min(1.0, (tcur[0] + 1) / (FIN_FRAC * nt))
                act_frac = (C.ninst_f - f_base) / float(NF[0])
                if act_frac < exp_frac:
                    mult = BOOST
            for e_ in credit:
                credit[e_] = min(credit[e_] + mult * BUDGET[e_], 3.0 * mult * BUDGET[e_])

        C.pump = pump if stepper is not None else None
        for t in range(min(DPF, nt)):
            gather(t)
        for t in range(nt + LAG):
            if t < nt:
                if t + DPF < nt:
                    gather(t + DPF)
                pp = bcP[t % 2]; uv = UV[t % NS]
                for hf in range(2):
                    C.op("pe", lambda e: e.matmul(pp.f(0, 128, hf * 512, [[1, 512]]), identb.pap(0, nt, t, [[0, 128]]), x2[0:nt, hf * 512:(hf + 1) * 512],
                                                  start=True, stop=True), r=[identb, x2], w=[pp.r(hf)])
                C.op("dve", lambda e: e.scalar_tensor_tensor(out=junkc[:, :], in0=uv[:, 0:1024], scalar=1.0, in1=pp.f(0, 128, 0, [[1, 1024]]), op0=ALU.mult, op1=ALU.mult,
                                                             accum_out=apre[:, t:t + 1]), r=[uv] + pp.both, w=[junkc, a_r[t % 8]])
                C.op("act", lambda e: e.activation(out=gel[:, t:t + 1], in_=apre[:, t:t + 1], func=AF.Gelu_apprx_tanh), r=[a_r[t % 8]], w=[g_r[t % 8]])
                wd = WDr[t % NW]
                if WD_ON_ACT:
                    C.op("act", lambda e: e.activation(out=gel[:, t:t + 1], in_=gel[:, t:t + 1], func=AF.Copy, scale=gtT[:, t:t + 1]), r=[g_r[t % 8], gtT], w=[g_r[t % 8]])
                    C.op("act", lambda e: e.activation(out=wd[:, 0:nt], in_=Zc[:, 127 - t:127 - t + nt], func=AF.Copy, scale=gel[:, t:t + 1]), r=[Zc, g_r[t % 8]], w=[wd])
                else:
                    C.op("dve", lambda e: e.tensor_scalar(out=wd[:, 0:nt], in0=Zc[:, 127 - t:127 - t + nt], scalar1=gel[:, t:t + 1], scalar2=gtT[:, t:t + 1],
                                                          op0=ALU.mult, op1=ALU.mult), r=[Zc, g_r[t % 8], gtT], w=[wd])
            tcur[0] = t
            refill()
            if DEBUG and gi == 2 and t % 8 == 0:
                print("      tok", t, {k_: round(v_, 1) for k_, v_ in C.eng_time.items()}, "front", round(C.front_time, 1), "dma_free", round(C.dma_free, 1), "F done" if (stepper is None or stepper.finished) else "F next " + str(stepper.next_e))
            tv = t - LAG
            if 0 <= tv < nt:
                uv = UV[tv % NS]; wd = WDr[tv % NW]
                for hf in range(2):
                    C.op("pe", lambda e: e.matmul(P[3].f(0, nt, hf * 512, [[1, 512]]), wd[:, 0:nt], uv[:, 1024 + hf * 512:1024 + (hf + 1) * 512],
                                                  start=(tv == 0), stop=(tv == nt - 1)), r=[wd, uv], w=[P[3].r(hf)])
        C.op("dve", lambda e: e.tensor_tensor(out=xt[0:nt, :], in0=hb[0:nt, :], in1=P[3].f(0, nt, 0, [[1, 1024]]), op=ALU.add), r=[hb] + P[3].both, w=[xt])
        C.dma("sp", T["y"], xt[0:nt, :], r=[xt], sbuf_side=xt, is_store=True, final=True)
        C.pump = None

    NF = [0]
    n_before = C.ninst
    front(0)
    NF[0] = C.ninst - n_before
    for gi in range(len(tiles)):
        stp = None
        if gi + 1 < len(tiles):
            stp = Stepper(C, lambda: front(gi + 1))
            if MARKMODE:
                stp.run_to_mark()
        n0 = C.ninst
        back(gi, stp, KSTEP)
        n1 = C.ninst
        if stp is not None:
            stp.drain()
        if DEBUG:
            print("   F(gi+1) ended at model t=%.1f ; F started at %.1f" % (DBG.get("fend", 0), DBG.get("fstart", 0)))
            print("tile", gi, "model time", {k_: round(v_, 1) for k_, v_ in C.eng_time.items()}, "instr in back(incl F)", n1 - n0, "drained", C.ninst - n1)
    C.finish()
    return nc


KSTEP = 7
MARKMODE = 0
NSLOT, NDPF, NLAG = 7, 4, 2
WD_ON_ACT = 1
FIN_FRAC = 0.8
BOOST = 2.0
DEBUG = 0
DBG = {}
SLACK = 0.15
BUDGET = {"dve": 0.7, "pe": 0.6, "act": 1.0, "pool": 0.5, "sp": 1.0}
COST = {"dve": 0.33, "pe": 0.15, "act": 0.4, "pool": 1.0}
SKIPRAW = 0
_CACHE = {}


def kernel(x_prompt, x_sample, cache_attn_k, cache_attn_v, state_conv, state_lru,
           ln1_g, w_in, conv_w, conv_b, lru_wa, lru_ba, lru_wi, lru_bi, lru_lambda,
           q_norm_g, k_norm_g, attn_sinks, g_lru_out, g_attn_out, w_out, ln2_g,
           peer_w_query, peer_sub_keys1, peer_sub_keys2, peer_u, peer_v):
    f = lambda a: np.ascontiguousarray(np.asarray(a, dtype=np.float32))
    x_prompt = f(x_prompt); x_sample = f(x_sample)
    if "nc" not in _CACHE:
        _CACHE["nc"] = build_program()
    nc = _CACHE["nc"]
    shared = dict(
        ln1=f(ln1_g).reshape(1, 1024), ln2=f(ln2_g).reshape(1, 1024),
        qg=f(q_norm_g).reshape(1, 64), kg=f(k_norm_g).reshape(1, 64), snk=f(attn_sinks).reshape(1, 8),
        w_in=f(w_in).reshape(1024, 1792), w_out=f(w_out).reshape(1024, 1024), w_q=f(peer_w_query).reshape(1024, 2048),
        wa=f(lru_wa).reshape(8, 64, 64), wi=f(lru_wi).reshape(8, 64, 64),
        sk1=f(peer_sub_keys1).reshape(128, 128), sk2=f(peer_sub_keys2).reshape(128, 128),
        pu=f(peer_u).reshape(16384, 1024), pv=f(peer_v).reshape(16384, 1024),
    )
    common_rows = [f(conv_w).reshape(4, 512), f(conv_b).reshape(1, 512), f(lru_ba).reshape(1, 512), f(lru_bi).reshape(1, 512),
                   f(lru_lambda).reshape(1, 512), f(g_lru_out).reshape(1, 512)]
    ga_row = f(g_attn_out).reshape(1, 512)
    sc = f(state_conv).reshape(16, 3, 512); sl = f(state_lru).reshape(16, 512)
    ckf = f(cache_attn_k).reshape(16, 128, 128); cvf = f(cache_attn_v).reshape(16, 128, 128)
    in_maps = []
    for c in range(8):
        s0, s1 = 2 * c, 2 * c + 1
        v512 = np.concatenate(common_rows + [sc[s0], sc[s1], sl[s0:s0 + 1], sl[s1:s1 + 1], ga_row], axis=0)
        m = dict(shared)
        m.update(xp=x_prompt[c], xs=np.ascontiguousarray(x_sample[s0:s1 + 1].reshape(32, 1024)),
                 ck=np.ascontiguousarray(ckf[s0:s1 + 1]), cv=np.ascontiguousarray(cvf[s0:s1 + 1]), v512=np.ascontiguousarray(v512))
        in_maps.append(m)
    res = run_bass_kernel_spmd(nc, in_maps, core_ids=list(range(8)))
    R = res.results
    y_p = np.stack([R[c]["yp"] for c in range(8)], 0).reshape(8, 2048, 1024)
    y_s = np.concatenate([R[c]["ys"] for c in range(8)], 0).reshape(16, 16, 1024)
    nk_p = np.stack([R[c]["nkp"] for c in range(8)], 0).reshape(1, 8, 128, 2, 64)
    nv_p = np.stack([R[c]["nvp"] for c in range(8)], 0).reshape(1, 8, 128, 2, 64)
    nc_p = np.stack([R[c]["ncp"] for c in range(8)], 0).reshape(1, 8, 3, 512)
    nh_p = np.stack([R[c]["nhp"] for c in range(8)], 0).reshape(1, 8, 512)
    nk_s = np.concatenate([R[c]["nks"] for c in range(8)], 0).reshape(1, 16, 128, 2, 64)
    nv_s = np.concatenate([R[c]["nvs"] for c in range(8)], 0).reshape(1, 16, 128, 2, 64)
    nc_s = np.concatenate([R[c]["ncs"] for c in range(8)], 0).reshape(1, 16, 3, 512)
    nh_s = np.concatenate([R[c]["nhs"] for c in range(8)], 0).reshape(1, 16, 512)
    return (y_p, y_s, nk_p, nv_p, nc_p, nh_p, nk_s, nv_s, nc_s, nh_s)
```

```python
import numpy as np
import concourse.bass as bass
import concourse.mybir as mybir
from concourse.bass_utils import run_bass_kernel_spmd

F32 = mybir.dt.float32
BF16 = mybir.dt.bfloat16
U32 = mybir.dt.uint32
I32 = mybir.dt.int32
AF = mybir.ActivationFunctionType
ALU = mybir.AluOpType
AX = mybir.AxisListType

SEM_LIMIT = 30000
SKIP_SAME_RAW = ()
STRICT = 1


class Res:
    __slots__ = ("name", "last_w", "reads", "dsem_w", "dsem_r", "dcnt_w", "dcnt_r")

    def __init__(self, name):
        self.name = name
        self.last_w = None
        self.reads = []
        self.dsem_w = None
        self.dsem_r = None
        self.dcnt_w = 0
        self.dcnt_r = 0


class Buf:
    def __init__(self, C, name, shape, dtype, psum=False, stack=None):
        self.name = name
        if psum:
            self.t = C.nc.alloc_psum_tensor(name, list(shape), dtype)
        elif stack is not None:
            self.t = stack.enter_context(C.nc.sbuf_tensor(name, list(shape), dtype))
        else:
            self.t = C.nc.alloc_sbuf_tensor(name, list(shape), dtype)
        self.res = Res(name)
        if stack is not None:
            stack.callback(C.release_res, self.res)
        self.shape = list(shape)
        self.dtype = dtype

    def __getitem__(self, k):
        return self.t[k]

    def ap(self, offset, dims):
        fs = 1
        for s in self.shape[1:]:
            fs *= s
        return bass.AP(self.t, offset, [[fs, self.shape[0]]] + [list(d) for d in dims])

    def pap(self, p0, pn, offset, dims):
        fs = 1
        for s in self.shape[1:]:
            fs *= s
        return bass.AP(self.t, p0 * fs + offset, [[fs, pn]] + [list(d) for d in dims])


def _ap_n(ap):
    try:
        sh = list(ap.shape)
        n = 1
        for s_ in sh[1:]:
            n *= int(s_)
        return int(sh[0]), n
    except Exception:
        return 128, 128


_DT_SIZE = {}


def _dsize(ap):
    try:
        d = ap.dtype
        if d == BF16:
            return 2
        return 4
    except Exception:
        return 4


class EngProxy:
    def __init__(self, eng):
        self._eng = eng
        self.name = None
        self.n = 128
        self.passes = 1
        self.nbytes = 0

    def __getattr__(self, name):
        real = getattr(self._eng, name)

        def call(*a, **kw):
            self.name = name
            out = kw.get("out", a[0] if a else None)
            if name in ("matmul", "transpose"):
                rhs = kw.get("rhs", a[2] if len(a) > 2 else None)
                if name == "transpose":
                    src = kw.get("in_", a[1] if len(a) > 1 else None)
                    p_, n_ = _ap_n(src)
                    self.n = p_
                else:
                    p_, n_ = _ap_n(rhs)
                    self.n = n_
                    self.passes = 4 if _dsize(rhs) == 4 else 1
            elif name in ("max", "max_index", "match_replace"):
                src = kw.get("in_", kw.get("in_values", None))
                p_, n_ = _ap_n(src)
                self.n = n_
            elif name == "tensor_reduce":
                p_, n_ = _ap_n(kw.get("in_"))
                self.n = n_
            elif out is not None:
                p_, n_ = _ap_n(out)
                self.n = n_
                self.nbytes = p_ * n_ * _dsize(out)
            return real(*a, **kw)

        return call


def op_cost(e, name, n, passes):
    if e == "pe":
        return 0.11 + n * passes * 0.00032
    if e == "dve":
        return 0.07 + n * 0.00105
    if e == "act":
        return 0.2 + n * 0.00088
    if e == "pool":
        return 0.3 + n * 0.0023
    return 0.1


class Ctx:
    def __init__(self, nc):
        self.nc = nc
        self.eng = {"pe": nc.tensor, "act": nc.scalar, "dve": nc.vector,
                    "pool": nc.gpsimd, "sp": nc.sync}
        self.esem = {}
        self.ecnt = {}
        self.waited = {e: {} for e in self.eng}
        self.semn = 0
        self.ninst = 0
        self.final_tokens = []
        self.sem_pool = []
        self.pending = []
        self.hook_thread = None
        self.pump = None
        self.in_pump = False
        self.eng_time = {e: 0.0 for e in self.eng}
        self.tok_time = {}
        self.dma_free = 0.0
        self.front_time = 0.0
        self.log = None
        self.ninst_f = 0

    def release_res(self, res):
        if res.dsem_w is not None and res.dcnt_w < SEM_LIMIT:
            self.sem_pool.append((res.dsem_w, res.dcnt_w))
        if res.dsem_r is not None and res.dcnt_r < SEM_LIMIT:
            self.sem_pool.append((res.dsem_r, res.dcnt_r))
        res.dsem_w = res.dsem_r = None

    def get_dsem(self, name):
        if self.sem_pool:
            return self.sem_pool.pop()
        return (self.newsem(name), 0)

    def barrier(self):
        engs = list(self.eng)
        for e in engs:
            for e2 in list(self.esem):
                if e2 != e:
                    self._wait(e, (self.esem[e2], self.ecnt[e2], e2))
            d = {}
            for t in self.pending:
                k = id(t[0])
                if k not in d or d[k][1] < t[1]:
                    d[k] = t
            for t in d.values():
                self._wait(e, t)
        self.pending = []

    def newsem(self, name):
        self.semn += 1
        return self.nc.alloc_semaphore(name=f"{name}_{self.semn}")

    def buf(self, name, shape, dtype, psum=False):
        return Buf(self, name, shape, dtype, psum)

    def _next_tok(self, e):
        if e not in self.esem or self.ecnt[e] >= SEM_LIMIT:
            self.esem[e] = self.newsem("e" + e)
            self.ecnt[e] = 0
        self.ecnt[e] += 1
        return (self.esem[e], self.ecnt[e], e)

    def _wait(self, e, tok):
        sem, val, src = tok
        key = id(sem)
        w = self.waited[e]
        if w.get(key, (None, 0))[1] >= val:
            return False
        self.eng[e].wait_ge(sem, val)
        w[key] = (sem, val)
        return True

    def _dep_tokens(self, e, r, w, same_engine_raw=True):
        toks = []
        for b in r:
            res = b.res if hasattr(b, 'res') else b
            if res.last_w is not None:
                t = res.last_w
                if t[2] == e and (e == "pe" or not same_engine_raw or e in SKIP_SAME_RAW):
                    continue
                toks.append(t)
        strict = STRICT and e != "pe"
        for b in w:
            res = b.res if hasattr(b, 'res') else b
            if res.last_w is not None:
                t = res.last_w
                if strict or not (t[2] == e):
                    toks.append(t)
            for t in res.reads:
                if t[2] == e and not strict:
                    continue
                toks.append(t)
        return toks

    def pred_ready(self, e, r, w):
        t_ = 0.0
        for tk in self._dep_tokens(e, r, w):
            t_ = max(t_, self.tok_time.get((id(tk[0]), tk[1]), 0.0))
        return t_

    def _deps(self, e, r, w, same_engine_raw=True):
        toks = self._dep_tokens(e, r, w, same_engine_raw)
        t_ = 0.0
        nw = 0
        for t in toks:
            t_ = max(t_, self.tok_time.get((id(t[0]), t[1]), 0.0))
            if self._wait(e, t):
                nw += 1
        return t_, nw

    def _commit(self, tok, r, w):
        for b in w:
            res = b.res if hasattr(b, 'res') else b
            res.last_w = tok
            res.reads = []
        for b in r:
            res = b.res if hasattr(b, 'res') else b
            if res.last_w is tok:
                continue
            res.reads.append(tok)
            if len(res.reads) > 64:
                d = {}
                for t in res.reads:
                    k = id(t[0])
                    if k not in d or d[k][1] < t[1]:
                        d[k] = t
                res.reads = list(d.values())

    def _hook(self, e=None, r=(), w=()):
        ht = self.hook_thread
        import threading as _th
        if ht is not None and _th.current_thread() is ht[0]:
            ht[1](e, r, w)
        elif self.pump is not None and not self.in_pump:
            self.in_pump = True
            try:
                self.pump()
            finally:
                self.in_pump = False

    def mark(self, name="MARK"):
        self._hook(name)

    def _is_main(self):
        ht = self.hook_thread
        if ht is None:
            return True
        import threading as _th
        return _th.current_thread() is not ht[0]

    def op(self, e, fn, r=(), w=()):
        self._hook(e, r, w)
        t_ready, nw = self._deps(e, r, w)
        tok = self._next_tok(e)
        px = EngProxy(self.eng[e])
        ins = fn(px)
        ins.then_inc(tok[0], 1)
        self._commit(tok, r, w)
        self.ninst += 1
        start = max(self.eng_time[e], t_ready) + (0.08 if nw else 0.0)
        end = start + op_cost(e, px.name, px.n, px.passes)
        self.eng_time[e] = end
        self.tok_time[(id(tok[0]), tok[1])] = end + 0.06
        if self.log is not None:
            self.log.append(("M" if self._is_main() else "F", e, px.name, px.n, round(t_ready, 2), round(start, 2), round(end, 2)))
        if self._is_main():
            self.front_time = max(self.front_time, start)
        else:
            self.ninst_f += 1
        return tok

    def dma(self, q, out, in_, r=(), w=(), sbuf_side=None, is_store=False, fn=None, final=False, temp_store=False):
        self._hook(q, r, w)
        t_ready, nw = self._deps(q, r, w, same_engine_raw=True)
        res = sbuf_side.res if hasattr(sbuf_side, 'res') else sbuf_side
        if is_store:
            if res.dsem_r is None:
                res.dsem_r, res.dcnt_r = self.get_dsem("dr")
            res.dcnt_r += 16
            tok = (res.dsem_r, res.dcnt_r, "dma")
        else:
            if res.dsem_w is None:
                if q == "pool":
                    res.dsem_w, res.dcnt_w = self.newsem("dg"), 0
                else:
                    res.dsem_w, res.dcnt_w = self.get_dsem("dw")
            res.dcnt_w += 16
            tok = (res.dsem_w, res.dcnt_w, "dma")
        px = EngProxy(self.eng[q])
        if fn is None:
            ins = px.dma_start(out=out, in_=in_)
        else:
            ins = fn(px)
        ins.then_inc(tok[0], 16)
        self._commit(tok, r, w)
        self.ninst += 1
        if final:
            self.final_tokens.append(tok)
        if temp_store:
            self.pending.append(tok)
        start = max(self.eng_time[q], t_ready) + (0.08 if nw else 0.0)
        issue = 1.1 if px.name == "indirect_dma_start" else 0.1
        self.eng_time[q] = start + issue
        xfer = px.nbytes / 3.0e5
        s2 = max(start + issue, self.dma_free)
        self.dma_free = s2 + xfer
        self.tok_time[(id(tok[0]), tok[1])] = s2 + xfer + 2.0
        if self._is_main():
            self.front_time = max(self.front_time, start)
        return tok

    def finish(self):
        for t in self.final_tokens:
            self._wait("sp", t)
        for e in self.esem:
            self._wait("sp", (self.esem[e], self.ecnt[e], e))


import contextlib
import threading

EPS = 1e-6
NEG = -1.0e30
R_CW, R_CB, R_BA, R_BI, R_LAM, R_GL, R_SC, R_SL, R_GA, NROW = 0, 4, 5, 6, 7, 8, 9, 15, 17, 18


class PS:
    def __init__(self, C, name):
        self.b = C.buf(name, [128, 1024], F32, psum=True)
        self.t = self.b.t
        self.ra = Res(name + "a")
        self.rb = Res(name + "b")
        self.bt = self.t[:, :].bitcast(BF16)

    def f(self, p0, pn, off, dims):
        return bass.AP(self.t, p0 * 1024 + off, [[1024, pn]] + [list(d) for d in dims])

    def bf(self, p0, pn, off, dims):
        return bass.AP(self.bt.tensor, self.bt.offset + p0 * 2048 + off, [[2048, pn]] + [list(d) for d in dims])

    def r(self, half):
        return self.ra if half == 0 else self.rb

    @property
    def both(self):
        return [self.ra, self.rb]


class VBuf:
    def __init__(self, arena, f32_off, shape, dtype, name):
        esz = 2 if dtype == BF16 else 4
        n = 1
        for s_ in shape[1:]:
            n *= s_
        nf32 = (n * esz + 3) // 4
        self.nf32 = nf32
        base = arena.t[:, f32_off:f32_off + nf32]
        if dtype != F32:
            base = base.bitcast(dtype)
        self.base = base
        self.pstep = base.ap[0][0]
        self.off0 = base.offset
        self.tensor = base.tensor
        self.shape = list(shape)
        self.n = n
        self.res = Res(name)
        self.name = name
        if len(shape) == 2:
            self.v = base[:, 0:n] if n != base.shape[1] else base
        elif len(shape) == 3:
            self.v = base[:, 0:n].rearrange("p (a b) -> p a b", a=shape[1], b=shape[2])
        else:
            raise ValueError

    def __getitem__(self, k):
        return self.v[k]

    def pap(self, p0, pn, off, dims):
        return bass.AP(self.tensor, self.off0 + p0 * self.pstep + off, [[self.pstep, pn]] + [list(d) for d in dims])

    def ap(self, off, dims):
        return self.pap(0, self.shape[0], off, dims)


class Stepper:
    def __init__(self, C, fn):
        self.C = C
        self.go = threading.Semaphore(0)
        self.back = threading.Semaphore(0)
        self.finished = False
        self.err = None
        self.next_e = None
        self.at_mark = False

        def run():
            self.go.acquire()
            try:
                fn()
            except BaseException as ex:
                self.err = ex
            self.finished = True
            self.C.hook_thread = None
            self.back.release()

        self.th = threading.Thread(target=run)
        self.th.start()

    def hook(self, e=None, r=(), w=()):
        self.next_e = e
        self.next_rw = (r, w)
        if e == "MARK":
            self.at_mark = True
        self.back.release()
        self.go.acquire()

    def step(self, n=1):
        for _ in range(n):
            if self.finished:
                break
            self.C.hook_thread = (self.th, self.hook)
            self.go.release()
            self.back.acquire()
            self.C.hook_thread = None
        if self.err is not None:
            raise self.err

    def run_to_mark(self):
        while not self.finished and not self.at_mark:
            self.step(1)

    def drain(self):
        while not self.finished:
            self.step(64)
        self.th.join()
        if self.err is not None:
            raise self.err


def build_program():
    global SKIP_SAME_RAW
    SKIP_SAME_RAW = {0: (), 1: ("dve",), 2: ("dve", "act"), 3: ("dve", "act", "pool")}[SKIPRAW]
    nc = bass.Bass("TRN2", target_bir_lowering=False)
    C = Ctx(nc)
    if DEBUG:
        C.log = []
        DBG["C"] = C
    cnt = [0]

    def DI(name, shape, dt=F32):
        return nc.dram_tensor(name, list(shape), dt, kind="ExternalInput")

    def DO(name, shape, dt=F32):
        return nc.dram_tensor(name, list(shape), dt, kind="ExternalOutput")

    xp = DI("xp", [2048, 1024]); xs = DI("xs", [32, 1024])
    ck = DI("ck", [2, 128, 128]); cv = DI("cv", [2, 128, 128])
    v512 = DI("v512", [NROW, 512])
    ln1 = DI("ln1", [1, 1024]); ln2 = DI("ln2", [1, 1024])
    qg = DI("qg", [1, 64]); kg = DI("kg", [1, 64]); snk = DI("snk", [1, 8])
    w_in = DI("w_in", [1024, 1792]); w_out = DI("w_out", [1024, 1024]); w_q = DI("w_q", [1024, 2048])
    wa = DI("wa", [8, 64, 64]); wi = DI("wi", [8, 64, 64])
    sk1 = DI("sk1", [128, 128]); sk2 = DI("sk2", [128, 128])
    pu = DI("pu", [16384, 1024]); pv = DI("pv", [16384, 1024])
    yp = DO("yp", [2048, 1024]); ys = DO("ys", [32, 1024])
    nkp = DO("nkp", [128, 128]); nvp = DO("nvp", [128, 128]); ncp = DO("ncp", [3, 512]); nhp = DO("nhp", [1, 512])
    nks = DO("nks", [2, 128, 128]); nvs = DO("nvs", [2, 128, 128]); ncs = DO("ncs", [2, 3, 512]); nhs = DO("nhs", [2, 512])
    tab = nc.dram_tensor("tab", [16384, 2048], BF16, kind="Internal")

    def dap(t, off, dims):
        return bass.AP(t, off, [list(d) for d in dims])

    def B(name, shape, dt, stack=None):
        cnt[0] += 1
        return Buf(C, f"{name}_{cnt[0]}", shape, dt, stack=stack)

    def barrier():
        C.barrier()

    P = [PS(C, f"P{i}") for i in range(4)]
    identf = B("identf", [128, 128], F32); identb = B("identb", [128, 128], BF16); ones_f = B("ones_f", [128, 128], F32)
    wi_bf = B("wi_bf", [128, 8, 1792], BF16); wo_bf = B("wo_bf", [128, 8, 1024], BF16); wq_bf = B("wq_bf", [128, 8, 2048], BF16)
    wi_r = [Res(f"wi{k}") for k in range(8)]; wo_r = [Res(f"wo{k}") for k in range(8)]; wq_r = [Res(f"wq{k}") for k in range(8)]
    wa_bd = B("wa_bd", [128, 4, 128], BF16); wi_bd = B("wi_bd", [128, 4, 128], BF16)
    skT = [B("skT0", [128, 128], BF16), B("skT1", [128, 128], BF16)]
    vecT = B("vecT", [128, 4, NROW], F32); gT8 = B("gT8", [128, 2, 8], F32)
    clam = B("clam", [128, 4], F32); nclam = B("nclam", [128, 4], F32)
    qgrep = B("qgrep", [128, 64], F32); kgrep = B("kgrep", [128, 64], F32); esink = B("esink", [128, 8], F32)
    uext = B("uext", [128, 4, 131], F32); hstate = B("hstate", [128, 4], F32)
    kTb = [B("kT0", [64, 2, 128], BF16), B("kT1", [64, 2, 128], BF16)]
    Vaug = [B("Va0", [128, 2, 65], BF16), B("Va1", [128, 2, 65], BF16)]
    PT = {(g, nm): B(f"PT{g}{nm}", [128, 4, 128], BF16) for g in range(2) for nm in ("prev", "own")}
    Zc = B("Zc", [128, 256], BF16)
    iota16 = B("iota16", [128, 16], F32)

    C.op("pool", lambda e: e.memset(ones_f[:], 1.0), w=[ones_f])
    C.op("pool", lambda e: e.affine_select(out=identf[:], in_=ones_f[:], pattern=[[-1, 128]], compare_op=ALU.is_equal,
                                           fill=0.0, base=0, channel_multiplier=1), r=[ones_f], w=[identf])
    C.op("dve", lambda e: e.tensor_copy(out=identb[:], in_=identf[:]), r=[identf], w=[identb])
    for b_ in Vaug:
        C.op("pool", lambda e: e.memset(b_[:], 1.0), w=[b_])
    for b_ in PT.values():
        C.op("pool", lambda e: e.memset(b_[:], 0.0), w=[b_])
    C.op("pool", lambda e: e.memset(uext[:], 0.0), w=[uext])
    C.op("pool", lambda e: e.memset(hstate[:], 0.0), w=[hstate])
    C.op("pool", lambda e: e.memset(Zc[:], 0.0), w=[Zc])
    C.op("pool", lambda e: e.memset(Zc[:, 127:128], 1.0), w=[Zc])
    C.op("pool", lambda e: e.iota(iota16[:], pattern=[[1, 16]], base=0, channel_multiplier=0, allow_small_or_imprecise_dtypes=True), w=[iota16])

    with contextlib.ExitStack() as st:
        v_sb = B("v_sb", [32, 512], F32, st)
        C.dma("sp", v_sb[0:NROW, :], v512.ap(), w=[v_sb], sbuf_side=v_sb)
        for ct in range(4):
            C.op("pe", lambda e: e.transpose(P[0].f(0, 128, 512 + ct * 32, [[1, NROW]]), v_sb[0:NROW, ct * 128:(ct + 1) * 128], identf[0:NROW, 0:NROW]),
                 r=[v_sb, identf], w=[P[0].rb])
        C.op("act", lambda e: e.activation(out=vecT[:], in_=P[0].f(0, 128, 512, [[32, 4], [1, NROW]]), func=AF.Copy), r=[P[0].rb], w=[vecT])
        g_sb = B("g_sb", [16, 128], F32, st)
        C.dma("sp", g_sb[0:8, :], dap(ln1, 0, [[128, 8], [1, 128]]), w=[g_sb], sbuf_side=g_sb)
        C.dma("sp", g_sb[8:16, :], dap(ln2, 0, [[128, 8], [1, 128]]), w=[g_sb], sbuf_side=g_sb)
        C.op("pe", lambda e: e.transpose(P[0].f(0, 128, 0, [[1, 16]]), g_sb[0:16, :], identf[0:16, 0:16]), r=[g_sb, identf], w=[P[0].ra])
        C.op("act", lambda e: e.activation(out=gT8[:], in_=P[0].f(0, 128, 0, [[8, 2], [1, 8]]), func=AF.Copy), r=[P[0].ra], w=[gT8])
        stg = [B("stg0", [128, 2048], F32, st), B("stg1", [128, 2048], F32, st), B("stg2", [128, 2048], F32, st)]
        jobs = []
        for kc in range(8):
            jobs.append((dap(w_in, kc * 128 * 1792, [[1792, 128], [1, 1792]]), wi_bf[:, kc, :], 1792, wi_r[kc], gT8[:, 0, kc:kc + 1], [gT8]))
        for kc in range(8):
            sc_ = vecT[:, kc, R_GL:R_GL + 1] if kc < 4 else vecT[:, kc - 4, R_GA:R_GA + 1]
            jobs.append((dap(w_out, kc * 128 * 1024, [[1024, 128], [1, 1024]]), wo_bf[:, kc, :], 1024, wo_r[kc], sc_, [vecT]))
        for kc in range(8):
            jobs.append((dap(w_q, kc * 128 * 2048, [[2048, 128], [1, 2048]]), wq_bf[:, kc, :], 2048, wq_r[kc], gT8[:, 1, kc:kc + 1], [gT8]))
        for j, (src, dst, n, rr, scl, sr) in enumerate(jobs):
            s_ = stg[j % 3]
            C.dma("sp", s_[:, 0:n], src, w=[s_], sbuf_side=s_)
            en = ("act", "dve", "pool")[j % 3]
            if en == "act":
                C.op("act", lambda e: e.activation(out=dst, in_=s_[:, 0:n], func=AF.Copy, scale=scl), r=[s_] + sr, w=[rr])
            else:
                C.op(en, lambda e: e.tensor_scalar(out=dst, in0=s_[:, 0:n], scalar1=scl, scalar2=None, op0=ALU.mult), r=[s_] + sr, w=[rr])
        for (src_t, dstb) in ((wa, wa_bd), (wi, wi_bd)):
            sw = B("stgw", [128, 4, 128], F32, st)
            C.op("pool", lambda e: e.memset(sw[:], 0.0), w=[sw])
            C.dma("sp", sw.pap(0, 64, 0, [[128, 4], [1, 64]]), dap(src_t, 0, [[64, 64], [8192, 4], [1, 64]]), w=[sw], sbuf_side=sw)
            C.dma("sp", sw.pap(64, 64, 64, [[128, 4], [1, 64]]), dap(src_t, 4096, [[64, 64], [8192, 4], [1, 64]]), w=[sw], sbuf_side=sw)
            C.op("dve", lambda e: e.tensor_copy(out=dstb[:], in_=sw[:]), r=[sw], w=[dstb])
        for i_, src_t in enumerate((sk1, sk2)):
            sk_sb = B("sk_sb", [128, 128], F32, st)
            C.dma("sp", sk_sb[:], src_t.ap(), w=[sk_sb], sbuf_side=sk_sb)
            C.op("pe", lambda e: e.transpose(P[0].f(0, 128, i_ * 128, [[1, 128]]), sk_sb[:], identf[:]), r=[sk_sb, identf], w=[P[0].ra])
            C.op("act", lambda e: e.activation(out=skT[i_][:], in_=P[0].f(0, 128, i_ * 128, [[1, 128]]), func=AF.Copy), r=[P[0].ra], w=[skT[i_]])
        e1 = B("e1", [128, 4], F32, st)
        C.op("act", lambda e: e.activation(out=e1[:], in_=vecT[:, :, R_LAM], func=AF.Exp, scale=-1.0), r=[vecT], w=[e1])
        C.op("act", lambda e: e.activation(out=e1[:], in_=e1[:], func=AF.Ln, bias=1.0), r=[e1], w=[e1])
        C.op("dve", lambda e: e.tensor_scalar(out=clam[:], in0=e1[:], scalar1=-8.0, scalar2=None, op0=ALU.mult), r=[e1], w=[clam])
        C.op("dve", lambda e: e.tensor_scalar(out=nclam[:], in0=e1[:], scalar1=8.0, scalar2=None, op0=ALU.mult), r=[e1], w=[nclam])
        for (dt_, buf_, n_) in ((qg, qgrep, 64), (kg, kgrep, 64), (snk, esink, 8)):
            C.dma("sp", buf_[:], dap(dt_, 0, [[0, 128], [1, n_]]), w=[buf_], sbuf_side=buf_)
        C.op("act", lambda e: e.activation(out=esink[:], in_=esink[:], func=AF.Exp), r=[esink], w=[esink])
        barrier()
    with contextlib.ExitStack() as st:
        NSL = 4
        g2rep = B("g2rep", [128, 1024], F32, st)
        C.dma("sp", g2rep[:], dap(ln2, 0, [[0, 128], [1, 1024]]), w=[g2rep], sbuf_side=g2rep)
        su = [B("su", [128, 1024], F32, st) for _ in range(NSL)]
        sv = [B("sv", [128, 1024], F32, st) for _ in range(NSL)]
        tb = [B("tb", [128, 2048], BF16, st) for _ in range(NSL)]
        tbu = [Res("tbu") for _ in range(NSL)]; tbv = [Res("tbv") for _ in range(NSL)]
        for c in range(128 + 2):
            if c < 128:
                s_ = c % NSL
                C.dma("sp", su[s_][:], dap(pu, c * 128 * 1024, [[1024, 128], [1, 1024]]), w=[su[s_]], sbuf_side=su[s_])
                C.dma("sp", sv[s_][:], dap(pv, c * 128 * 1024, [[1024, 128], [1, 1024]]), w=[sv[s_]], sbuf_side=sv[s_])
                C.op("act", lambda e: e.activation(out=tb[s_][:, 1024:2048], in_=sv[s_][:], func=AF.Copy), r=[sv[s_]], w=[tbv[s_]])
                en = "dve" if c % 2 == 0 else "pool"
                C.op(en, lambda e: e.tensor_tensor(out=tb[s_][:, 0:1024], in0=su[s_][:], in1=g2rep[:], op=ALU.mult), r=[su[s_], g2rep], w=[tbu[s_]])
            cs = c - 2
            if cs >= 0:
                s2 = cs % NSL
                C.dma("sp", dap(tab, cs * 128 * 2048, [[2048, 128], [1, 2048]]), tb[s2][:], r=[tbu[s2], tbv[s2]], sbuf_side=tb[s2], is_store=True, temp_store=True)
        barrier()

    xtb = [B("xt0", [128, 1024], F32), B("xt1", [128, 1024], F32)]
    h_sb = [B("h_sb0", [128, 1024], F32), B("h_sb1", [128, 1024], F32)]
    xn2_bf = [B("xn2_0", [128, 1024], BF16), B("xn2_1", [128, 1024], BF16)]
    idxT = [B("idxT0", [128, 128], U32), B("idxT1", [128, 128], U32)]
    gT = [B("gT0", [128, 128], F32), B("gT1", [128, 128], F32)]
    NS, DPF, LAG, NW = NSLOT, NDPF, NLAG, NLAG + 3
    assert NS >= DPF + LAG + 1
    UV = [B(f"UV{i}", [128, 2048], BF16) for i in range(NS)]
    WDr = [B(f"WD{i}", [128, 128], BF16) for i in range(NW)]
    apre = B("apre", [128, 128], F32); gel = B("gel", [128, 128], F32)
    a_r = [Res(f"ap{i}") for i in range(8)]; g_r = [Res(f"gl{i}") for i in range(8)]

    ARENA = 11008
    arena = B("arena", [128, ARENA], F32)
    A_res, B_res = [], []

    class Alloc:
        def __init__(self, lst):
            self.off = 0
            self.lst = lst

        def __call__(self, name, shape, dt):
            v = VBuf(arena, self.off, shape, dt, name)
            self.off += v.nf32
            assert self.off <= ARENA, (name, self.off)
            self.lst.append(v.res)
            return v

        def res(self, name):
            r_ = Res(name)
            self.lst.append(r_)
            return r_

    VA = Alloc(A_res); VB = Alloc(B_res)
    ss1 = VA("ss1", [128, 1], F32); rs1 = VA("rs1", [128, 1], F32)
    xn_bf = VA("xn_bf", [128, 1024], BF16); xnT = VA("xnT", [128, 8, 128], BF16)
    gate = VA("gate", [128, 4, 128], F32); xc = VA("xc", [128, 4, 128], F32); xc_bf = VA("xc_bf", [128, 4, 128], BF16)
    rr = VA("rr", [128, 4, 128], F32); ig = VA("ig", [128, 4, 128], F32); aa = VA("aa", [128, 4, 128], F32)
    t1 = VA("t1", [128, 4, 128], F32); t2 = VA("t2", [128, 4, 128], F32); hh = VA("hh", [128, 4, 128], F32)
    lo = VA("lo", [128, 4, 128], F32); ril = VA("ril", [128, 128], F32)
    catT = VA("catT", [128, 8, 128], BF16); cat_l = VA.res("cat_l"); cat_a = VA.res("cat_a")
    qkv = VA("qkv", [128, 768], F32); sqq = VA("sqq", [128, 640], F32); tmpq = VA("tmpq", [128, 640], F32)
    ssq = VA("ssq", [128, 10], F32); riq = VA("riq", [128, 10], F32)
    qn_bf = VA("qn_bf", [128, 512], BF16); kn = VA("kn", [128, 128], F32); kn_bf = VA("kn_bf", [128, 128], BF16)
    qT = VA("qT", [64, 8, 128], BF16)
    den = VA("den", [128, 8], F32); o_sb = VA("o_sb", [128, 512], F32)
    ssa = VA("ssa", [128, 1], F32); rsa = VA("rsa", [128, 1], F32); an_bf = VA("an_bf", [128, 512], BF16)
    ck_sb = VA("ck_sb", [128, 128], F32); cv_sb = VA("cv_sb", [128, 128], F32); ck_bf = VA("ck_bf", [128, 128], BF16)
    ss2 = VB("ss2", [128, 1], F32)
    xn2T = VB("xn2T", [128, 8, 128], BF16); pq_bf = VB("pq_bf", [128, 16, 128], BF16)
    S = VB("S", [128, 16, 128], F32); v16 = VB("v16", [128, 16, 16], F32); i16 = VB("i16", [128, 16, 16], U32)
    i16f = VB("i16f", [128, 16, 16], F32); big = VB("big", [128, 2048], F32)
    tmp = [VB("tmpa", [128, 128], F32), VB("tmpb", [128, 128], F32)]
    tmp2 = [VB("tmp2a", [128, 256], F32), VB("tmp2b", [128, 256], F32)]
    ts = VB("ts", [128, 8, 16], F32); tp = VB("tp", [128, 8, 16], U32)
    pA = VB("pA", [128, 128], U32); pB = VB("pB", [128, 128], U32); pAf = VB("pAf", [128, 128], F32); pBf = VB("pBf", [128, 128], F32)
    i1s = VB("i1s", [128, 128], F32); i2s = VB("i2s", [128, 128], F32); idxf = VB("idxf", [128, 128], F32)
    ge = VB("ge", [128, 128], F32); gs = VB("gs", [128, 8], F32)
    pq_r = [VB.res(f"pq{i}") for i in range(4)]; S_r = [VB.res(f"S{i}") for i in range(4)]
    v_r = [VB.res(f"v{i}") for i in range(16)]; i_r = [VB.res(f"i{i}") for i in range(16)]
    t_r = [VB.res(f"t{i}") for i in range(8)]; p_r = [VB.res(f"p{i}") for i in range(8)]

    def inherit(news, olds):
        d = {}
        for o in olds:
            toks = list(o.reads)
            if o.last_w is not None:
                toks.append(o.last_w)
            for t in toks:
                k = id(t[0])
                if k not in d or d[k][1] < t[1]:
                    d[k] = t
        for n_ in news:
            n_.reads = list(n_.reads) + list(d.values())

    tiles = []
    for n in range(16):
        tiles.append(dict(kind="p", n=n, nt=128, x=dap(xp, n * 128 * 1024, [[1024, 128], [1, 1024]]),
                          y=dap(yp, n * 128 * 1024, [[1024, 128], [1, 1024]]), first=(n == 0), last=(n == 15)))
    for s in range(2):
        tiles.append(dict(kind="s", s=s, nt=16, x=dap(xs, s * 16 * 1024, [[1024, 16], [1, 1024]]),
                          y=dap(ys, s * 16 * 1024, [[1024, 16], [1, 1024]]), first=True, last=True))

    F0 = P[0]

    def front(gi):
        T = tiles[gi]
        nt = T["nt"]
        xt = xtb[gi % 2]
        hb = h_sb[gi % 2]; x2 = xn2_bf[gi % 2]; ixT = idxT[gi % 2]; gtT = gT[gi % 2]
        C.dma("sp", xt[0:nt, :], T["x"], w=[xt], sbuf_side=xt)
        DBG["fstart"] = C.eng_time["sp"]
        samp = T["kind"] == "s"
        if samp:
            sp_, so_ = 0, 1
        else:
            so_ = T["n"] % 2
            sp_ = 1 - so_
        inherit(A_res, B_res)
        C.op("dve", lambda e: e.scalar_tensor_tensor(out=xn_bf[0:nt, :], in0=xt[0:nt, :], scalar=1.0, in1=xt[0:nt, :], op0=ALU.mult,
                                                     op1=ALU.mult, accum_out=ss1[0:nt, :]), r=[xt], w=[xn_bf, ss1])
        C.op("act", lambda e: e.activation(out=rs1[0:nt, :], in_=ss1[0:nt, :], func=AF.Sqrt, scale=1.0 / 1024, bias=EPS), r=[ss1], w=[rs1])
        C.op("dve", lambda e: e.reciprocal(out=rs1[0:nt, :], in_=rs1[0:nt, :]), r=[rs1], w=[rs1])
        C.op("act", lambda e: e.activation(out=xn_bf[0:nt, :], in_=xt[0:nt, :], func=AF.Copy, scale=rs1[0:nt, 0:1]),
             r=[xt, rs1], w=[xn_bf])
        for kc in range(8):
            C.op("pe", lambda e: e.transpose(F0.bf(0, 128, kc * 128, [[1, nt]]), xn_bf[0:nt, kc * 128:(kc + 1) * 128], identb[0:nt, 0:nt]),
                 r=[xn_bf, identb], w=[F0.ra])
        C.op("act", lambda e: e.activation(out=xnT[:, :, 0:nt], in_=F0.bf(0, 128, 0, [[128, 8], [1, nt]]), func=AF.Copy), r=[F0.ra], w=[xnT])
        for ct in range(8):
            hf = ct // 4
            for kc in range(8):
                C.op("pe", lambda e: e.matmul(F0.f(0, 128, hf * 512 + (ct % 4) * 128, [[1, nt]]), wi_bf[:, kc, ct * 128:(ct + 1) * 128],
                                              xnT[:, kc, 0:nt], start=(kc == 0), stop=(kc == 7)), r=[wi_r[kc], xnT], w=[F0.r(hf)])
        if samp:
            s = T["s"]
            C.op("pool", lambda e: e.tensor_copy(out=uext[:, :, 0:3], in_=vecT[:, :, R_SC + 3 * s:R_SC + 3 * s + 3]), r=[vecT], w=[uext])
            C.op("pool", lambda e: e.tensor_copy(out=hstate[:], in_=vecT[:, :, R_SL + s]), r=[vecT], w=[hstate])
        C.op("act", lambda e: e.activation(out=uext[:, :, 3:3 + nt], in_=F0.f(0, 128, 0, [[128, 4], [1, nt]]), func=AF.Copy), r=[F0.ra], w=[uext])
        C.op("act", lambda e: e.activation(out=gate[:, :, 0:nt], in_=F0.f(0, 128, 512, [[128, 4], [1, nt]]), func=AF.Copy), r=[F0.rb], w=[gate])
        for hf, (c0, c1) in enumerate(((1024, 1536), (1536, 1792))):
            for kc in range(8):
                C.op("pe", lambda e: e.matmul(F0.f(0, nt, hf * 512, [[1, c1 - c0]]), xnT[:, kc, 0:nt], wi_bf[:, kc, c0:c1],
                                              start=(kc == 0), stop=(kc == 7)), r=[wi_r[kc], xnT], w=[F0.r(hf)])
        C.op("act", lambda e: e.activation(out=qkv[0:nt, :], in_=F0.f(0, nt, 0, [[1, 768]]), func=AF.Copy), r=F0.both, w=[qkv])
        for ct in range(4):
            C.op("dve", lambda e: e.tensor_scalar(out=xc[:, ct, 0:nt], in0=uext[:, ct, 0:nt], scalar1=vecT[:, ct, R_CW:R_CW + 1],
                                                  scalar2=vecT[:, ct, R_CB:R_CB + 1], op0=ALU.mult, op1=ALU.add), r=[uext, vecT], w=[xc])
            for j in range(1, 4):
                C.op("dve", lambda e: e.scalar_tensor_tensor(out=xc[:, ct, 0:nt], in0=uext[:, ct, j:j + nt], scalar=vecT[:, ct, R_CW + j:R_CW + j + 1],
                                                             in1=xc[:, ct, 0:nt], op0=ALU.mult, op1=ALU.add), r=[uext, vecT, xc], w=[xc])
        C.op("act", lambda e: e.activation(out=xc_bf[:, :, 0:nt], in_=xc[:, :, 0:nt], func=AF.Copy), r=[xc], w=[xc_bf])
        if T["last"]:
            dstt, base = (ncp, 0) if not samp else (ncs, T["s"] * 1536)
            for ct in range(4):
                C.dma("sp", None, None, r=[uext], sbuf_side=uext, is_store=True, final=True,
                      fn=lambda e: e.dma_start(out=dap(dstt, base + ct * 128, [[1, 128], [512, 3]]), in_=uext[:, ct, nt:nt + 3],
                                               allow_slow_non_contiguous=True))
        else:
            C.op("pool", lambda e: e.tensor_copy(out=uext[:, :, 0:3], in_=uext[:, :, nt:nt + 3]), r=[uext], w=[uext])
        for ct in range(4):
            C.op("pe", lambda e: e.matmul(F0.f(0, 128, ct * 128, [[1, nt]]), wa_bd[:, ct, :], xc_bf[:, ct, 0:nt], start=True, stop=True),
                 r=[wa_bd, xc_bf], w=[F0.ra])
            C.op("pe", lambda e: e.matmul(F0.f(0, 128, 512 + ct * 128, [[1, nt]]), wi_bd[:, ct, :], xc_bf[:, ct, 0:nt], start=True, stop=True),
                 r=[wi_bd, xc_bf], w=[F0.rb])
        for ct in range(4):
            C.op("act", lambda e: e.activation(out=rr[:, ct, 0:nt], in_=F0.f(0, 128, ct * 128, [[1, nt]]), func=AF.Sigmoid,
                                               bias=vecT[:, ct, R_BA:R_BA + 1]), r=[F0.ra, vecT], w=[rr])
        for ct in range(4):
            C.op("act", lambda e: e.activation(out=ig[:, ct, 0:nt], in_=F0.f(0, 128, 512 + ct * 128, [[1, nt]]), func=AF.Sigmoid,
                                               bias=vecT[:, ct, R_BI:R_BI + 1]), r=[F0.rb, vecT], w=[ig])
        for ct in range(4):
            C.op("act", lambda e: e.activation(out=aa[:, ct, 0:nt], in_=rr[:, ct, 0:nt], func=AF.Exp, scale=clam[:, ct:ct + 1]), r=[rr, clam], w=[aa])
        for ct in range(4):
            C.op("act", lambda e: e.activation(out=t1[:, ct, 0:nt], in_=rr[:, ct, 0:nt], func=AF.Tanh, scale=nclam[:, ct:ct + 1]), r=[rr, nclam], w=[t1])
        C.op("pool", lambda e: e.tensor_tensor(out=t2[:, :, 0:nt], in0=aa[:, :, 0:nt], in1=aa[:, :, 0:nt], op=ALU.mult), r=[aa], w=[t2])
        C.op("dve", lambda e: e.scalar_tensor_tensor(out=t2[:, :, 0:nt], in0=t2[:, :, 0:nt], scalar=1.0, in1=t1[:, :, 0:nt], op0=ALU.add, op1=ALU.mult),
             r=[t2, t1], w=[t2])
        C.op("act", lambda e: e.activation(out=t2[:, :, 0:nt], in_=t2[:, :, 0:nt], func=AF.Sqrt), r=[t2], w=[t2])
        C.op("pool", lambda e: e.tensor_tensor(out=ig[:, :, 0:nt], in0=ig[:, :, 0:nt], in1=xc[:, :, 0:nt], op=ALU.mult), r=[ig, xc], w=[ig])
        C.op("pool", lambda e: e.tensor_tensor(out=ig[:, :, 0:nt], in0=ig[:, :, 0:nt], in1=t2[:, :, 0:nt], op=ALU.mult), r=[ig, t2], w=[ig])
        for ct in range(4):
            C.op("dve", lambda e: e.tensor_tensor_scan(out=hh[:, ct, 0:nt], data0=aa[:, ct, 0:nt], data1=ig[:, ct, 0:nt], initial=hstate[:, ct:ct + 1],
                                                       op0=ALU.mult, op1=ALU.add), r=[aa, ig, hstate], w=[hh])
        C.op("dve", lambda e: e.tensor_copy(out=hstate[:], in_=hh[:, :, nt - 1]), r=[hh], w=[hstate])
        if T["last"]:
            dstt, base = (nhp, 0) if not samp else (nhs, T["s"] * 512)
            C.dma("sp", None, None, r=[hstate], sbuf_side=hstate, is_store=True, final=True,
                  fn=lambda e: e.dma_start(out=dap(dstt, base, [[1, 128], [128, 4]]), in_=hstate[:], allow_slow_non_contiguous=True))
        C.op("act", lambda e: e.activation(out=t1[:, :, 0:nt], in_=gate[:, :, 0:nt], func=AF.Gelu_apprx_tanh), r=[gate], w=[t1])
        C.op("pool", lambda e: e.tensor_tensor(out=lo[:, :, 0:nt], in0=hh[:, :, 0:nt], in1=t1[:, :, 0:nt], op=ALU.mult), r=[hh, t1], w=[lo])
        C.op("pool", lambda e: e.tensor_tensor(out=t1[:, :, 0:nt], in0=lo[:, :, 0:nt], in1=lo[:, :, 0:nt], op=ALU.mult), r=[lo], w=[t1])
        for ct in range(4):
            C.op("pe", lambda e: e.matmul(F0.f(0, 128, 0, [[1, nt]]), ones_f[:], t1[:, ct, 0:nt], start=(ct == 0), stop=(ct == 3)),
                 r=[ones_f, t1], w=[F0.ra])
        C.op("act", lambda e: e.activation(out=ril[:, 0:nt], in_=F0.f(0, 128, 0, [[1, nt]]), func=AF.Sqrt, scale=1.0 / 512, bias=EPS), r=[F0.ra], w=[ril])
        C.op("dve", lambda e: e.reciprocal(out=ril[:, 0:nt], in_=ril[:, 0:nt]), r=[ril], w=[ril])
        C.op("pool", lambda e: e.tensor_tensor(out=catT[:, 0:4, 0:nt], in0=lo[:, :, 0:nt], in1=ril.pap(0, 128, 0, [[0, 4], [1, nt]]), op=ALU.mult),
             r=[lo, ril], w=[cat_l])
        C.op("pool", lambda e: e.tensor_tensor(out=sqq[0:nt, :], in0=qkv[0:nt, 0:640], in1=qkv[0:nt, 0:640], op=ALU.mult), r=[qkv], w=[sqq])
        C.op("dve", lambda e: e.tensor_reduce(out=ssq[0:nt, :], in_=sqq.pap(0, nt, 0, [[64, 10], [1, 64]]), axis=AX.X, op=ALU.add), r=[sqq], w=[ssq])
        C.op("act", lambda e: e.activation(out=riq[0:nt, :], in_=ssq[0:nt, :], func=AF.Sqrt, scale=1.0 / 64, bias=EPS), r=[ssq], w=[riq])
        C.op("dve", lambda e: e.reciprocal(out=riq[0:nt, :], in_=riq[0:nt, :]), r=[riq], w=[riq])
        C.op("pool", lambda e: e.tensor_tensor(out=tmpq.pap(0, nt, 0, [[64, 10], [1, 64]]), in0=qkv.pap(0, nt, 0, [[64, 10], [1, 64]]),
                                               in1=riq.pap(0, nt, 0, [[1, 10], [0, 64]]), op=ALU.mult), r=[qkv, riq], w=[tmpq])
        C.op("pool", lambda e: e.tensor_tensor(out=qn_bf.pap(0, nt, 0, [[64, 8], [1, 64]]), in0=tmpq.pap(0, nt, 0, [[64, 8], [1, 64]]),
                                               in1=qgrep.pap(0, nt, 0, [[0, 8], [1, 64]]), op=ALU.mult), r=[tmpq, qgrep], w=[qn_bf])
        C.op("dve", lambda e: e.tensor_tensor(out=kn.pap(0, nt, 0, [[64, 2], [1, 64]]), in0=tmpq.pap(0, nt, 512, [[64, 2], [1, 64]]),
                                              in1=kgrep.pap(0, nt, 0, [[0, 2], [1, 64]]), op=ALU.mult), r=[tmpq, kgrep], w=[kn])
        C.op("act", lambda e: e.activation(out=kn_bf[0:nt, :], in_=kn[0:nt, :], func=AF.Copy), r=[kn], w=[kn_bf])
        C.op("act", lambda e: e.activation(out=Vaug[so_].pap(0, nt, 0, [[65, 2], [1, 64]]), in_=qkv.pap(0, nt, 640, [[64, 2], [1, 64]]), func=AF.Copy),
             r=[qkv], w=[Vaug[so_]])
        if samp:
            s = T["s"]
            C.dma("sp", ck_sb[:], dap(ck, s * 16384, [[128, 128], [1, 128]]), w=[ck_sb], sbuf_side=ck_sb)
            C.dma("sp", cv_sb[:], dap(cv, s * 16384, [[128, 128], [1, 128]]), w=[cv_sb], sbuf_side=cv_sb)
            C.op("pool", lambda e: e.tensor_copy(out=ck_bf[:], in_=ck_sb[:]), r=[ck_sb], w=[ck_bf])
            C.op("pool", lambda e: e.tensor_copy(out=Vaug[sp_].pap(0, 128, 0, [[65, 2], [1, 64]]), in_=cv_sb.pap(0, 128, 0, [[64, 2], [1, 64]])),
                 r=[cv_sb], w=[Vaug[sp_]])
            for g in range(2):
                C.op("pe", lambda e: e.transpose(F0.bf(0, 64, 1024 + 256 + g * 128, [[1, 128]]), ck_bf[:, g * 64:(g + 1) * 64], identb[:]),
                     r=[ck_bf, identb], w=[F0.rb])
            C.op("dve", lambda e: e.tensor_copy(out=kTb[sp_][:], in_=F0.bf(0, 64, 1024 + 256, [[128, 2], [1, 128]])), r=[F0.rb], w=[kTb[sp_]])
            C.dma("sp", dap(nks, s * 16384, [[128, 112], [1, 128]]), ck_sb[16:128, :], r=[ck_sb], sbuf_side=ck_sb, is_store=True, final=True)
            C.dma("sp", dap(nvs, s * 16384, [[128, 112], [1, 128]]), cv_sb[16:128, :], r=[cv_sb], sbuf_side=cv_sb, is_store=True, final=True)
            C.dma("sp", dap(nks, s * 16384 + 112 * 128, [[128, 16], [1, 128]]), kn[0:16, :], r=[kn], sbuf_side=kn, is_store=True, final=True)
            C.dma("sp", dap(nvs, s * 16384 + 112 * 128, [[128, 16], [1, 128]]), qkv[0:16, 640:768], r=[qkv], sbuf_side=qkv, is_store=True, final=True)
        elif T["last"]:
            C.dma("sp", nkp.ap(), kn[:, :], r=[kn], sbuf_side=kn, is_store=True, final=True)
            C.dma("sp", nvp.ap(), qkv[:, 640:768], r=[qkv], sbuf_side=qkv, is_store=True, final=True)
        for h in range(8):
            C.op("pe", lambda e: e.transpose(F0.bf(0, 64, h * 128, [[1, nt]]), qn_bf[0:nt, h * 64:(h + 1) * 64], identb[0:nt, 0:nt]),
                 r=[qn_bf, identb], w=[F0.ra])
        for g in range(2):
            C.op("pe", lambda e: e.transpose(F0.bf(0, 64, 1024 + g * 128, [[1, nt]]), kn_bf[0:nt, g * 64:(g + 1) * 64], identb[0:nt, 0:nt]),
                 r=[kn_bf, identb], w=[F0.rb])
        C.op("act", lambda e: e.activation(out=qT[0:64, :, 0:nt], in_=F0.bf(0, 64, 0, [[128, 8], [1, nt]]), func=AF.Copy), r=[F0.ra], w=[qT])
        C.op("dve", lambda e: e.tensor_copy(out=kTb[so_][:, :, 0:nt], in_=F0.bf(0, 64, 1024, [[128, 2], [1, nt]])), r=[F0.rb], w=[kTb[so_]])
        if samp:
            blocks = [("prev", sp_, 128, [(0, 128, 0, 16)]), ("own", so_, 16, [(0, 16, 0, 16)])]
        else:
            blocks = []
            if not T["first"]:
                blocks.append(("prev", sp_, 128, [(0, 128, 0, 64), (64, 128, 64, 128)]))
            blocks.append(("own", so_, 128, [(0, 64, 0, 64), (0, 128, 64, 128)]))
        k_ = 0
        for g in range(2):
            for bi, (nm, slot, nk, regions) in enumerate(blocks):
                hf = k_ % 2
                k_ += 1
                C.op("pe", lambda e: e.matmul(F0.f(0, nk, hf * 512, [[128, 4], [1, nt]]), kTb[slot][:, g, 0:nk], qT[0:64, 4 * g:4 * g + 4, 0:nt],
                                              start=True, stop=True), r=[kTb[slot], qT], w=[F0.r(hf)])
                for (k0, k1, q0, q1) in regions:
                    C.op("act", lambda e: e.activation(out=PT[(g, nm)].pap(k0, k1 - k0, q0, [[128, 4], [1, q1 - q0]]),
                                                       in_=F0.f(k0, k1 - k0, hf * 512 + q0, [[128, 4], [1, q1 - q0]]), func=AF.Exp, scale=0.125),
                         r=[F0.r(hf)], w=[PT[(g, nm)]])
        for h in range(8):
            g, h4 = h // 4, h % 4
            for bi, (nm, slot, nk, regions) in enumerate(blocks):
                C.op("pe", lambda e: e.matmul(F0.f(0, nt, g * 512 + h4 * 65, [[1, 65]]), PT[(g, nm)].pap(0, nk, h4 * 128, [[1, nt]]),
                                              Vaug[slot].pap(0, nk, g * 65, [[1, 65]]), start=(bi == 0), stop=(bi == len(blocks) - 1)),
                     r=[PT[(g, nm)], Vaug[slot]], w=[F0.r(g)])
        C.op("dve", lambda e: e.tensor_tensor(out=den.pap(0, nt, 0, [[4, 2], [1, 4]]), in0=F0.f(0, nt, 64, [[512, 2], [65, 4]]),
                                              in1=esink.pap(0, nt, 0, [[4, 2], [1, 4]]), op=ALU.add), r=F0.both + [esink], w=[den])
        C.op("dve", lambda e: e.reciprocal(out=den[0:nt, :], in_=den[0:nt, :]), r=[den], w=[den])
        C.op("dve", lambda e: e.tensor_tensor(out=o_sb.pap(0, nt, 0, [[256, 2], [64, 4], [1, 64]]), in0=F0.f(0, nt, 0, [[512, 2], [65, 4], [1, 64]]),
                                              in1=den.pap(0, nt, 0, [[4, 2], [1, 4], [0, 64]]), op=ALU.mult), r=F0.both + [den], w=[o_sb])
        C.op("dve", lambda e: e.scalar_tensor_tensor(out=an_bf[0:nt, :], in0=o_sb[0:nt, :], scalar=1.0, in1=o_sb[0:nt, :], op0=ALU.mult, op1=ALU.mult,
                                                     accum_out=ssa[0:nt, :]), r=[o_sb], w=[an_bf, ssa])
        C.op("act", lambda e: e.activation(out=rsa[0:nt, :], in_=ssa[0:nt, :], func=AF.Sqrt, scale=1.0 / 512, bias=EPS), r=[ssa], w=[rsa])
        C.op("dve", lambda e: e.reciprocal(out=rsa[0:nt, :], in_=rsa[0:nt, :]), r=[rsa], w=[rsa])
        C.op("act", lambda e: e.activation(out=an_bf[0:nt, :], in_=o_sb[0:nt, :], func=AF.Copy, scale=rsa[0:nt, 0:1]),
             r=[o_sb, rsa], w=[an_bf])
        for j in range(4):
            C.op("pe", lambda e: e.transpose(F0.bf(0, 128, j * 128, [[1, nt]]), an_bf[0:nt, j * 128:(j + 1) * 128], identb[0:nt, 0:nt]),
                 r=[an_bf, identb], w=[F0.ra])
        C.op("act", lambda e: e.activation(out=catT[:, 4:8, 0:nt], in_=F0.bf(0, 128, 0, [[128, 4], [1, nt]]), func=AF.Copy), r=[F0.ra], w=[cat_a])
        for hf in range(2):
            for c in range(8):
                C.op("pe", lambda e: e.matmul(F0.f(0, nt, hf * 512, [[1, 512]]), catT[:, c, 0:nt], wo_bf[:, c, hf * 512:(hf + 1) * 512],
                                              start=(c == 0), stop=(c == 7)), r=[cat_l if c < 4 else cat_a, wo_r[c]], w=[F0.r(hf)])
        C.op("dve", lambda e: e.tensor_tensor(out=hb[0:nt, :], in0=xt[0:nt, :], in1=F0.f(0, nt, 0, [[1, 1024]]), op=ALU.add), r=[xt] + F0.both, w=[hb])
        inherit(B_res, A_res)
        C.op("dve", lambda e: e.scalar_tensor_tensor(out=x2[0:nt, :], in0=hb[0:nt, :], scalar=1.0, in1=hb[0:nt, :], op0=ALU.mult, op1=ALU.mult,
                                                     accum_out=ss2[0:nt, :]), r=[hb], w=[x2, ss2])
        C.op("act", lambda e: e.activation(out=ss2[0:nt, :], in_=ss2[0:nt, :], func=AF.Sqrt, scale=1.0 / 1024, bias=EPS), r=[ss2], w=[ss2])
        C.op("dve", lambda e: e.reciprocal(out=ss2[0:nt, :], in_=ss2[0:nt, :]), r=[ss2], w=[ss2])
        C.op("act", lambda e: e.activation(out=x2[0:nt, :], in_=hb[0:nt, :], func=AF.Copy, scale=ss2[0:nt, 0:1]),
             r=[hb, ss2], w=[x2])
        for kc in range(8):
            C.op("pe", lambda e: e.transpose(F0.bf(0, 128, kc * 128, [[1, nt]]), x2[0:nt, kc * 128:(kc + 1) * 128], identb[0:nt, 0:nt]),
                 r=[x2, identb], w=[F0.ra])
        C.op("act", lambda e: e.activation(out=xn2T[:, :, 0:nt], in_=F0.bf(0, 128, 0, [[128, 8], [1, nt]]), func=AF.Copy), r=[F0.ra], w=[xn2T])
        for b4 in range(4):
            hf = b4 % 2
            for g4 in range(4):
                grp = b4 * 4 + g4
                for kc in range(8):
                    C.op("pe", lambda e: e.matmul(F0.f(0, 128, hf * 512 + g4 * 128, [[1, nt]]), wq_bf[:, kc, grp * 128:(grp + 1) * 128], xn2T[:, kc, 0:nt],
                                                  start=(kc == 0), stop=(kc == 7)), r=[wq_r[kc], xn2T], w=[F0.r(hf)])
            if b4 % 2 == 0:
                C.op("act", lambda e: e.activation(out=pq_bf[:, 4 * b4:4 * b4 + 4, 0:nt], in_=F0.f(0, 128, hf * 512, [[128, 4], [1, nt]]), func=AF.Copy),
                     r=[F0.r(hf)], w=[pq_r[b4]])
            else:
                C.op("dve", lambda e: e.tensor_copy(out=pq_bf[:, 4 * b4:4 * b4 + 4, 0:nt], in_=F0.f(0, 128, hf * 512, [[128, 4], [1, nt]])),
                     r=[F0.r(hf)], w=[pq_r[b4]])
        for b4 in range(4):
            hf = b4 % 2
            for g4 in range(4):
                grp = b4 * 4 + g4
                C.op("pe", lambda e: e.matmul(F0.f(0, nt, hf * 512 + g4 * 128, [[1, 128]]), pq_bf[:, grp, 0:nt], skT[grp % 2][:], start=True, stop=True),
                     r=[pq_r[b4], skT[grp % 2]], w=[F0.r(hf)])
            if b4 % 2 == 0:
                C.op("act", lambda e: e.activation(out=S.pap(0, nt, b4 * 512, [[1, 512]]), in_=F0.f(0, nt, hf * 512, [[1, 512]]), func=AF.Copy), r=[F0.r(hf)], w=[S_r[b4]])
            else:
                C.op("dve", lambda e: e.tensor_copy(out=S.pap(0, nt, b4 * 512, [[1, 512]]), in_=F0.f(0, nt, hf * 512, [[1, 512]])), r=[F0.r(hf)], w=[S_r[b4]])
        C.mark()
        for grp in range(16):
            tb_ = tmp[grp % 2]; sr = S_r[grp // 4]
            C.op("dve", lambda e: e.max(out=v16[0:nt, grp, 0:8], in_=S[0:nt, grp, :]), r=[sr], w=[v_r[grp]])
            C.op("dve", lambda e: e.max_index(out=i16[0:nt, grp, 0:8], in_max=v16[0:nt, grp, 0:8], in_values=S[0:nt, grp, :]), r=[sr, v_r[grp]], w=[i_r[grp]])
            C.op("dve", lambda e: e.match_replace(out=tb_[0:nt, :], in_to_replace=v16[0:nt, grp, 0:8], in_values=S[0:nt, grp, :], imm_value=NEG), r=[sr, v_r[grp]], w=[tb_])
            C.op("dve", lambda e: e.max(out=v16[0:nt, grp, 8:16], in_=tb_[0:nt, :]), r=[tb_], w=[v_r[grp]])
            C.op("dve", lambda e: e.max_index(out=i16[0:nt, grp, 8:16], in_max=v16[0:nt, grp, 8:16], in_values=tb_[0:nt, :]), r=[tb_, v_r[grp]], w=[i_r[grp]])
        C.op("act", lambda e: e.activation(out=i16f[0:nt, :, :], in_=i16[0:nt, :, :], func=AF.Copy), r=i_r, w=[i16f])
        C.op("dve", lambda e: e.tensor_tensor(out=big.pap(0, nt, 0, [[256, 8], [16, 16], [1, 16]]), in0=v16.pap(0, nt, 0, [[32, 8], [1, 16], [0, 16]]),
                                               in1=v16.pap(0, nt, 16, [[32, 8], [0, 16], [1, 16]]), op=ALU.add), r=v_r, w=[big])
        for h in range(8):
            tb_ = tmp2[h % 2]
            cand = big[0:nt, h * 256:(h + 1) * 256]
            C.op("dve", lambda e: e.max(out=ts[0:nt, h, 0:8], in_=cand), r=[big], w=[t_r[h]])
            C.op("dve", lambda e: e.max_index(out=tp[0:nt, h, 0:8], in_max=ts[0:nt, h, 0:8], in_values=cand), r=[big, t_r[h]], w=[p_r[h]])
            C.op("dve", lambda e: e.match_replace(out=tb_[0:nt, :], in_to_replace=ts[0:nt, h, 0:8], in_values=cand, imm_value=NEG), r=[big, t_r[h]], w=[tb_])
            C.op("dve", lambda e: e.max(out=ts[0:nt, h, 8:16], in_=tb_[0:nt, :]), r=[tb_], w=[t_r[h]])
            C.op("dve", lambda e: e.max_index(out=tp[0:nt, h, 8:16], in_max=ts[0:nt, h, 8:16], in_values=tb_[0:nt, :]), r=[tb_, t_r[h]], w=[p_r[h]])
        C.op("dve", lambda e: e.tensor_scalar(out=pA[0:nt, :], in0=tp.pap(0, nt, 0, [[1, 128]]), scalar1=4, scalar2=None, op0=ALU.logical_shift_right), r=p_r, w=[pA])
        C.op("dve", lambda e: e.tensor_scalar(out=pB[0:nt, :], in0=tp.pap(0, nt, 0, [[1, 128]]), scalar1=15, scalar2=None, op0=ALU.bitwise_and), r=p_r, w=[pB])
        C.op("act", lambda e: e.activation(out=pAf[0:nt, :], in_=pA[0:nt, :], func=AF.Copy), r=[pA], w=[pAf])
        C.op("act", lambda e: e.activation(out=pBf[0:nt, :], in_=pB[0:nt, :], func=AF.Copy), r=[pB], w=[pBf])
        for (pf, ioff, dst) in ((pAf, 0, i1s), (pBf, 16, i2s)):
            C.op("dve", lambda e: e.tensor_tensor(out=big.pap(0, nt, 0, [[256, 8], [16, 16], [1, 16]]), in0=iota16.pap(0, nt, 0, [[0, 8], [0, 16], [1, 16]]),
                                                  in1=pf.pap(0, nt, 0, [[16, 8], [1, 16], [0, 16]]), op=ALU.is_equal), r=[iota16, pf], w=[big])
            C.op("dve", lambda e: e.tensor_tensor(out=big.pap(0, nt, 0, [[256, 8], [16, 16], [1, 16]]), in0=big.pap(0, nt, 0, [[256, 8], [16, 16], [1, 16]]),
                                                   in1=i16f.pap(0, nt, ioff, [[32, 8], [0, 16], [1, 16]]), op=ALU.mult), r=[big, i16f], w=[big])
            C.op("dve", lambda e: e.tensor_reduce(out=dst[0:nt, :], in_=big.pap(0, nt, 0, [[16, 128], [1, 16]]), axis=AX.X, op=ALU.add), r=[big], w=[dst])
        C.op("dve", lambda e: e.scalar_tensor_tensor(out=idxf[0:nt, :], in0=i1s[0:nt, :], scalar=128.0, in1=i2s[0:nt, :], op0=ALU.mult, op1=ALU.add),
             r=[i1s, i2s], w=[idxf])
        C.op("dve", lambda e: e.tensor_tensor(out=ge.pap(0, nt, 0, [[16, 8], [1, 16]]), in0=ts.pap(0, nt, 0, [[16, 8], [1, 16]]),
                                               in1=ts.pap(0, nt, 0, [[16, 8], [0, 16]]), op=ALU.subtract), r=t_r, w=[ge])
        C.op("act", lambda e: e.activation(out=ge[0:nt, :], in_=ge[0:nt, :], func=AF.Exp), r=[ge], w=[ge])
        C.op("dve", lambda e: e.tensor_reduce(out=gs[0:nt, :], in_=ge.pap(0, nt, 0, [[16, 8], [1, 16]]), axis=AX.X, op=ALU.add), r=[ge], w=[gs])
        C.op("dve", lambda e: e.reciprocal(out=gs[0:nt, :], in_=gs[0:nt, :]), r=[gs], w=[gs])
        C.op("dve", lambda e: e.tensor_tensor(out=ge.pap(0, nt, 0, [[16, 8], [1, 16]]), in0=ge.pap(0, nt, 0, [[16, 8], [1, 16]]),
                                               in1=gs.pap(0, nt, 0, [[1, 8], [0, 16]]), op=ALU.mult), r=[ge, gs], w=[ge])
        C.op("pe", lambda e: e.transpose(F0.f(0, 128, 0, [[1, nt]]), idxf[0:nt, :], identf[0:nt, 0:nt]), r=[idxf, identf], w=[F0.ra])
        C.op("pe", lambda e: e.transpose(F0.f(0, 128, 512, [[1, nt]]), ge[0:nt, :], identf[0:nt, 0:nt]), r=[ge, identf], w=[F0.rb])
        C.op("dve", lambda e: e.tensor_copy(out=ixT[:, 0:nt], in_=F0.f(0, 128, 0, [[1, nt]])), r=[F0.ra], w=[ixT])
        C.op("act", lambda e: e.activation(out=gtT[:, 0:nt], in_=F0.f(0, 128, 512, [[1, nt]]), func=AF.Copy), r=[F0.rb], w=[gtT])
        DBG["fend"] = C.eng_time["act"]

    bcP = [P[1], P[2]]

    def back(gi, stepper, kstep):
        T = tiles[gi]
        nt = T["nt"]
        xt = xtb[gi % 2]
        hb = h_sb[gi % 2]; x2 = xn2_bf[gi % 2]; ixT = idxT[gi % 2]; gtT = gT[gi % 2]

        def gather(t):
            uv = UV[t % NS]
            C.dma("pool", None, None, r=[ixT], w=[uv], sbuf_side=uv,
                  fn=lambda e: e.indirect_dma_start(out=uv[:], out_offset=None, in_=tab.ap(),
                                                    in_offset=bass.IndirectOffsetOnAxis(ap=ixT[:, t:t + 1], axis=0)))

        credit = {k_: 0.0 for k_ in BUDGET}
        tcur = [0]

        def pump():
            if stepper is None:
                return
            while not stepper.finished:
                ne = stepper.next_e
                if ne in C.eng_time:
                    if credit.get(ne, 1.0) <= 0.0:
                        break
                    r_, w_ = stepper.next_rw
                    ready = C.pred_ready(ne, r_, w_)
                    if ready > max(C.eng_time[ne], C.front_time) + SLACK:
                        break
                    t_before = max(C.eng_time[ne], ready)
                    stepper.step(1)
                    if ne in credit:
                        credit[ne] -= max(0.05, C.eng_time[ne] - t_before)
                else:
                    stepper.step(1)

        f_base = C.ninst_f

        def refill():
            mult = 1.0
            if stepper is not None and NF[0] > 0:
                exp_frac = min(1.0, (tcur[0] + 1) / (FIN_FRAC * nt))
                act_frac = (C.ninst_f - f_base) / float(NF[0])
                if act_frac < exp_frac:
                    mult = BOOST
            for e_ in credit:
                credit[e_] = min(credit[e_] + mult * BUDGET[e_], 3.0 * mult * BUDGET[e_])

        C.pump = pump if stepper is not None else None
        for t in range(min(DPF, nt)):
            gather(t)
        for t in range(nt + LAG):
            if t < nt:
                if t + DPF < nt:
                    gather(t + DPF)
                pp = bcP[t % 2]; uv = UV[t % NS]
                for hf in range(2):
                    C.op("pe", lambda e: e.matmul(pp.f(0, 128, hf * 512, [[1, 512]]), identb.pap(0, nt, t, [[0, 128]]), x2[0:nt, hf * 512:(hf + 1) * 512],
                                                  start=True, stop=True), r=[identb, x2], w=[pp.r(hf)])
                C.op("dve", lambda e: e.scalar_tensor_tensor(out=uv[:, 0:1024], in0=uv[:, 0:1024], scalar=1.0, in1=pp.f(0, 128, 0, [[1, 1024]]), op0=ALU.mult, op1=ALU.mult,
                                                             accum_out=apre[:, t:t + 1]), r=[uv] + pp.both, w=[uv, a_r[t % 8]])
                C.op("act", lambda e: e.activation(out=gel[:, t:t + 1], in_=apre[:, t:t + 1], func=AF.Gelu_apprx_tanh), r=[a_r[t % 8]], w=[g_r[t % 8]])
                wd = WDr[t % NW]
                if WD_ON_ACT:
                    C.op("act", lambda e: e.activation(out=gel[:, t:t + 1], in_=gel[:, t:t + 1], func=AF.Copy, scale=gtT[:, t:t + 1]), r=[g_r[t % 8], gtT], w=[g_r[t % 8]])
                    C.op("act", lambda e: e.activation(out=wd[:, 0:nt], in_=Zc[:, 127 - t:127 - t + nt], func=AF.Copy, scale=gel[:, t:t + 1]), r=[Zc, g_r[t % 8]], w=[wd])
                else:
                    C.op("dve", lambda e: e.tensor_scalar(out=wd[:, 0:nt], in0=Zc[:, 127 - t:127 - t + nt], scalar1=gel[:, t:t + 1], scalar2=gtT[:, t:t + 1],
                                                          op0=ALU.mult, op1=ALU.mult), r=[Zc, g_r[t % 8], gtT], w=[wd])
            tcur[0] = t
            refill()
            if DEBUG and gi == 2 and t % 8 == 0:
                print("      tok", t, {k_: round(v_, 1) for k_, v_ in C.eng_time.items()}, "front", round(C.front_time, 1), "dma_free", round(C.dma_free, 1), "F done" if (stepper is None or stepper.finished) else "F next " + str(stepper.next_e))
            tv = t - LAG
            if 0 <= tv < nt:
                uv = UV[tv % NS]; wd = WDr[tv % NW]
                for hf in range(2):
                    C.op("pe", lambda e: e.matmul(P[3].f(0, nt, hf * 512, [[1, 512]]), wd[:, 0:nt], uv[:, 1024 + hf * 512:1024 + (hf + 1) * 512],
                                                  start=(tv == 0), stop=(tv == nt - 1)), r=[wd, uv], w=[P[3].r(hf)])
        C.op("dve", lambda e: e.tensor_tensor(out=xt[0:nt, :], in0=hb[0:nt, :], in1=P[3].f(0, nt, 0, [[1, 1024]]), op=ALU.add), r=[hb] + P[3].both, w=[xt])
        C.dma("sp", T["y"], xt[0:nt, :], r=[xt], sbuf_side=xt, is_store=True, final=True)
        C.pump = None

    NF = [0]
    n_before = C.ninst
    front(0)
    NF[0] = C.ninst - n_before
    for gi in range(len(tiles)):
        stp = None
        if gi + 1 < len(tiles):
            stp = Stepper(C, lambda: front(gi + 1))
            if MARKMODE:
                stp.run_to_mark()
        n0 = C.ninst
        back(gi, stp, KSTEP)
        n1 = C.ninst
        if stp is not None:
            stp.drain()
        if DEBUG:
            print("   F(gi+1) ended at model t=%.1f ; F started at %.1f" % (DBG.get("fend", 0), DBG.get("fstart", 0)))
            print("tile", gi, "model time", {k_: round(v_, 1) for k_, v_ in C.eng_time.items()}, "instr in back(incl F)", n1 - n0, "drained", C.ninst - n1)
    C.finish()
    return nc


KSTEP = 7
MARKMODE = 0
NSLOT, NDPF, NLAG = 8, 4, 3
WD_ON_ACT = 1
FIN_FRAC = 0.8
BOOST = 2.0
DEBUG = 0
DBG = {}
SLACK = 0.15
BUDGET = {"dve": 0.7, "pe": 0.6, "act": 1.0, "pool": 0.5, "sp": 1.0}
COST = {"dve": 0.33, "pe": 0.15, "act": 0.4, "pool": 1.0}
SKIPRAW = 0
_CACHE = {}


def kernel(x_prompt, x_sample, cache_attn_k, cache_attn_v, state_conv, state_lru,
           ln1_g, w_in, conv_w, conv_b, lru_wa, lru_ba, lru_wi, lru_bi, lru_lambda,
           q_norm_g, k_norm_g, attn_sinks, g_lru_out, g_attn_out, w_out, ln2_g,
           peer_w_query, peer_sub_keys1, peer_sub_keys2, peer_u, peer_v):
    f = lambda a: np.ascontiguousarray(np.asarray(a, dtype=np.float32))
    x_prompt = f(x_prompt); x_sample = f(x_sample)
    if "nc" not in _CACHE:
        _CACHE["nc"] = build_program()
    nc = _CACHE["nc"]
    shared = dict(
        ln1=f(ln1_g).reshape(1, 1024), ln2=f(ln2_g).reshape(1, 1024),
        qg=f(q_norm_g).reshape(1, 64), kg=f(k_norm_g).reshape(1, 64), snk=f(attn_sinks).reshape(1, 8),
        w_in=f(w_in).reshape(1024, 1792), w_out=f(w_out).reshape(1024, 1024), w_q=f(peer_w_query).reshape(1024, 2048),
        wa=f(lru_wa).reshape(8, 64, 64), wi=f(lru_wi).reshape(8, 64, 64),
        sk1=f(peer_sub_keys1).reshape(128, 128), sk2=f(peer_sub_keys2).reshape(128, 128),
        pu=f(peer_u).reshape(16384, 1024), pv=f(peer_v).reshape(16384, 1024),
    )
    common_rows = [f(conv_w).reshape(4, 512), f(conv_b).reshape(1, 512), f(lru_ba).reshape(1, 512), f(lru_bi).reshape(1, 512),
                   f(lru_lambda).reshape(1, 512), f(g_lru_out).reshape(1, 512)]
    ga_row = f(g_attn_out).reshape(1, 512)
    sc = f(state_conv).reshape(16, 3, 512); sl = f(state_lru).reshape(16, 512)
    ckf = f(cache_attn_k).reshape(16, 128, 128); cvf = f(cache_attn_v).reshape(16, 128, 128)
    in_maps = []
    for c in range(8):
        s0, s1 = 2 * c, 2 * c + 1
        v512 = np.concatenate(common_rows + [sc[s0], sc[s1], sl[s0:s0 + 1], sl[s1:s1 + 1], ga_row], axis=0)
        m = dict(shared)
        m.update(xp=x_prompt[c], xs=np.ascontiguousarray(x_sample[s0:s1 + 1].reshape(32, 1024)),
                 ck=np.ascontiguousarray(ckf[s0:s1 + 1]), cv=np.ascontiguousarray(cvf[s0:s1 + 1]), v512=np.ascontiguousarray(v512))
        in_maps.append(m)
    res = run_bass_kernel_spmd(nc, in_maps, core_ids=list(range(8)))
    R = res.results
    y_p = np.stack([R[c]["yp"] for c in range(8)], 0).reshape(8, 2048, 1024)
    y_s = np.concatenate([R[c]["ys"] for c in range(8)], 0).reshape(16, 16, 1024)
    nk_p = np.stack([R[c]["nkp"] for c in range(8)], 0).reshape(1, 8, 128, 2, 64)
    nv_p = np.stack([R[c]["nvp"] for c in range(8)], 0).reshape(1, 8, 128, 2, 64)
    nc_p = np.stack([R[c]["ncp"] for c in range(8)], 0).reshape(1, 8, 3, 512)
    nh_p = np.stack([R[c]["nhp"] for c in range(8)], 0).reshape(1, 8, 512)
    nk_s = np.concatenate([R[c]["nks"] for c in range(8)], 0).reshape(1, 16, 128, 2, 64)
    nv_s = np.concatenate([R[c]["nvs"] for c in range(8)], 0).reshape(1, 16, 128, 2, 64)
    nc_s = np.concatenate([R[c]["ncs"] for c in range(8)], 0).reshape(1, 16, 3, 512)
    nh_s = np.concatenate([R[c]["nhs"] for c in range(8)], 0).reshape(1, 16, 512)
    return (y_p, y_s, nk_p, nv_p, nc_p, nh_p, nk_s, nv_s, nc_s, nh_s)
```

```python
import numpy as np
import concourse.bass as bass
import concourse.mybir as mybir
from concourse.bass_utils import run_bass_kernel_spmd

F32 = mybir.dt.float32
BF16 = mybir.dt.bfloat16
U32 = mybir.dt.uint32
I32 = mybir.dt.int32
AF = mybir.ActivationFunctionType
ALU = mybir.AluOpType
AX = mybir.AxisListType

SEM_LIMIT = 30000
SKIP_SAME_RAW = ()
STRICT = 1


class Res:
    __slots__ = ("name", "last_w", "reads", "dsem_w", "dsem_r", "dcnt_w", "dcnt_r")

    def __init__(self, name):
        self.name = name
        self.last_w = None
        self.reads = []
        self.dsem_w = None
        self.dsem_r = None
        self.dcnt_w = 0
        self.dcnt_r = 0


class Buf:
    def __init__(self, C, name, shape, dtype, psum=False, stack=None):
        self.name = name
        if psum:
            self.t = C.nc.alloc_psum_tensor(name, list(shape), dtype)
        elif stack is not None:
            self.t = stack.enter_context(C.nc.sbuf_tensor(name, list(shape), dtype))
        else:
            self.t = C.nc.alloc_sbuf_tensor(name, list(shape), dtype)
        self.res = Res(name)
        if stack is not None:
            stack.callback(C.release_res, self.res)
        self.shape = list(shape)
        self.dtype = dtype

    def __getitem__(self, k):
        return self.t[k]

    def ap(self, offset, dims):
        fs = 1
        for s in self.shape[1:]:
            fs *= s
        return bass.AP(self.t, offset, [[fs, self.shape[0]]] + [list(d) for d in dims])

    def pap(self, p0, pn, offset, dims):
        fs = 1
        for s in self.shape[1:]:
            fs *= s
        return bass.AP(self.t, p0 * fs + offset, [[fs, pn]] + [list(d) for d in dims])


def _ap_n(ap):
    try:
        sh = list(ap.shape)
        n = 1
        for s_ in sh[1:]:
            n *= int(s_)
        return int(sh[0]), n
    except Exception:
        return 128, 128


_DT_SIZE = {}


def _dsize(ap):
    try:
        d = ap.dtype
        if d == BF16:
            return 2
        return 4
    except Exception:
        return 4


class EngProxy:
    def __init__(self, eng):
        self._eng = eng
        self.name = None
        self.n = 128
        self.passes = 1
        self.nbytes = 0

    def __getattr__(self, name):
        real = getattr(self._eng, name)

        def call(*a, **kw):
            self.name = name
            out = kw.get("out", a[0] if a else None)
            if name in ("matmul", "transpose"):
                rhs = kw.get("rhs", a[2] if len(a) > 2 else None)
                if name == "transpose":
                    src = kw.get("in_", a[1] if len(a) > 1 else None)
                    p_, n_ = _ap_n(src)
                    self.n = p_
                else:
                    p_, n_ = _ap_n(rhs)
                    self.n = n_
                    self.passes = 4 if _dsize(rhs) == 4 else 1
            elif name in ("max", "max_index", "match_replace"):
                src = kw.get("in_", kw.get("in_values", None))
                p_, n_ = _ap_n(src)
                self.n = n_
            elif name == "tensor_reduce":
                p_, n_ = _ap_n(kw.get("in_"))
                self.n = n_
            elif out is not None:
                p_, n_ = _ap_n(out)
                self.n = n_
                self.nbytes = p_ * n_ * _dsize(out)
            return real(*a, **kw)

        return call


def op_cost(e, name, n, passes):
    if e == "pe":
        return 0.11 + n * passes * 0.00032
    if e == "dve":
        return 0.07 + n * 0.00105
    if e == "act":
        return 0.2 + n * 0.00088
    if e == "pool":
        return 0.3 + n * 0.0023
    return 0.1


class Ctx:
    def __init__(self, nc):
        self.nc = nc
        self.eng = {"pe": nc.tensor, "act": nc.scalar, "dve": nc.vector,
                    "pool": nc.gpsimd, "sp": nc.sync}
        self.esem = {}
        self.ecnt = {}
        self.waited = {e: {} for e in self.eng}
        self.semn = 0
        self.ninst = 0
        self.final_tokens = []
        self.sem_pool = []
        self.pending = []
        self.hook_thread = None
        self.pump = None
        self.in_pump = False
        self.eng_time = {e: 0.0 for e in self.eng}
        self.tok_time = {}
        self.dma_free = 0.0
        self.front_time = 0.0
        self.log = None
        self.ninst_f = 0

    def release_res(self, res):
        if res.dsem_w is not None and res.dcnt_w < SEM_LIMIT:
            self.sem_pool.append((res.dsem_w, res.dcnt_w))
        if res.dsem_r is not None and res.dcnt_r < SEM_LIMIT:
            self.sem_pool.append((res.dsem_r, res.dcnt_r))
        res.dsem_w = res.dsem_r = None

    def get_dsem(self, name):
        if self.sem_pool:
            return self.sem_pool.pop()
        return (self.newsem(name), 0)

    def barrier(self):
        engs = list(self.eng)
        for e in engs:
            for e2 in list(self.esem):
                if e2 != e:
                    self._wait(e, (self.esem[e2], self.ecnt[e2], e2))
            d = {}
            for t in self.pending:
                k = id(t[0])
                if k not in d or d[k][1] < t[1]:
                    d[k] = t
            for t in d.values():
                self._wait(e, t)
        self.pending = []

    def newsem(self, name):
        self.semn += 1
        return self.nc.alloc_semaphore(name=f"{name}_{self.semn}")

    def buf(self, name, shape, dtype, psum=False):
        return Buf(self, name, shape, dtype, psum)

    def _next_tok(self, e):
        if e not in self.esem or self.ecnt[e] >= SEM_LIMIT:
            self.esem[e] = self.newsem("e" + e)
            self.ecnt[e] = 0
        self.ecnt[e] += 1
        return (self.esem[e], self.ecnt[e], e)

    def _wait(self, e, tok):
        sem, val, src = tok
        key = id(sem)
        w = self.waited[e]
        if w.get(key, (None, 0))[1] >= val:
            return False
        self.eng[e].wait_ge(sem, val)
        w[key] = (sem, val)
        return True

    def _dep_tokens(self, e, r, w, same_engine_raw=True):
        toks = []
        for b in r:
            res = b.res if hasattr(b, 'res') else b
            if res.last_w is not None:
                t = res.last_w
                if t[2] == e and (e == "pe" or not same_engine_raw or e in SKIP_SAME_RAW):
                    continue
                toks.append(t)
        strict = STRICT and e != "pe"
        for b in w:
            res = b.res if hasattr(b, 'res') else b
            if res.last_w is not None:
                t = res.last_w
                if strict or not (t[2] == e):
                    toks.append(t)
            for t in res.reads:
                if t[2] == e and not strict:
                    continue
                toks.append(t)
        return toks

    def pred_ready(self, e, r, w):
        t_ = 0.0
        for tk in self._dep_tokens(e, r, w):
            t_ = max(t_, self.tok_time.get((id(tk[0]), tk[1]), 0.0))
        return t_

    def _deps(self, e, r, w, same_engine_raw=True):
        toks = self._dep_tokens(e, r, w, same_engine_raw)
        t_ = 0.0
        nw = 0
        for t in toks:
            t_ = max(t_, self.tok_time.get((id(t[0]), t[1]), 0.0))
            if self._wait(e, t):
                nw += 1
        return t_, nw

    def _commit(self, tok, r, w):
        for b in w:
            res = b.res if hasattr(b, 'res') else b
            res.last_w = tok
            res.reads = []
        for b in r:
            res = b.res if hasattr(b, 'res') else b
            if res.last_w is tok:
                continue
            res.reads.append(tok)
            if len(res.reads) > 64:
                d = {}
                for t in res.reads:
                    k = id(t[0])
                    if k not in d or d[k][1] < t[1]:
                        d[k] = t
                res.reads = list(d.values())

    def _hook(self, e=None, r=(), w=()):
        ht = self.hook_thread
        import threading as _th
        if ht is not None and _th.current_thread() is ht[0]:
            ht[1](e, r, w)
        elif self.pump is not None and not self.in_pump:
            self.in_pump = True
            try:
                self.pump()
            finally:
                self.in_pump = False

    def mark(self, name="MARK"):
        self._hook(name)

    def _is_main(self):
        ht = self.hook_thread
        if ht is None:
            return True
        import threading as _th
        return _th.current_thread() is not ht[0]

    def op(self, e, fn, r=(), w=()):
        self._hook(e, r, w)
        t_ready, nw = self._deps(e, r, w)
        tok = self._next_tok(e)
        px = EngProxy(self.eng[e])
        ins = fn(px)
        ins.then_inc(tok[0], 1)
        self._commit(tok, r, w)
        self.ninst += 1
        start = max(self.eng_time[e], t_ready) + (0.08 if nw else 0.0)
        end = start + op_cost(e, px.name, px.n, px.passes)
        self.eng_time[e] = end
        self.tok_time[(id(tok[0]), tok[1])] = end + 0.06
        if self.log is not None:
            self.log.append(("M" if self._is_main() else "F", e, px.name, px.n, round(t_ready, 2), round(start, 2), round(end, 2)))
        if self._is_main():
            self.front_time = max(self.front_time, start)
        else:
            self.ninst_f += 1
        return tok

    def dma(self, q, out, in_, r=(), w=(), sbuf_side=None, is_store=False, fn=None, final=False, temp_store=False):
        self._hook(q, r, w)
        t_ready, nw = self._deps(q, r, w, same_engine_raw=True)
        res = sbuf_side.res if hasattr(sbuf_side, 'res') else sbuf_side
        if is_store:
            if res.dsem_r is None:
                res.dsem_r, res.dcnt_r = self.get_dsem("dr")
            res.dcnt_r += 16
            tok = (res.dsem_r, res.dcnt_r, "dma")
        else:
            if res.dsem_w is None:
                if q == "pool":
                    res.dsem_w, res.dcnt_w = self.newsem("dg"), 0
                else:
                    res.dsem_w, res.dcnt_w = self.get_dsem("dw")
            res.dcnt_w += 16
            tok = (res.dsem_w, res.dcnt_w, "dma")
        px = EngProxy(self.eng[q])
        if fn is None:
            ins = px.dma_start(out=out, in_=in_)
        else:
            ins = fn(px)
        ins.then_inc(tok[0], 16)
        self._commit(tok, r, w)
        self.ninst += 1
        if final:
            self.final_tokens.append(tok)
        if temp_store:
            self.pending.append(tok)
        start = max(self.eng_time[q], t_ready) + (0.08 if nw else 0.0)
        issue = 1.1 if px.name == "indirect_dma_start" else 0.1
        self.eng_time[q] = start + issue
        xfer = px.nbytes / 3.0e5
        s2 = max(start + issue, self.dma_free)
        self.dma_free = s2 + xfer
        self.tok_time[(id(tok[0]), tok[1])] = s2 + xfer + 2.0
        if self._is_main():
            self.front_time = max(self.front_time, start)
        return tok

    def finish(self):
        for t in self.final_tokens:
            self._wait("sp", t)
        for e in self.esem:
            self._wait("sp", (self.esem[e], self.ecnt[e], e))


import contextlib
import threading

EPS = 1e-6
NEG = -1.0e30
R_CW, R_CB, R_BA, R_BI, R_LAM, R_GL, R_SC, R_SL, R_GA, NROW = 0, 4, 5, 6, 7, 8, 9, 15, 17, 18


class PS:
    def __init__(self, C, name):
        self.b = C.buf(name, [128, 1024], F32, psum=True)
        self.t = self.b.t
        self.ra = Res(name + "a")
        self.rb = Res(name + "b")
        self.bt = self.t[:, :].bitcast(BF16)

    def f(self, p0, pn, off, dims):
        return bass.AP(self.t, p0 * 1024 + off, [[1024, pn]] + [list(d) for d in dims])

    def bf(self, p0, pn, off, dims):
        return bass.AP(self.bt.tensor, self.bt.offset + p0 * 2048 + off, [[2048, pn]] + [list(d) for d in dims])

    def r(self, half):
        return self.ra if half == 0 else self.rb

    @property
    def both(self):
        return [self.ra, self.rb]


class VBuf:
    def __init__(self, arena, f32_off, shape, dtype, name):
        esz = 2 if dtype == BF16 else 4
        n = 1
        for s_ in shape[1:]:
            n *= s_
        nf32 = (n * esz + 3) // 4
        self.nf32 = nf32
        base = arena.t[:, f32_off:f32_off + nf32]
        if dtype != F32:
            base = base.bitcast(dtype)
        self.base = base
        self.pstep = base.ap[0][0]
        self.off0 = base.offset
        self.tensor = base.tensor
        self.shape = list(shape)
        self.n = n
        self.res = Res(name)
        self.name = name
        if len(shape) == 2:
            self.v = base[:, 0:n] if n != base.shape[1] else base
        elif len(shape) == 3:
            self.v = base[:, 0:n].rearrange("p (a b) -> p a b", a=shape[1], b=shape[2])
        else:
            raise ValueError

    def __getitem__(self, k):
        return self.v[k]

    def pap(self, p0, pn, off, dims):
        return bass.AP(self.tensor, self.off0 + p0 * self.pstep + off, [[self.pstep, pn]] + [list(d) for d in dims])

    def ap(self, off, dims):
        return self.pap(0, self.shape[0], off, dims)


class Stepper:
    def __init__(self, C, fn):
        self.C = C
        self.go = threading.Semaphore(0)
        self.back = threading.Semaphore(0)
        self.finished = False
        self.err = None
        self.next_e = None
        self.at_mark = False

        def run():
            self.go.acquire()
            try:
                fn()
            except BaseException as ex:
                self.err = ex
            self.finished = True
            self.C.hook_thread = None
            self.back.release()

        self.th = threading.Thread(target=run)
        self.th.start()

    def hook(self, e=None, r=(), w=()):
        self.next_e = e
        self.next_rw = (r, w)
        if e == "MARK":
            self.at_mark = True
        self.back.release()
        self.go.acquire()

    def step(self, n=1):
        for _ in range(n):
            if self.finished:
                break
            self.C.hook_thread = (self.th, self.hook)
            self.go.release()
            self.back.acquire()
            self.C.hook_thread = None
        if self.err is not None:
            raise self.err

    def run_to_mark(self):
        while not self.finished and not self.at_mark:
            self.step(1)

    def drain(self):
        while not self.finished:
            self.step(64)
        self.th.join()
        if self.err is not None:
            raise self.err


def build_program():
    global SKIP_SAME_RAW
    SKIP_SAME_RAW = {0: (), 1: ("dve",), 2: ("dve", "act"), 3: ("dve", "act", "pool")}[SKIPRAW]
    nc = bass.Bass("TRN2", target_bir_lowering=False)
    C = Ctx(nc)
    if DEBUG:
        C.log = []
        DBG["C"] = C
    cnt = [0]

    def DI(name, shape, dt=F32):
        return nc.dram_tensor(name, list(shape), dt, kind="ExternalInput")

    def DO(name, shape, dt=F32):
        return nc.dram_tensor(name, list(shape), dt, kind="ExternalOutput")

    xp = DI("xp", [2048, 1024]); xs = DI("xs", [32, 1024])
    ck = DI("ck", [2, 128, 128]); cv = DI("cv", [2, 128, 128])
    v512 = DI("v512", [NROW, 512])
    ln1 = DI("ln1", [1, 1024]); ln2 = DI("ln2", [1, 1024])
    qg = DI("qg", [1, 64]); kg = DI("kg", [1, 64]); snk = DI("snk", [1, 8])
    w_in = DI("w_in", [1024, 1792]); w_out = DI("w_out", [1024, 1024]); w_q = DI("w_q", [1024, 2048])
    wa = DI("wa", [8, 64, 64]); wi = DI("wi", [8, 64, 64])
    sk1 = DI("sk1", [128, 128]); sk2 = DI("sk2", [128, 128])
    pu = DI("pu", [16384, 1024]); pv = DI("pv", [16384, 1024])
    yp = DO("yp", [2048, 1024]); ys = DO("ys", [32, 1024])
    nkp = DO("nkp", [128, 128]); nvp = DO("nvp", [128, 128]); ncp = DO("ncp", [3, 512]); nhp = DO("nhp", [1, 512])
    nks = DO("nks", [2, 128, 128]); nvs = DO("nvs", [2, 128, 128]); ncs = DO("ncs", [2, 3, 512]); nhs = DO("nhs", [2, 512])
    tab = nc.dram_tensor("tab", [16384, 2048], BF16, kind="Internal")

    def dap(t, off, dims):
        return bass.AP(t, off, [list(d) for d in dims])

    def B(name, shape, dt, stack=None):
        cnt[0] += 1
        return Buf(C, f"{name}_{cnt[0]}", shape, dt, stack=stack)

    def barrier():
        C.barrier()

    P = [PS(C, f"P{i}") for i in range(4)]
    identf = B("identf", [128, 128], F32); identb = B("identb", [128, 128], BF16); ones_f = B("ones_f", [128, 128], F32)
    wi_bf = B("wi_bf", [128, 8, 1792], BF16); wo_bf = B("wo_bf", [128, 8, 1024], BF16); wq_bf = B("wq_bf", [128, 8, 2048], BF16)
    wi_r = [Res(f"wi{k}") for k in range(8)]; wo_r = [Res(f"wo{k}") for k in range(8)]; wq_r = [Res(f"wq{k}") for k in range(8)]
    wa_bd = B("wa_bd", [128, 4, 128], BF16); wi_bd = B("wi_bd", [128, 4, 128], BF16)
    skT = [B("skT0", [128, 128], BF16), B("skT1", [128, 128], BF16)]
    vecT = B("vecT", [128, 4, NROW], F32); gT8 = B("gT8", [128, 2, 8], F32)
    clam = B("clam", [128, 4], F32); nclam = B("nclam", [128, 4], F32)
    qgrep = B("qgrep", [128, 64], F32); kgrep = B("kgrep", [128, 64], F32); esink = B("esink", [128, 8], F32)
    uext = B("uext", [128, 4, 131], F32); hstate = B("hstate", [128, 4], F32)
    kTb = [B("kT0", [64, 2, 128], BF16), B("kT1", [64, 2, 128], BF16)]
    Vaug = [B("Va0", [128, 2, 65], BF16), B("Va1", [128, 2, 65], BF16)]
    PT = {(g, nm): B(f"PT{g}{nm}", [128, 4, 128], BF16) for g in range(2) for nm in ("prev", "own")}
    Zc = B("Zc", [128, 256], BF16)
    iota16 = B("iota16", [128, 16], F32)

    C.op("pool", lambda e: e.memset(ones_f[:], 1.0), w=[ones_f])
    C.op("pool", lambda e: e.affine_select(out=identf[:], in_=ones_f[:], pattern=[[-1, 128]], compare_op=ALU.is_equal,
                                           fill=0.0, base=0, channel_multiplier=1), r=[ones_f], w=[identf])
    C.op("dve", lambda e: e.tensor_copy(out=identb[:], in_=identf[:]), r=[identf], w=[identb])
    for b_ in Vaug:
        C.op("pool", lambda e: e.memset(b_[:], 1.0), w=[b_])
    for b_ in PT.values():
        C.op("pool", lambda e: e.memset(b_[:], 0.0), w=[b_])
    C.op("pool", lambda e: e.memset(uext[:], 0.0), w=[uext])
    C.op("pool", lambda e: e.memset(hstate[:], 0.0), w=[hstate])
    C.op("pool", lambda e: e.memset(Zc[:], 0.0), w=[Zc])
    C.op("pool", lambda e: e.memset(Zc[:, 127:128], 1.0), w=[Zc])
    C.op("pool", lambda e: e.iota(iota16[:], pattern=[[1, 16]], base=0, channel_multiplier=0, allow_small_or_imprecise_dtypes=True), w=[iota16])

    with contextlib.ExitStack() as st:
        v_sb = B("v_sb", [32, 512], F32, st)
        C.dma("sp", v_sb[0:NROW, :], v512.ap(), w=[v_sb], sbuf_side=v_sb)
        for ct in range(4):
            C.op("pe", lambda e: e.transpose(P[0].f(0, 128, 512 + ct * 32, [[1, NROW]]), v_sb[0:NROW, ct * 128:(ct + 1) * 128], identf[0:NROW, 0:NROW]),
                 r=[v_sb, identf], w=[P[0].rb])
        C.op("act", lambda e: e.activation(out=vecT[:], in_=P[0].f(0, 128, 512, [[32, 4], [1, NROW]]), func=AF.Copy), r=[P[0].rb], w=[vecT])
        g_sb = B("g_sb", [16, 128], F32, st)
        C.dma("sp", g_sb[0:8, :], dap(ln1, 0, [[128, 8], [1, 128]]), w=[g_sb], sbuf_side=g_sb)
        C.dma("sp", g_sb[8:16, :], dap(ln2, 0, [[128, 8], [1, 128]]), w=[g_sb], sbuf_side=g_sb)
        C.op("pe", lambda e: e.transpose(P[0].f(0, 128, 0, [[1, 16]]), g_sb[0:16, :], identf[0:16, 0:16]), r=[g_sb, identf], w=[P[0].ra])
        C.op("act", lambda e: e.activation(out=gT8[:], in_=P[0].f(0, 128, 0, [[8, 2], [1, 8]]), func=AF.Copy), r=[P[0].ra], w=[gT8])
        stg = [B("stg0", [128, 2048], F32, st), B("stg1", [128, 2048], F32, st), B("stg2", [128, 2048], F32, st)]
        jobs = []
        for kc in range(8):
            jobs.append((dap(w_in, kc * 128 * 1792, [[1792, 128], [1, 1792]]), wi_bf[:, kc, :], 1792, wi_r[kc], gT8[:, 0, kc:kc + 1], [gT8]))
        for kc in range(8):
            sc_ = vecT[:, kc, R_GL:R_GL + 1] if kc < 4 else vecT[:, kc - 4, R_GA:R_GA + 1]
            jobs.append((dap(w_out, kc * 128 * 1024, [[1024, 128], [1, 1024]]), wo_bf[:, kc, :], 1024, wo_r[kc], sc_, [vecT]))
        for kc in range(8):
            jobs.append((dap(w_q, kc * 128 * 2048, [[2048, 128], [1, 2048]]), wq_bf[:, kc, :], 2048, wq_r[kc], gT8[:, 1, kc:kc + 1], [gT8]))
        for j, (src, dst, n, rr, scl, sr) in enumerate(jobs):
            s_ = stg[j % 3]
            C.dma("sp", s_[:, 0:n], src, w=[s_], sbuf_side=s_)
            en = ("act", "dve", "pool")[j % 3]
            if en == "act":
                C.op("act", lambda e: e.activation(out=dst, in_=s_[:, 0:n], func=AF.Copy, scale=scl), r=[s_] + sr, w=[rr])
            else:
                C.op(en, lambda e: e.tensor_scalar(out=dst, in0=s_[:, 0:n], scalar1=scl, scalar2=None, op0=ALU.mult), r=[s_] + sr, w=[rr])
        for (src_t, dstb) in ((wa, wa_bd), (wi, wi_bd)):
            sw = B("stgw", [128, 4, 128], F32, st)
            C.op("pool", lambda e: e.memset(sw[:], 0.0), w=[sw])
            C.dma("sp", sw.pap(0, 64, 0, [[128, 4], [1, 64]]), dap(src_t, 0, [[64, 64], [8192, 4], [1, 64]]), w=[sw], sbuf_side=sw)
            C.dma("sp", sw.pap(64, 64, 64, [[128, 4], [1, 64]]), dap(src_t, 4096, [[64, 64], [8192, 4], [1, 64]]), w=[sw], sbuf_side=sw)
            C.op("dve", lambda e: e.tensor_copy(out=dstb[:], in_=sw[:]), r=[sw], w=[dstb])
        for i_, src_t in enumerate((sk1, sk2)):
            sk_sb = B("sk_sb", [128, 128], F32, st)
            C.dma("sp", sk_sb[:], src_t.ap(), w=[sk_sb], sbuf_side=sk_sb)
            C.op("pe", lambda e: e.transpose(P[0].f(0, 128, i_ * 128, [[1, 128]]), sk_sb[:], identf[:]), r=[sk_sb, identf], w=[P[0].ra])
            C.op("act", lambda e: e.activation(out=skT[i_][:], in_=P[0].f(0, 128, i_ * 128, [[1, 128]]), func=AF.Copy), r=[P[0].ra], w=[skT[i_]])
        e1 = B("e1", [128, 4], F32, st)
        C.op("act", lambda e: e.activation(out=e1[:], in_=vecT[:, :, R_LAM], func=AF.Exp, scale=-1.0), r=[vecT], w=[e1])
        C.op("act", lambda e: e.activation(out=e1[:], in_=e1[:], func=AF.Ln, bias=1.0), r=[e1], w=[e1])
        C.op("dve", lambda e: e.tensor_scalar(out=clam[:], in0=e1[:], scalar1=-8.0, scalar2=None, op0=ALU.mult), r=[e1], w=[clam])
        C.op("dve", lambda e: e.tensor_scalar(out=nclam[:], in0=e1[:], scalar1=8.0, scalar2=None, op0=ALU.mult), r=[e1], w=[nclam])
        for (dt_, buf_, n_) in ((qg, qgrep, 64), (kg, kgrep, 64), (snk, esink, 8)):
            C.dma("sp", buf_[:], dap(dt_, 0, [[0, 128], [1, n_]]), w=[buf_], sbuf_side=buf_)
        C.op("act", lambda e: e.activation(out=esink[:], in_=esink[:], func=AF.Exp), r=[esink], w=[esink])
        barrier()
    with contextlib.ExitStack() as st:
        NSL = 4
        g2rep = B("g2rep", [128, 1024], F32, st)
        C.dma("sp", g2rep[:], dap(ln2, 0, [[0, 128], [1, 1024]]), w=[g2rep], sbuf_side=g2rep)
        su = [B("su", [128, 1024], F32, st) for _ in range(NSL)]
        sv = [B("sv", [128, 1024], F32, st) for _ in range(NSL)]
        tb = [B("tb", [128, 2048], BF16, st) for _ in range(NSL)]
        tbu = [Res("tbu") for _ in range(NSL)]; tbv = [Res("tbv") for _ in range(NSL)]
        for c in range(128 + 2):
            if c < 128:
                s_ = c % NSL
                C.dma("sp", su[s_][:], dap(pu, c * 128 * 1024, [[1024, 128], [1, 1024]]), w=[su[s_]], sbuf_side=su[s_])
                C.dma("sp", sv[s_][:], dap(pv, c * 128 * 1024, [[1024, 128], [1, 1024]]), w=[sv[s_]], sbuf_side=sv[s_])
                C.op("act", lambda e: e.activation(out=tb[s_][:, 1024:2048], in_=sv[s_][:], func=AF.Copy), r=[sv[s_]], w=[tbv[s_]])
                en = "dve" if c % 2 == 0 else "pool"
                C.op(en, lambda e: e.tensor_tensor(out=tb[s_][:, 0:1024], in0=su[s_][:], in1=g2rep[:], op=ALU.mult), r=[su[s_], g2rep], w=[tbu[s_]])
            cs = c - 2
            if cs >= 0:
                s2 = cs % NSL
                C.dma("sp", dap(tab, cs * 128 * 2048, [[2048, 128], [1, 2048]]), tb[s2][:], r=[tbu[s2], tbv[s2]], sbuf_side=tb[s2], is_store=True, temp_store=True)
        barrier()

    xtb = [B("xt0", [128, 1024], F32), B("xt1", [128, 1024], F32)]
    h_sb = [B("h_sb0", [128, 1024], F32), B("h_sb1", [128, 1024], F32)]
    xn2_bf = [B("xn2_0", [128, 1024], BF16), B("xn2_1", [128, 1024], BF16)]
    idxT = [B("idxT0", [128, 128], U32), B("idxT1", [128, 128], U32)]
    gT = [B("gT0", [128, 128], F32), B("gT1", [128, 128], F32)]
    NS, DPF, LAG, NW = NSLOT, NDPF, NLAG, NLAG + 3
    assert NS >= DPF + LAG + 1
    UV = [B(f"UV{i}", [128, 2048], BF16) for i in range(NS)]
    WDr = [B(f"WD{i}", [128, 128], BF16) for i in range(NW)]
    apre = B("apre", [128, 128], F32); gel = B("gel", [128, 128], F32)
    a_r = [Res(f"ap{i}") for i in range(8)]; g_r = [Res(f"gl{i}") for i in range(8)]

    ARENA = 11008
    arena = B("arena", [128, ARENA], F32)
    A_res, B_res = [], []

    class Alloc:
        def __init__(self, lst):
            self.off = 0
            self.lst = lst

        def __call__(self, name, shape, dt):
            v = VBuf(arena, self.off, shape, dt, name)
            self.off += v.nf32
            assert self.off <= ARENA, (name, self.off)
            self.lst.append(v.res)
            return v

        def res(self, name):
            r_ = Res(name)
            self.lst.append(r_)
            return r_

    VA = Alloc(A_res); VB = Alloc(B_res)
    ss1 = VA("ss1", [128, 1], F32); rs1 = VA("rs1", [128, 1], F32)
    xn_bf = VA("xn_bf", [128, 1024], BF16); xnT = VA("xnT", [128, 8, 128], BF16)
    gate = VA("gate", [128, 4, 128], F32); xc = VA("xc", [128, 4, 128], F32); xc_bf = VA("xc_bf", [128, 4, 128], BF16)
    rr = VA("rr", [128, 4, 128], F32); ig = VA("ig", [128, 4, 128], F32); aa = VA("aa", [128, 4, 128], F32)
    t1 = VA("t1", [128, 4, 128], F32); t2 = VA("t2", [128, 4, 128], F32); hh = VA("hh", [128, 4, 128], F32)
    lo = VA("lo", [128, 4, 128], F32); ril = VA("ril", [128, 128], F32)
    catT = VA("catT", [128, 8, 128], BF16); cat_l = VA.res("cat_l"); cat_a = VA.res("cat_a")
    qkv = VA("qkv", [128, 768], F32); sqq = VA("sqq", [128, 640], F32); tmpq = VA("tmpq", [128, 640], F32)
    ssq = VA("ssq", [128, 10], F32); riq = VA("riq", [128, 10], F32)
    qn_bf = VA("qn_bf", [128, 512], BF16); kn = VA("kn", [128, 128], F32); kn_bf = VA("kn_bf", [128, 128], BF16)
    qT = VA("qT", [64, 8, 128], BF16)
    den = VA("den", [128, 8], F32); o_sb = VA("o_sb", [128, 512], F32)
    ssa = VA("ssa", [128, 1], F32); rsa = VA("rsa", [128, 1], F32); an_bf = VA("an_bf", [128, 512], BF16)
    ck_sb = VA("ck_sb", [128, 128], F32); cv_sb = VA("cv_sb", [128, 128], F32); ck_bf = VA("ck_bf", [128, 128], BF16)
    ss2 = VB("ss2", [128, 1], F32)
    xn2T = VB("xn2T", [128, 8, 128], BF16); pq_bf = VB("pq_bf", [128, 16, 128], BF16)
    S = VB("S", [128, 16, 128], F32); v16 = VB("v16", [128, 16, 16], F32); i16 = VB("i16", [128, 16, 16], U32)
    i16f = VB("i16f", [128, 16, 16], F32); big = VB("big", [128, 2048], F32)
    tmp = [VB("tmpa", [128, 128], F32), VB("tmpb", [128, 128], F32)]
    tmp2 = [VB("tmp2a", [128, 256], F32), VB("tmp2b", [128, 256], F32)]
    ts = VB("ts", [128, 8, 16], F32); tp = VB("tp", [128, 8, 16], U32)
    pA = VB("pA", [128, 128], U32); pB = VB("pB", [128, 128], U32); pAf = VB("pAf", [128, 128], F32); pBf = VB("pBf", [128, 128], F32)
    i1s = VB("i1s", [128, 128], F32); i2s = VB("i2s", [128, 128], F32); idxf = VB("idxf", [128, 128], F32)
    ge = VB("ge", [128, 128], F32); gs = VB("gs", [128, 8], F32)
    pq_r = [VB.res(f"pq{i}") for i in range(4)]; S_r = [VB.res(f"S{i}") for i in range(4)]
    v_r = [VB.res(f"v{i}") for i in range(16)]; i_r = [VB.res(f"i{i}") for i in range(16)]
    t_r = [VB.res(f"t{i}") for i in range(8)]; p_r = [VB.res(f"p{i}") for i in range(8)]

    def inherit(news, olds):
        d = {}
        for o in olds:
            toks = list(o.reads)
            if o.last_w is not None:
                toks.append(o.last_w)
            for t in toks:
                k = id(t[0])
                if k not in d or d[k][1] < t[1]:
                    d[k] = t
        for n_ in news:
            n_.reads = list(n_.reads) + list(d.values())

    tiles = []
    for n in range(16):
        tiles.append(dict(kind="p", n=n, nt=128, x=dap(xp, n * 128 * 1024, [[1024, 128], [1, 1024]]),
                          y=dap(yp, n * 128 * 1024, [[1024, 128], [1, 1024]]), first=(n == 0), last=(n == 15)))
    for s in range(2):
        tiles.append(dict(kind="s", s=s, nt=16, x=dap(xs, s * 16 * 1024, [[1024, 16], [1, 1024]]),
                          y=dap(ys, s * 16 * 1024, [[1024, 16], [1, 1024]]), first=True, last=True))

    F0 = P[0]

    def front(gi):
        T = tiles[gi]
        nt = T["nt"]
        xt = xtb[gi % 2]
        hb = h_sb[gi % 2]; x2 = xn2_bf[gi % 2]; ixT = idxT[gi % 2]; gtT = gT[gi % 2]
        C.dma("sp", xt[0:nt, :], T["x"], w=[xt], sbuf_side=xt)
        DBG["fstart"] = C.eng_time["sp"]
        samp = T["kind"] == "s"
        if samp:
            sp_, so_ = 0, 1
        else:
            so_ = T["n"] % 2
            sp_ = 1 - so_
        inherit(A_res, B_res)
        C.op("dve", lambda e: e.scalar_tensor_tensor(out=xn_bf[0:nt, :], in0=xt[0:nt, :], scalar=1.0, in1=xt[0:nt, :], op0=ALU.mult,
                                                     op1=ALU.mult, accum_out=ss1[0:nt, :]), r=[xt], w=[xn_bf, ss1])
        C.op("act", lambda e: e.activation(out=rs1[0:nt, :], in_=ss1[0:nt, :], func=AF.Sqrt, scale=1.0 / 1024, bias=EPS), r=[ss1], w=[rs1])
        C.op("dve", lambda e: e.reciprocal(out=rs1[0:nt, :], in_=rs1[0:nt, :]), r=[rs1], w=[rs1])
        C.op("act", lambda e: e.activation(out=xn_bf[0:nt, :], in_=xt[0:nt, :], func=AF.Copy, scale=rs1[0:nt, 0:1]),
             r=[xt, rs1], w=[xn_bf])
        for kc in range(8):
            C.op("pe", lambda e: e.transpose(F0.bf(0, 128, kc * 128, [[1, nt]]), xn_bf[0:nt, kc * 128:(kc + 1) * 128], identb[0:nt, 0:nt]),
                 r=[xn_bf, identb], w=[F0.ra])
        C.op("act", lambda e: e.activation(out=xnT[:, :, 0:nt], in_=F0.bf(0, 128, 0, [[128, 8], [1, nt]]), func=AF.Copy), r=[F0.ra], w=[xnT])
        for ct in range(8):
            hf = ct // 4
            for kc in range(8):
                C.op("pe", lambda e: e.matmul(F0.f(0, 128, hf * 512 + (ct % 4) * 128, [[1, nt]]), wi_bf[:, kc, ct * 128:(ct + 1) * 128],
                                              xnT[:, kc, 0:nt], start=(kc == 0), stop=(kc == 7)), r=[wi_r[kc], xnT], w=[F0.r(hf)])
        if samp:
            s = T["s"]
            C.op("pool", lambda e: e.tensor_copy(out=uext[:, :, 0:3], in_=vecT[:, :, R_SC + 3 * s:R_SC + 3 * s + 3]), r=[vecT], w=[uext])
            C.op("pool", lambda e: e.tensor_copy(out=hstate[:], in_=vecT[:, :, R_SL + s]), r=[vecT], w=[hstate])
        C.op("act", lambda e: e.activation(out=uext[:, :, 3:3 + nt], in_=F0.f(0, 128, 0, [[128, 4], [1, nt]]), func=AF.Copy), r=[F0.ra], w=[uext])
        C.op("act", lambda e: e.activation(out=gate[:, :, 0:nt], in_=F0.f(0, 128, 512, [[128, 4], [1, nt]]), func=AF.Copy), r=[F0.rb], w=[gate])
        for hf, (c0, c1) in enumerate(((1024, 1536), (1536, 1792))):
            for kc in range(8):
                C.op("pe", lambda e: e.matmul(F0.f(0, nt, hf * 512, [[1, c1 - c0]]), xnT[:, kc, 0:nt], wi_bf[:, kc, c0:c1],
                                              start=(kc == 0), stop=(kc == 7)), r=[wi_r[kc], xnT], w=[F0.r(hf)])
        C.op("act", lambda e: e.activation(out=qkv[0:nt, :], in_=F0.f(0, nt, 0, [[1, 768]]), func=AF.Copy), r=F0.both, w=[qkv])
        for ct in range(4):
            C.op("dve", lambda e: e.tensor_scalar(out=xc[:, ct, 0:nt], in0=uext[:, ct, 0:nt], scalar1=vecT[:, ct, R_CW:R_CW + 1],
                                                  scalar2=vecT[:, ct, R_CB:R_CB + 1], op0=ALU.mult, op1=ALU.add), r=[uext, vecT], w=[xc])
            for j in range(1, 4):
                C.op("dve", lambda e: e.scalar_tensor_tensor(out=xc[:, ct, 0:nt], in0=uext[:, ct, j:j + nt], scalar=vecT[:, ct, R_CW + j:R_CW + j + 1],
                                                             in1=xc[:, ct, 0:nt], op0=ALU.mult, op1=ALU.add), r=[uext, vecT, xc], w=[xc])
        C.op("act", lambda e: e.activation(out=xc_bf[:, :, 0:nt], in_=xc[:, :, 0:nt], func=AF.Copy), r=[xc], w=[xc_bf])
        if T["last"]:
            dstt, base = (ncp, 0) if not samp else (ncs, T["s"] * 1536)
            for ct in range(4):
                C.dma("sp", None, None, r=[uext], sbuf_side=uext, is_store=True, final=True,
                      fn=lambda e: e.dma_start(out=dap(dstt, base + ct * 128, [[1, 128], [512, 3]]), in_=uext[:, ct, nt:nt + 3],
                                               allow_slow_non_contiguous=True))
        else:
            C.op("pool", lambda e: e.tensor_copy(out=uext[:, :, 0:3], in_=uext[:, :, nt:nt + 3]), r=[uext], w=[uext])
        for ct in range(4):
            C.op("pe", lambda e: e.matmul(F0.f(0, 128, ct * 128, [[1, nt]]), wa_bd[:, ct, :], xc_bf[:, ct, 0:nt], start=True, stop=True),
                 r=[wa_bd, xc_bf], w=[F0.ra])
            C.op("pe", lambda e: e.matmul(F0.f(0, 128, 512 + ct * 128, [[1, nt]]), wi_bd[:, ct, :], xc_bf[:, ct, 0:nt], start=True, stop=True),
                 r=[wi_bd, xc_bf], w=[F0.rb])
        for ct in range(4):
            C.op("act", lambda e: e.activation(out=rr[:, ct, 0:nt], in_=F0.f(0, 128, ct * 128, [[1, nt]]), func=AF.Sigmoid,
                                               bias=vecT[:, ct, R_BA:R_BA + 1]), r=[F0.ra, vecT], w=[rr])
        for ct in range(4):
            C.op("act", lambda e: e.activation(out=ig[:, ct, 0:nt], in_=F0.f(0, 128, 512 + ct * 128, [[1, nt]]), func=AF.Sigmoid,
                                               bias=vecT[:, ct, R_BI:R_BI + 1]), r=[F0.rb, vecT], w=[ig])
        for ct in range(4):
            C.op("act", lambda e: e.activation(out=aa[:, ct, 0:nt], in_=rr[:, ct, 0:nt], func=AF.Exp, scale=clam[:, ct:ct + 1]), r=[rr, clam], w=[aa])
        for ct in range(4):
            C.op("act", lambda e: e.activation(out=t1[:, ct, 0:nt], in_=rr[:, ct, 0:nt], func=AF.Tanh, scale=nclam[:, ct:ct + 1]), r=[rr, nclam], w=[t1])
        C.op("pool", lambda e: e.tensor_tensor(out=t2[:, :, 0:nt], in0=aa[:, :, 0:nt], in1=aa[:, :, 0:nt], op=ALU.mult), r=[aa], w=[t2])
        C.op("dve", lambda e: e.scalar_tensor_tensor(out=t2[:, :, 0:nt], in0=t2[:, :, 0:nt], scalar=1.0, in1=t1[:, :, 0:nt], op0=ALU.add, op1=ALU.mult),
             r=[t2, t1], w=[t2])
        C.op("act", lambda e: e.activation(out=t2[:, :, 0:nt], in_=t2[:, :, 0:nt], func=AF.Sqrt), r=[t2], w=[t2])
        C.op("pool", lambda e: e.tensor_tensor(out=ig[:, :, 0:nt], in0=ig[:, :, 0:nt], in1=xc[:, :, 0:nt], op=ALU.mult), r=[ig, xc], w=[ig])
        C.op("pool", lambda e: e.tensor_tensor(out=ig[:, :, 0:nt], in0=ig[:, :, 0:nt], in1=t2[:, :, 0:nt], op=ALU.mult), r=[ig, t2], w=[ig])
        for ct in range(4):
            C.op("dve", lambda e: e.tensor_tensor_scan(out=hh[:, ct, 0:nt], data0=aa[:, ct, 0:nt], data1=ig[:, ct, 0:nt], initial=hstate[:, ct:ct + 1],
                                                       op0=ALU.mult, op1=ALU.add), r=[aa, ig, hstate], w=[hh])
        C.op("dve", lambda e: e.tensor_copy(out=hstate[:], in_=hh[:, :, nt - 1]), r=[hh], w=[hstate])
        if T["last"]:
            dstt, base = (nhp, 0) if not samp else (nhs, T["s"] * 512)
            C.dma("sp", None, None, r=[hstate], sbuf_side=hstate, is_store=True, final=True,
                  fn=lambda e: e.dma_start(out=dap(dstt, base, [[1, 128], [128, 4]]), in_=hstate[:], allow_slow_non_contiguous=True))
        C.op("act", lambda e: e.activation(out=t1[:, :, 0:nt], in_=gate[:, :, 0:nt], func=AF.Gelu_apprx_tanh), r=[gate], w=[t1])
        C.op("pool", lambda e: e.tensor_tensor(out=lo[:, :, 0:nt], in0=hh[:, :, 0:nt], in1=t1[:, :, 0:nt], op=ALU.mult), r=[hh, t1], w=[lo])
        C.op("pool", lambda e: e.tensor_tensor(out=t1[:, :, 0:nt], in0=lo[:, :, 0:nt], in1=lo[:, :, 0:nt], op=ALU.mult), r=[lo], w=[t1])
        for ct in range(4):
            C.op("pe", lambda e: e.matmul(F0.f(0, 128, 0, [[1, nt]]), ones_f[:], t1[:, ct, 0:nt], start=(ct == 0), stop=(ct == 3)),
                 r=[ones_f, t1], w=[F0.ra])
        C.op("act", lambda e: e.activation(out=ril[:, 0:nt], in_=F0.f(0, 128, 0, [[1, nt]]), func=AF.Sqrt, scale=1.0 / 512, bias=EPS), r=[F0.ra], w=[ril])
        C.op("dve", lambda e: e.reciprocal(out=ril[:, 0:nt], in_=ril[:, 0:nt]), r=[ril], w=[ril])
        C.op("pool", lambda e: e.tensor_tensor(out=catT[:, 0:4, 0:nt], in0=lo[:, :, 0:nt], in1=ril.pap(0, 128, 0, [[0, 4], [1, nt]]), op=ALU.mult),
             r=[lo, ril], w=[cat_l])
        C.op("pool", lambda e: e.tensor_tensor(out=sqq[0:nt, :], in0=qkv[0:nt, 0:640], in1=qkv[0:nt, 0:640], op=ALU.mult), r=[qkv], w=[sqq])
        C.op("dve", lambda e: e.tensor_reduce(out=ssq[0:nt, :], in_=sqq.pap(0, nt, 0, [[64, 10], [1, 64]]), axis=AX.X, op=ALU.add), r=[sqq], w=[ssq])
        C.op("act", lambda e: e.activation(out=riq[0:nt, :], in_=ssq[0:nt, :], func=AF.Sqrt, scale=1.0 / 64, bias=EPS), r=[ssq], w=[riq])
        C.op("dve", lambda e: e.reciprocal(out=riq[0:nt, :], in_=riq[0:nt, :]), r=[riq], w=[riq])
        C.op("pool", lambda e: e.tensor_tensor(out=tmpq.pap(0, nt, 0, [[64, 10], [1, 64]]), in0=qkv.pap(0, nt, 0, [[64, 10], [1, 64]]),
                                               in1=riq.pap(0, nt, 0, [[1, 10], [0, 64]]), op=ALU.mult), r=[qkv, riq], w=[tmpq])
        C.op("pool", lambda e: e.tensor_tensor(out=qn_bf.pap(0, nt, 0, [[64, 8], [1, 64]]), in0=tmpq.pap(0, nt, 0, [[64, 8], [1, 64]]),
                                               in1=qgrep.pap(0, nt, 0, [[0, 8], [1, 64]]), op=ALU.mult), r=[tmpq, qgrep], w=[qn_bf])
        C.op("dve", lambda e: e.tensor_tensor(out=kn.pap(0, nt, 0, [[64, 2], [1, 64]]), in0=tmpq.pap(0, nt, 512, [[64, 2], [1, 64]]),
                                              in1=kgrep.pap(0, nt, 0, [[0, 2], [1, 64]]), op=ALU.mult), r=[tmpq, kgrep], w=[kn])
        C.op("act", lambda e: e.activation(out=kn_bf[0:nt, :], in_=kn[0:nt, :], func=AF.Copy), r=[kn], w=[kn_bf])
        C.op("act", lambda e: e.activation(out=Vaug[so_].pap(0, nt, 0, [[65, 2], [1, 64]]), in_=qkv.pap(0, nt, 640, [[64, 2], [1, 64]]), func=AF.Copy),
             r=[qkv], w=[Vaug[so_]])
        if samp:
            s = T["s"]
            C.dma("sp", ck_sb[:], dap(ck, s * 16384, [[128, 128], [1, 128]]), w=[ck_sb], sbuf_side=ck_sb)
            C.dma("sp", cv_sb[:], dap(cv, s * 16384, [[128, 128], [1, 128]]), w=[cv_sb], sbuf_side=cv_sb)
            C.op("pool", lambda e: e.tensor_copy(out=ck_bf[:], in_=ck_sb[:]), r=[ck_sb], w=[ck_bf])
            C.op("pool", lambda e: e.tensor_copy(out=Vaug[sp_].pap(0, 128, 0, [[65, 2], [1, 64]]), in_=cv_sb.pap(0, 128, 0, [[64, 2], [1, 64]])),
                 r=[cv_sb], w=[Vaug[sp_]])
            for g in range(2):
                C.op("pe", lambda e: e.transpose(F0.bf(0, 64, 1024 + 256 + g * 128, [[1, 128]]), ck_bf[:, g * 64:(g + 1) * 64], identb[:]),
                     r=[ck_bf, identb], w=[F0.rb])
            C.op("dve", lambda e: e.tensor_copy(out=kTb[sp_][:], in_=F0.bf(0, 64, 1024 + 256, [[128, 2], [1, 128]])), r=[F0.rb], w=[kTb[sp_]])
            C.dma("sp", dap(nks, s * 16384, [[128, 112], [1, 128]]), ck_sb[16:128, :], r=[ck_sb], sbuf_side=ck_sb, is_store=True, final=True)
            C.dma("sp", dap(nvs, s * 16384, [[128, 112], [1, 128]]), cv_sb[16:128, :], r=[cv_sb], sbuf_side=cv_sb, is_store=True, final=True)
            C.dma("sp", dap(nks, s * 16384 + 112 * 128, [[128, 16], [1, 128]]), kn[0:16, :], r=[kn], sbuf_side=kn, is_store=True, final=True)
            C.dma("sp", dap(nvs, s * 16384 + 112 * 128, [[128, 16], [1, 128]]), qkv[0:16, 640:768], r=[qkv], sbuf_side=qkv, is_store=True, final=True)
        elif T["last"]:
            C.dma("sp", nkp.ap(), kn[:, :], r=[kn], sbuf_side=kn, is_store=True, final=True)
            C.dma("sp", nvp.ap(), qkv[:, 640:768], r=[qkv], sbuf_side=qkv, is_store=True, final=True)
        for h in range(8):
            C.op("pe", lambda e: e.transpose(F0.bf(0, 64, h * 128, [[1, nt]]), qn_bf[0:nt, h * 64:(h + 1) * 64], identb[0:nt, 0:nt]),
                 r=[qn_bf, identb], w=[F0.ra])
        for g in range(2):
            C.op("pe", lambda e: e.transpose(F0.bf(0, 64, 1024 + g * 128, [[1, nt]]), kn_bf[0:nt, g * 64:(g + 1) * 64], identb[0:nt, 0:nt]),
                 r=[kn_bf, identb], w=[F0.rb])
        C.op("act", lambda e: e.activation(out=qT[0:64, :, 0:nt], in_=F0.bf(0, 64, 0, [[128, 8], [1, nt]]), func=AF.Copy), r=[F0.ra], w=[qT])
        C.op("dve", lambda e: e.tensor_copy(out=kTb[so_][:, :, 0:nt], in_=F0.bf(0, 64, 1024, [[128, 2], [1, nt]])), r=[F0.rb], w=[kTb[so_]])
        if samp:
            blocks = [("prev", sp_, 128, [(0, 128, 0, 16)]), ("own", so_, 16, [(0, 16, 0, 16)])]
        else:
            blocks = []
            if not T["first"]:
                blocks.append(("prev", sp_, 128, [(0, 128, 0, 64), (64, 128, 64, 128)]))
            blocks.append(("own", so_, 128, [(0, 64, 0, 64), (0, 128, 64, 128)]))
        k_ = 0
        for g in range(2):
            for bi, (nm, slot, nk, regions) in enumerate(blocks):
                hf = k_ % 2
                k_ += 1
                C.op("pe", lambda e: e.matmul(F0.f(0, nk, hf * 512, [[128, 4], [1, nt]]), kTb[slot][:, g, 0:nk], qT[0:64, 4 * g:4 * g + 4, 0:nt],
                                              start=True, stop=True), r=[kTb[slot], qT], w=[F0.r(hf)])
                for (k0, k1, q0, q1) in regions:
                    C.op("act", lambda e: e.activation(out=PT[(g, nm)].pap(k0, k1 - k0, q0, [[128, 4], [1, q1 - q0]]),
                                                       in_=F0.f(k0, k1 - k0, hf * 512 + q0, [[128, 4], [1, q1 - q0]]), func=AF.Exp, scale=0.125),
                         r=[F0.r(hf)], w=[PT[(g, nm)]])
        for h in range(8):
            g, h4 = h // 4, h % 4
            for bi, (nm, slot, nk, regions) in enumerate(blocks):
                C.op("pe", lambda e: e.matmul(F0.f(0, nt, g * 512 + h4 * 65, [[1, 65]]), PT[(g, nm)].pap(0, nk, h4 * 128, [[1, nt]]),
                                              Vaug[slot].pap(0, nk, g * 65, [[1, 65]]), start=(bi == 0), stop=(bi == len(blocks) - 1)),
                     r=[PT[(g, nm)], Vaug[slot]], w=[F0.r(g)])
        C.op("dve", lambda e: e.tensor_tensor(out=den.pap(0, nt, 0, [[4, 2], [1, 4]]), in0=F0.f(0, nt, 64, [[512, 2], [65, 4]]),
                                              in1=esink.pap(0, nt, 0, [[4, 2], [1, 4]]), op=ALU.add), r=F0.both + [esink], w=[den])
        C.op("dve", lambda e: e.reciprocal(out=den[0:nt, :], in_=den[0:nt, :]), r=[den], w=[den])
        C.op("dve", lambda e: e.tensor_tensor(out=o_sb.pap(0, nt, 0, [[256, 2], [64, 4], [1, 64]]), in0=F0.f(0, nt, 0, [[512, 2], [65, 4], [1, 64]]),
                                              in1=den.pap(0, nt, 0, [[4, 2], [1, 4], [0, 64]]), op=ALU.mult), r=F0.both + [den], w=[o_sb])
        C.op("dve", lambda e: e.scalar_tensor_tensor(out=an_bf[0:nt, :], in0=o_sb[0:nt, :], scalar=1.0, in1=o_sb[0:nt, :], op0=ALU.mult, op1=ALU.mult,
                                                     accum_out=ssa[0:nt, :]), r=[o_sb], w=[an_bf, ssa])
        C.op("act", lambda e: e.activation(out=rsa[0:nt, :], in_=ssa[0:nt, :], func=AF.Sqrt, scale=1.0 / 512, bias=EPS), r=[ssa], w=[rsa])
        C.op("dve", lambda e: e.reciprocal(out=rsa[0:nt, :], in_=rsa[0:nt, :]), r=[rsa], w=[rsa])
        C.op("act", lambda e: e.activation(out=an_bf[0:nt, :], in_=o_sb[0:nt, :], func=AF.Copy, scale=rsa[0:nt, 0:1]),
             r=[o_sb, rsa], w=[an_bf])
        for j in range(4):
            C.op("pe", lambda e: e.transpose(F0.bf(0, 128, j * 128, [[1, nt]]), an_bf[0:nt, j * 128:(j + 1) * 128], identb[0:nt, 0:nt]),
                 r=[an_bf, identb], w=[F0.ra])
        C.op("act", lambda e: e.activation(out=catT[:, 4:8, 0:nt], in_=F0.bf(0, 128, 0, [[128, 4], [1, nt]]), func=AF.Copy), r=[F0.ra], w=[cat_a])
        for hf in range(2):
            for c in range(8):
                C.op("pe", lambda e: e.matmul(F0.f(0, nt, hf * 512, [[1, 512]]), catT[:, c, 0:nt], wo_bf[:, c, hf * 512:(hf + 1) * 512],
                                              start=(c == 0), stop=(c == 7)), r=[cat_l if c < 4 else cat_a, wo_r[c]], w=[F0.r(hf)])
        C.op("dve", lambda e: e.tensor_tensor(out=hb[0:nt, :], in0=xt[0:nt, :], in1=F0.f(0, nt, 0, [[1, 1024]]), op=ALU.add), r=[xt] + F0.both, w=[hb])
        inherit(B_res, A_res)
        C.op("dve", lambda e: e.scalar_tensor_tensor(out=x2[0:nt, :], in0=hb[0:nt, :], scalar=1.0, in1=hb[0:nt, :], op0=ALU.mult, op1=ALU.mult,
                                                     accum_out=ss2[0:nt, :]), r=[hb], w=[x2, ss2])
        C.op("act", lambda e: e.activation(out=ss2[0:nt, :], in_=ss2[0:nt, :], func=AF.Sqrt, scale=1.0 / 1024, bias=EPS), r=[ss2], w=[ss2])
        C.op("dve", lambda e: e.reciprocal(out=ss2[0:nt, :], in_=ss2[0:nt, :]), r=[ss2], w=[ss2])
        C.op("act", lambda e: e.activation(out=x2[0:nt, :], in_=hb[0:nt, :], func=AF.Copy, scale=ss2[0:nt, 0:1]),
             r=[hb, ss2], w=[x2])
        for kc in range(8):
            C.op("pe", lambda e: e.transpose(F0.bf(0, 128, kc * 128, [[1, nt]]), x2[0:nt, kc * 128:(kc + 1) * 128], identb[0:nt, 0:nt]),
                 r=[x2, identb], w=[F0.ra])
        C.op("act", lambda e: e.activation(out=xn2T[:, :, 0:nt], in_=F0.bf(0, 128, 0, [[128, 8], [1, nt]]), func=AF.Copy), r=[F0.ra], w=[xn2T])
        for b4 in range(4):
            hf = b4 % 2
            for g4 in range(4):
                grp = b4 * 4 + g4
                for kc in range(8):
                    C.op("pe", lambda e: e.matmul(F0.f(0, 128, hf * 512 + g4 * 128, [[1, nt]]), wq_bf[:, kc, grp * 128:(grp + 1) * 128], xn2T[:, kc, 0:nt],
                                                  start=(kc == 0), stop=(kc == 7)), r=[wq_r[kc], xn2T], w=[F0.r(hf)])
            if b4 % 2 == 0:
                C.op("act", lambda e: e.activation(out=pq_bf[:, 4 * b4:4 * b4 + 4, 0:nt], in_=F0.f(0, 128, hf * 512, [[128, 4], [1, nt]]), func=AF.Copy),
                     r=[F0.r(hf)], w=[pq_r[b4]])
            else:
                C.op("dve", lambda e: e.tensor_copy(out=pq_bf[:, 4 * b4:4 * b4 + 4, 0:nt], in_=F0.f(0, 128, hf * 512, [[128, 4], [1, nt]])),
                     r=[F0.r(hf)], w=[pq_r[b4]])
        for b4 in range(4):
            hf = b4 % 2
            for g4 in range(4):
                grp = b4 * 4 + g4
                C.op("pe", lambda e: e.matmul(F0.f(0, nt, hf * 512 + g4 * 128, [[1, 128]]), pq_bf[:, grp, 0:nt], skT[grp % 2][:], start=True, stop=True),
                     r=[pq_r[b4], skT[grp % 2]], w=[F0.r(hf)])
            if b4 % 2 == 0:
                C.op("act", lambda e: e.activation(out=S.pap(0, nt, b4 * 512, [[1, 512]]), in_=F0.f(0, nt, hf * 512, [[1, 512]]), func=AF.Copy), r=[F0.r(hf)], w=[S_r[b4]])
            else:
                C.op("dve", lambda e: e.tensor_copy(out=S.pap(0, nt, b4 * 512, [[1, 512]]), in_=F0.f(0, nt, hf * 512, [[1, 512]])), r=[F0.r(hf)], w=[S_r[b4]])
        C.mark()
        for grp in range(16):
            tb_ = tmp[grp % 2]; sr = S_r[grp // 4]
            C.op("dve", lambda e: e.max(out=v16[0:nt, grp, 0:8], in_=S[0:nt, grp, :]), r=[sr], w=[v_r[grp]])
            C.op("dve", lambda e: e.max_index(out=i16[0:nt, grp, 0:8], in_max=v16[0:nt, grp, 0:8], in_values=S[0:nt, grp, :]), r=[sr, v_r[grp]], w=[i_r[grp]])
            C.op("dve", lambda e: e.match_replace(out=tb_[0:nt, :], in_to_replace=v16[0:nt, grp, 0:8], in_values=S[0:nt, grp, :], imm_value=NEG), r=[sr, v_r[grp]], w=[tb_])
            C.op("dve", lambda e: e.max(out=v16[0:nt, grp, 8:16], in_=tb_[0:nt, :]), r=[tb_], w=[v_r[grp]])
            C.op("dve", lambda e: e.max_index(out=i16[0:nt, grp, 8:16], in_max=v16[0:nt, grp, 8:16], in_values=tb_[0:nt, :]), r=[tb_, v_r[grp]], w=[i_r[grp]])
        C.op("act", lambda e: e.activation(out=i16f[0:nt, :, :], in_=i16[0:nt, :, :], func=AF.Copy), r=i_r, w=[i16f])
        C.op("dve", lambda e: e.tensor_tensor(out=big.pap(0, nt, 0, [[256, 8], [16, 16], [1, 16]]), in0=v16.pap(0, nt, 0, [[32, 8], [1, 16], [0, 16]]),
                                               in1=v16.pap(0, nt, 16, [[32, 8], [0, 16], [1, 16]]), op=ALU.add), r=v_r, w=[big])
        for h in range(8):
            tb_ = tmp2[h % 2]
            cand = big[0:nt, h * 256:(h + 1) * 256]
            C.op("dve", lambda e: e.max(out=ts[0:nt, h, 0:8], in_=cand), r=[big], w=[t_r[h]])
            C.op("dve", lambda e: e.max_index(out=tp[0:nt, h, 0:8], in_max=ts[0:nt, h, 0:8], in_values=cand), r=[big, t_r[h]], w=[p_r[h]])
            C.op("dve", lambda e: e.match_replace(out=tb_[0:nt, :], in_to_replace=ts[0:nt, h, 0:8], in_values=cand, imm_value=NEG), r=[big, t_r[h]], w=[tb_])
            C.op("dve", lambda e: e.max(out=ts[0:nt, h, 8:16], in_=tb_[0:nt, :]), r=[tb_], w=[t_r[h]])
            C.op("dve", lambda e: e.max_index(out=tp[0:nt, h, 8:16], in_max=ts[0:nt, h, 8:16], in_values=tb_[0:nt, :]), r=[tb_, t_r[h]], w=[p_r[h]])
        C.op("dve", lambda e: e.tensor_scalar(out=pA[0:nt, :], in0=tp.pap(0, nt, 0, [[1, 128]]), scalar1=4, scalar2=None, op0=ALU.logical_shift_right), r=p_r, w=[pA])
        C.op("dve", lambda e: e.tensor_scalar(out=pB[0:nt, :], in0=tp.pap(0, nt, 0, [[1, 128]]), scalar1=15, scalar2=None, op0=ALU.bitwise_and), r=p_r, w=[pB])
        C.op("act", lambda e: e.activation(out=pAf[0:nt, :], in_=pA[0:nt, :], func=AF.Copy), r=[pA], w=[pAf])
        C.op("act", lambda e: e.activation(out=pBf[0:nt, :], in_=pB[0:nt, :], func=AF.Copy), r=[pB], w=[pBf])
        for (pf, ioff, dst) in ((pAf, 0, i1s), (pBf, 16, i2s)):
            C.op("dve", lambda e: e.tensor_tensor(out=big.pap(0, nt, 0, [[256, 8], [16, 16], [1, 16]]), in0=iota16.pap(0, nt, 0, [[0, 8], [0, 16], [1, 16]]),
                                                  in1=pf.pap(0, nt, 0, [[16, 8], [1, 16], [0, 16]]), op=ALU.is_equal), r=[iota16, pf], w=[big])
            C.op("dve", lambda e: e.tensor_tensor(out=big.pap(0, nt, 0, [[256, 8], [16, 16], [1, 16]]), in0=big.pap(0, nt, 0, [[256, 8], [16, 16], [1, 16]]),
                                                   in1=i16f.pap(0, nt, ioff, [[32, 8], [0, 16], [1, 16]]), op=ALU.mult), r=[big, i16f], w=[big])
            C.op("dve", lambda e: e.tensor_reduce(out=dst[0:nt, :], in_=big.pap(0, nt, 0, [[16, 128], [1, 16]]), axis=AX.X, op=ALU.add), r=[big], w=[dst])
        C.op("dve", lambda e: e.scalar_tensor_tensor(out=idxf[0:nt, :], in0=i1s[0:nt, :], scalar=128.0, in1=i2s[0:nt, :], op0=ALU.mult, op1=ALU.add),
             r=[i1s, i2s], w=[idxf])
        C.op("dve", lambda e: e.tensor_tensor(out=ge.pap(0, nt, 0, [[16, 8], [1, 16]]), in0=ts.pap(0, nt, 0, [[16, 8], [1, 16]]),
                                               in1=ts.pap(0, nt, 0, [[16, 8], [0, 16]]), op=ALU.subtract), r=t_r, w=[ge])
        C.op("act", lambda e: e.activation(out=ge[0:nt, :], in_=ge[0:nt, :], func=AF.Exp), r=[ge], w=[ge])
        C.op("dve", lambda e: e.tensor_reduce(out=gs[0:nt, :], in_=ge.pap(0, nt, 0, [[16, 8], [1, 16]]), axis=AX.X, op=ALU.add), r=[ge], w=[gs])
        C.op("dve", lambda e: e.reciprocal(out=gs[0:nt, :], in_=gs[0:nt, :]), r=[gs], w=[gs])
        C.op("dve", lambda e: e.tensor_tensor(out=ge.pap(0, nt, 0, [[16, 8], [1, 16]]), in0=ge.pap(0, nt, 0, [[16, 8], [1, 16]]),
                                               in1=gs.pap(0, nt, 0, [[1, 8], [0, 16]]), op=ALU.mult), r=[ge, gs], w=[ge])
        C.op("pe", lambda e: e.transpose(F0.f(0, 128, 0, [[1, nt]]), idxf[0:nt, :], identf[0:nt, 0:nt]), r=[idxf, identf], w=[F0.ra])
        C.op("pe", lambda e: e.transpose(F0.f(0, 128, 512, [[1, nt]]), ge[0:nt, :], identf[0:nt, 0:nt]), r=[ge, identf], w=[F0.rb])
        C.op("dve", lambda e: e.tensor_copy(out=ixT[:, 0:nt], in_=F0.f(0, 128, 0, [[1, nt]])), r=[F0.ra], w=[ixT])
        C.op("act", lambda e: e.activation(out=gtT[:, 0:nt], in_=F0.f(0, 128, 512, [[1, nt]]), func=AF.Copy), r=[F0.rb], w=[gtT])
        DBG["fend"] = C.eng_time["act"]

    bcP = [P[1], P[2]]

    def back(gi, stepper, kstep):
        T = tiles[gi]
        nt = T["nt"]
        xt = xtb[gi % 2]
        hb = h_sb[gi % 2]; x2 = xn2_bf[gi % 2]; ixT = idxT[gi % 2]; gtT = gT[gi % 2]

        def gather(t):
            uv = UV[t % NS]
            C.dma("pool", None, None, r=[ixT], w=[uv], sbuf_side=uv,
                  fn=lambda e: e.indirect_dma_start(out=uv[:], out_offset=None, in_=tab.ap(),
                                                    in_offset=bass.IndirectOffsetOnAxis(ap=ixT[:, t:t + 1], axis=0)))

        credit = {k_: 0.0 for k_ in BUDGET}
        tcur = [0]

        def pump():
            if stepper is None:
                return
            while not stepper.finished:
                ne = stepper.next_e
                if ne in C.eng_time:
                    if credit.get(ne, 1.0) <= 0.0:
                        break
                    r_, w_ = stepper.next_rw
                    ready = C.pred_ready(ne, r_, w_)
                    if ready > max(C.eng_time[ne], C.front_time) + SLACK:
                        break
                    t_before = max(C.eng_time[ne], ready)
                    stepper.step(1)
                    if ne in credit:
                        credit[ne] -= max(0.05, C.eng_time[ne] - t_before)
                else:
                    stepper.step(1)

        f_base = C.ninst_f

        def refill():
            mult = 1.0
            if stepper is not None and NF[0] > 0:
                exp_frac = min(1.0, (tcur[0] + 1) / (FIN_FRAC * nt))
                act_frac = (C.ninst_f - f_base) / float(NF[0])
                if act_frac < exp_frac:
                    mult = BOOST
            for e_ in credit:
                credit[e_] = min(credit[e_] + mult * BUDGET[e_], 3.0 * mult * BUDGET[e_])

        C.pump = pump if stepper is not None else None
        for t in range(min(DPF, nt)):
            gather(t)
        for t in range(nt + LAG):
            if t < nt:
                if t + DPF < nt:
                    gather(t + DPF)
                pp = bcP[t % 2]; uv = UV[t % NS]
                for hf in range(2):
                    C.op("pe", lambda e: e.matmul(pp.f(0, 128, hf * 512, [[1, 512]]), identb.pap(0, nt, t, [[0, 128]]), x2[0:nt, hf * 512:(hf + 1) * 512],
                                                  start=True, stop=True), r=[identb, x2], w=[pp.r(hf)])
                C.op("dve", lambda e: e.scalar_tensor_tensor(out=uv[:, 0:1024], in0=uv[:, 0:1024], scalar=1.0, in1=pp.f(0, 128, 0, [[1, 1024]]), op0=ALU.mult, op1=ALU.mult,
                                                             accum_out=apre[:, t:t + 1]), r=[uv] + pp.both, w=[uv, a_r[t % 8]])
                C.op("act", lambda e: e.activation(out=gel[:, t:t + 1], in_=apre[:, t:t + 1], func=AF.Gelu_apprx_tanh), r=[a_r[t % 8]], w=[g_r[t % 8]])
                wd = WDr[t % NW]
                if WD_ON_ACT:
                    C.op("act", lambda e: e.activation(out=gel[:, t:t + 1], in_=gel[:, t:t + 1], func=AF.Copy, scale=gtT[:, t:t + 1]), r=[g_r[t % 8], gtT], w=[g_r[t % 8]])
                    C.op("act", lambda e: e.activation(out=wd[:, 0:nt], in_=Zc[:, 127 - t:127 - t + nt], func=AF.Copy, scale=gel[:, t:t + 1]), r=[Zc, g_r[t % 8]], w=[wd])
                else:
                    C.op("dve", lambda e: e.tensor_scalar(out=wd[:, 0:nt], in0=Zc[:, 127 - t:127 - t + nt], scalar1=gel[:, t:t + 1], scalar2=gtT[:, t:t + 1],
                                                          op0=ALU.mult, op1=ALU.mult), r=[Zc, g_r[t % 8], gtT], w=[wd])
            tcur[0] = t
            refill()
            if DEBUG and gi == 2 and t % 8 == 0:
                print("      tok", t, {k_: round(v_, 1) for k_, v_ in C.eng_time.items()}, "front", round(C.front_time, 1), "dma_free", round(C.dma_free, 1), "F done" if (stepper is None or stepper.finished) else "F next " + str(stepper.next_e))
            tv = t - LAG
            if 0 <= tv < nt:
                uv = UV[tv % NS]; wd = WDr[tv % NW]
                for hf in range(2):
                    C.op("pe", lambda e: e.matmul(P[3].f(0, nt, hf * 512, [[1, 512]]), wd[:, 0:nt], uv[:, 1024 + hf * 512:1024 + (hf + 1) * 512],
                                                  start=(tv == 0), stop=(tv == nt - 1)), r=[wd, uv], w=[P[3].r(hf)])
        C.op("dve", lambda e: e.tensor_tensor(out=xt[0:nt, :], in0=hb[0:nt, :], in1=P[3].f(0, nt, 0, [[1, 1024]]), op=ALU.add), r=[hb] + P[3].both, w=[xt])
        C.dma("sp", T["y"], xt[0:nt, :], r=[xt], sbuf_side=xt, is_store=True, final=True)
        C.pump = None

    NF = [0]
    n_before = C.ninst
    front(0)
    NF[0] = C.ninst - n_before
    for gi in range(len(tiles)):
        stp = None
        if gi + 1 < len(tiles):
            stp = Stepper(C, lambda: front(gi + 1))
            if MARKMODE:
                stp.run_to_mark()
        n0 = C.ninst
        back(gi, stp, KSTEP)
        n1 = C.ninst
        if stp is not None:
            stp.drain()
        if DEBUG:
            print("   F(gi+1) ended at model t=%.1f ; F started at %.1f" % (DBG.get("fend", 0), DBG.get("fstart", 0)))
            print("tile", gi, "model time", {k_: round(v_, 1) for k_, v_ in C.eng_time.items()}, "instr in back(incl F)", n1 - n0, "drained", C.ninst - n1)
    C.finish()
    return nc


KSTEP = 7
MARKMODE = 0
NSLOT, NDPF, NLAG = 12, 6, 5
WD_ON_ACT = 1
FIN_FRAC = 0.8
BOOST = 2.0
DEBUG = 0
DBG = {}
SLACK = 0.15
BUDGET = {"dve": 0.7, "pe": 0.6, "act": 1.0, "pool": 0.5, "sp": 1.0}
COST = {"dve": 0.33, "pe": 0.15, "act": 0.4, "pool": 1.0}
SKIPRAW = 0
_CACHE = {}


def kernel(x_prompt, x_sample, cache_attn_k, cache_attn_v, state_conv, state_lru,
           ln1_g, w_in, conv_w, conv_b, lru_wa, lru_ba, lru_wi, lru_bi, lru_lambda,
           q_norm_g, k_norm_g, attn_sinks, g_lru_out, g_attn_out, w_out, ln2_g,
           peer_w_query, peer_sub_keys1, peer_sub_keys2, peer_u, peer_v):
    f = lambda a: np.ascontiguousarray(np.asarray(a, dtype=np.float32))
    x_prompt = f(x_prompt); x_sample = f(x_sample)
    if "nc" not in _CACHE:
        _CACHE["nc"] = build_program()
    nc = _CACHE["nc"]
    shared = dict(
        ln1=f(ln1_g).reshape(1, 1024), ln2=f(ln2_g).reshape(1, 1024),
        qg=f(q_norm_g).reshape(1, 64), kg=f(k_norm_g).reshape(1, 64), snk=f(attn_sinks).reshape(1, 8),
        w_in=f(w_in).reshape(1024, 1792), w_out=f(w_out).reshape(1024, 1024), w_q=f(peer_w_query).reshape(1024, 2048),
        wa=f(lru_wa).reshape(8, 64, 64), wi=f(lru_wi).reshape(8, 64, 64),
        sk1=f(peer_sub_keys1).reshape(128, 128), sk2=f(peer_sub_keys2).reshape(128, 128),
        pu=f(peer_u).reshape(16384, 1024), pv=f(peer_v).reshape(16384, 1024),
    )
    common_rows = [f(conv_w).reshape(4, 512), f(conv_b).reshape(1, 512), f(lru_ba).reshape(1, 512), f(lru_bi).reshape(1, 512),
                   f(lru_lambda).reshape(1, 512), f(g_lru_out).reshape(1, 512)]
    ga_row = f(g_attn_out).reshape(1, 512)
    sc = f(state_conv).reshape(16, 3, 512); sl = f(state_lru).reshape(16, 512)
    ckf = f(cache_attn_k).reshape(16, 128, 128); cvf = f(cache_attn_v).reshape(16, 128, 128)
    in_maps = []
    for c in range(8):
        s0, s1 = 2 * c, 2 * c + 1
        v512 = np.concatenate(common_rows + [sc[s0], sc[s1], sl[s0:s0 + 1], sl[s1:s1 + 1], ga_row], axis=0)
        m = dict(shared)
        m.update(xp=x_prompt[c], xs=np.ascontiguousarray(x_sample[s0:s1 + 1].reshape(32, 1024)),
                 ck=np.ascontiguousarray(ckf[s0:s1 + 1]), cv=np.ascontiguousarray(cvf[s0:s1 + 1]), v512=np.ascontiguousarray(v512))
        in_maps.append(m)
    res = run_bass_kernel_spmd(nc, in_maps, core_ids=list(range(8)))
    R = res.results
    y_p = np.stack([R[c]["yp"] for c in range(8)], 0).reshape(8, 2048, 1024)
    y_s = np.concatenate([R[c]["ys"] for c in range(8)], 0).reshape(16, 16, 1024)
    nk_p = np.stack([R[c]["nkp"] for c in range(8)], 0).reshape(1, 8, 128, 2, 64)
    nv_p = np.stack([R[c]["nvp"] for c in range(8)], 0).reshape(1, 8, 128, 2, 64)
    nc_p = np.stack([R[c]["ncp"] for c in range(8)], 0).reshape(1, 8, 3, 512)
    nh_p = np.stack([R[c]["nhp"] for c in range(8)], 0).reshape(1, 8, 512)
    nk_s = np.concatenate([R[c]["nks"] for c in range(8)], 0).reshape(1, 16, 128, 2, 64)
    nv_s = np.concatenate([R[c]["nvs"] for c in range(8)], 0).reshape(1, 16, 128, 2, 64)
    nc_s = np.concatenate([R[c]["ncs"] for c in range(8)], 0).reshape(1, 16, 3, 512)
    nh_s = np.concatenate([R[c]["nhs"] for c in range(8)], 0).reshape(1, 16, 512)
    return (y_p, y_s, nk_p, nv_p, nc_p, nh_p, nk_s, nv_s, nc_s, nh_s)
```

```python
import numpy as np
import concourse.bass as bass
import concourse.mybir as mybir
from concourse.bass_utils import run_bass_kernel_spmd

F32 = mybir.dt.float32
BF16 = mybir.dt.bfloat16
U32 = mybir.dt.uint32
I32 = mybir.dt.int32
AF = mybir.ActivationFunctionType
ALU = mybir.AluOpType
AX = mybir.AxisListType

SEM_LIMIT = 30000
SKIP_SAME_RAW = ()
STRICT = 1


class Res:
    __slots__ = ("name", "last_w", "reads", "dsem_w", "dsem_r", "dcnt_w", "dcnt_r")

    def __init__(self, name):
        self.name = name
        self.last_w = None
        self.reads = []
        self.dsem_w = None
        self.dsem_r = None
        self.dcnt_w = 0
        self.dcnt_r = 0


class Buf:
    def __init__(self, C, name, shape, dtype, psum=False, stack=None):
        self.name = name
        if psum:
            self.t = C.nc.alloc_psum_tensor(name, list(shape), dtype)
        elif stack is not None:
            self.t = stack.enter_context(C.nc.sbuf_tensor(name, list(shape), dtype))
        else:
            self.t = C.nc.alloc_sbuf_tensor(name, list(shape), dtype)
        self.res = Res(name)
        if stack is not None:
            stack.callback(C.release_res, self.res)
        self.shape = list(shape)
        self.dtype = dtype

    def __getitem__(self, k):
        return self.t[k]

    def ap(self, offset, dims):
        fs = 1
        for s in self.shape[1:]:
            fs *= s
        return bass.AP(self.t, offset, [[fs, self.shape[0]]] + [list(d) for d in dims])

    def pap(self, p0, pn, offset, dims):
        fs = 1
        for s in self.shape[1:]:
            fs *= s
        return bass.AP(self.t, p0 * fs + offset, [[fs, pn]] + [list(d) for d in dims])


def _ap_n(ap):
    try:
        sh = list(ap.shape)
        n = 1
        for s_ in sh[1:]:
            n *= int(s_)
        return int(sh[0]), n
    except Exception:
        return 128, 128


_DT_SIZE = {}


def _dsize(ap):
    try:
        d = ap.dtype
        if d == BF16:
            return 2
        return 4
    except Exception:
        return 4


class EngProxy:
    def __init__(self, eng):
        self._eng = eng
        self.name = None
        self.n = 128
        self.passes = 1
        self.nbytes = 0

    def __getattr__(self, name):
        real = getattr(self._eng, name)

        def call(*a, **kw):
            self.name = name
            out = kw.get("out", a[0] if a else None)
            if name in ("matmul", "transpose"):
                rhs = kw.get("rhs", a[2] if len(a) > 2 else None)
                if name == "transpose":
                    src = kw.get("in_", a[1] if len(a) > 1 else None)
                    p_, n_ = _ap_n(src)
                    self.n = p_
                else:
                    p_, n_ = _ap_n(rhs)
                    self.n = n_
                    self.passes = 4 if _dsize(rhs) == 4 else 1
            elif name in ("max", "max_index", "match_replace"):
                src = kw.get("in_", kw.get("in_values", None))
                p_, n_ = _ap_n(src)
                self.n = n_
            elif name == "tensor_reduce":
                p_, n_ = _ap_n(kw.get("in_"))
                self.n = n_
            elif out is not None:
                p_, n_ = _ap_n(out)
                self.n = n_
                self.nbytes = p_ * n_ * _dsize(out)
            return real(*a, **kw)

        return call


def op_cost(e, name, n, passes):
    if e == "pe":
        return 0.11 + n * passes * 0.00032
    if e == "dve":
        return 0.07 + n * 0.00105
    if e == "act":
        return 0.2 + n * 0.00088
    if e == "pool":
        return 0.3 + n * 0.0023
    return 0.1


class Ctx:
    def __init__(self, nc):
        self.nc = nc
        self.eng = {"pe": nc.tensor, "act": nc.scalar, "dve": nc.vector,
                    "pool": nc.gpsimd, "sp": nc.sync}
        self.esem = {}
        self.ecnt = {}
        self.waited = {e: {} for e in self.eng}
        self.semn = 0
        self.ninst = 0
        self.final_tokens = []
        self.sem_pool = []
        self.pending = []
        self.hook_thread = None
        self.pump = None
        self.in_pump = False
        self.eng_time = {e: 0.0 for e in self.eng}
        self.tok_time = {}
        self.dma_free = 0.0
        self.front_time = 0.0
        self.log = None
        self.ninst_f = 0

    def release_res(self, res):
        if res.dsem_w is not None and res.dcnt_w < SEM_LIMIT:
            self.sem_pool.append((res.dsem_w, res.dcnt_w))
        if res.dsem_r is not None and res.dcnt_r < SEM_LIMIT:
            self.sem_pool.append((res.dsem_r, res.dcnt_r))
        res.dsem_w = res.dsem_r = None

    def get_dsem(self, name):
        if self.sem_pool:
            return self.sem_pool.pop()
        return (self.newsem(name), 0)

    def barrier(self):
        engs = list(self.eng)
        for e in engs:
            for e2 in list(self.esem):
                if e2 != e:
                    self._wait(e, (self.esem[e2], self.ecnt[e2], e2))
            d = {}
            for t in self.pending:
                k = id(t[0])
                if k not in d or d[k][1] < t[1]:
                    d[k] = t
            for t in d.values():
                self._wait(e, t)
        self.pending = []

    def newsem(self, name):
        self.semn += 1
        return self.nc.alloc_semaphore(name=f"{name}_{self.semn}")

    def buf(self, name, shape, dtype, psum=False):
        return Buf(self, name, shape, dtype, psum)

    def _next_tok(self, e):
        if e not in self.esem or self.ecnt[e] >= SEM_LIMIT:
            self.esem[e] = self.newsem("e" + e)
            self.ecnt[e] = 0
        self.ecnt[e] += 1
        return (self.esem[e], self.ecnt[e], e)

    def _wait(self, e, tok):
        sem, val, src = tok
        key = id(sem)
        w = self.waited[e]
        if w.get(key, (None, 0))[1] >= val:
            return False
        self.eng[e].wait_ge(sem, val)
        w[key] = (sem, val)
        return True

    def _dep_tokens(self, e, r, w, same_engine_raw=True):
        toks = []
        for b in r:
            res = b.res if hasattr(b, 'res') else b
            if res.last_w is not None:
                t = res.last_w
                if t[2] == e and (e == "pe" or not same_engine_raw or e in SKIP_SAME_RAW):
                    continue
                toks.append(t)
        strict = STRICT and e != "pe"
        for b in w:
            res = b.res if hasattr(b, 'res') else b
            if res.last_w is not None:
                t = res.last_w
                if strict or not (t[2] == e):
                    toks.append(t)
            for t in res.reads:
                if t[2] == e and not strict:
                    continue
                toks.append(t)
        return toks

    def pred_ready(self, e, r, w):
        t_ = 0.0
        for tk in self._dep_tokens(e, r, w):
            t_ = max(t_, self.tok_time.get((id(tk[0]), tk[1]), 0.0))
        return t_

    def _deps(self, e, r, w, same_engine_raw=True):
        toks = self._dep_tokens(e, r, w, same_engine_raw)
        t_ = 0.0
        nw = 0
        for t in toks:
            t_ = max(t_, self.tok_time.get((id(t[0]), t[1]), 0.0))
            if self._wait(e, t):
                nw += 1
        return t_, nw

    def _commit(self, tok, r, w):
        for b in w:
            res = b.res if hasattr(b, 'res') else b
            res.last_w = tok
            res.reads = []
        for b in r:
            res = b.res if hasattr(b, 'res') else b
            if res.last_w is tok:
                continue
            res.reads.append(tok)
            if len(res.reads) > 64:
                d = {}
                for t in res.reads:
                    k = id(t[0])
                    if k not in d or d[k][1] < t[1]:
                        d[k] = t
                res.reads = list(d.values())

    def _hook(self, e=None, r=(), w=()):
        ht = self.hook_thread
        import threading as _th
        if ht is not None and _th.current_thread() is ht[0]:
            ht[1](e, r, w)
        elif self.pump is not None and not self.in_pump:
            self.in_pump = True
            try:
                self.pump()
            finally:
                self.in_pump = False

    def mark(self, name="MARK"):
        self._hook(name)

    def _is_main(self):
        ht = self.hook_thread
        if ht is None:
            return True
        import threading as _th
        return _th.current_thread() is not ht[0]

    def op(self, e, fn, r=(), w=()):
        self._hook(e, r, w)
        t_ready, nw = self._deps(e, r, w)
        tok = self._next_tok(e)
        px = EngProxy(self.eng[e])
        ins = fn(px)
        ins.then_inc(tok[0], 1)
        self._commit(tok, r, w)
        self.ninst += 1
        start = max(self.eng_time[e], t_ready) + (0.08 if nw else 0.0)
        end = start + op_cost(e, px.name, px.n, px.passes)
        self.eng_time[e] = end
        self.tok_time[(id(tok[0]), tok[1])] = end + 0.06
        if self.log is not None:
            self.log.append(("M" if self._is_main() else "F", e, px.name, px.n, round(t_ready, 2), round(start, 2), round(end, 2)))
        if self._is_main():
            self.front_time = max(self.front_time, start)
        else:
            self.ninst_f += 1
        return tok

    def dma(self, q, out, in_, r=(), w=(), sbuf_side=None, is_store=False, fn=None, final=False, temp_store=False):
        self._hook(q, r, w)
        t_ready, nw = self._deps(q, r, w, same_engine_raw=True)
        res = sbuf_side.res if hasattr(sbuf_side, 'res') else sbuf_side
        if is_store:
            if res.dsem_r is None:
                res.dsem_r, res.dcnt_r = self.get_dsem("dr")
            res.dcnt_r += 16
            tok = (res.dsem_r, res.dcnt_r, "dma")
        else:
            if res.dsem_w is None:
                if q == "pool":
                    res.dsem_w, res.dcnt_w = self.newsem("dg"), 0
                else:
                    res.dsem_w, res.dcnt_w = self.get_dsem("dw")
            res.dcnt_w += 16
            tok = (res.dsem_w, res.dcnt_w, "dma")
        px = EngProxy(self.eng[q])
        if fn is None:
            ins = px.dma_start(out=out, in_=in_)
        else:
            ins = fn(px)
        ins.then_inc(tok[0], 16)
        self._commit(tok, r, w)
        self.ninst += 1
        if final:
            self.final_tokens.append(tok)
        if temp_store:
            self.pending.append(tok)
        start = max(self.eng_time[q], t_ready) + (0.08 if nw else 0.0)
        issue = 1.1 if px.name == "indirect_dma_start" else 0.1
        self.eng_time[q] = start + issue
        xfer = px.nbytes / 3.0e5
        s2 = max(start + issue, self.dma_free)
        self.dma_free = s2 + xfer
        self.tok_time[(id(tok[0]), tok[1])] = s2 + xfer + 2.0
        if self._is_main():
            self.front_time = max(self.front_time, start)
        return tok

    def finish(self):
        for t in self.final_tokens:
            self._wait("sp", t)
        for e in self.esem:
            self._wait("sp", (self.esem[e], self.ecnt[e], e))


import contextlib
import threading

EPS = 1e-6
NEG = -1.0e30
R_CW, R_CB, R_BA, R_BI, R_LAM, R_GL, R_SC, R_SL, R_GA, NROW = 0, 4, 5, 6, 7, 8, 9, 15, 17, 18


class PS:
    def __init__(self, C, name):
        self.b = C.buf(name, [128, 1024], F32, psum=True)
        self.t = self.b.t
        self.ra = Res(name + "a")
        self.rb = Res(name + "b")
        self.bt = self.t[:, :].bitcast(BF16)

    def f(self, p0, pn, off, dims):
        return bass.AP(self.t, p0 * 1024 + off, [[1024, pn]] + [list(d) for d in dims])

    def bf(self, p0, pn, off, dims):
        return bass.AP(self.bt.tensor, self.bt.offset + p0 * 2048 + off, [[2048, pn]] + [list(d) for d in dims])

    def r(self, half):
        return self.ra if half == 0 else self.rb

    @property
    def both(self):
        return [self.ra, self.rb]


class VBuf:
    def __init__(self, arena, f32_off, shape, dtype, name):
        esz = 2 if dtype == BF16 else 4
        n = 1
        for s_ in shape[1:]:
            n *= s_
        nf32 = (n * esz + 3) // 4
        self.nf32 = nf32
        base = arena.t[:, f32_off:f32_off + nf32]
        if dtype != F32:
            base = base.bitcast(dtype)
        self.base = base
        self.pstep = base.ap[0][0]
        self.off0 = base.offset
        self.tensor = base.tensor
        self.shape = list(shape)
        self.n = n
        self.res = Res(name)
        self.name = name
        if len(shape) == 2:
            self.v = base[:, 0:n] if n != base.shape[1] else base
        elif len(shape) == 3:
            self.v = base[:, 0:n].rearrange("p (a b) -> p a b", a=shape[1], b=shape[2])
        else:
            raise ValueError

    def __getitem__(self, k):
        return self.v[k]

    def pap(self, p0, pn, off, dims):
        return bass.AP(self.tensor, self.off0 + p0 * self.pstep + off, [[self.pstep, pn]] + [list(d) for d in dims])

    def ap(self, off, dims):
        return self.pap(0, self.shape[0], off, dims)


class Stepper:
    def __init__(self, C, fn):
        self.C = C
        self.go = threading.Semaphore(0)
        self.back = threading.Semaphore(0)
        self.finished = False
        self.err = None
        self.next_e = None
        self.at_mark = False

        def run():
            self.go.acquire()
            try:
                fn()
            except BaseException as ex:
                self.err = ex
            self.finished = True
            self.C.hook_thread = None
            self.back.release()

        self.th = threading.Thread(target=run)
        self.th.start()

    def hook(self, e=None, r=(), w=()):
        self.next_e = e
        self.next_rw = (r, w)
        if e == "MARK":
            self.at_mark = True
        self.back.release()
        self.go.acquire()

    def step(self, n=1):
        for _ in range(n):
            if self.finished:
                break
            self.C.hook_thread = (self.th, self.hook)
            self.go.release()
            self.back.acquire()
            self.C.hook_thread = None
        if self.err is not None:
            raise self.err

    def run_to_mark(self):
        while not self.finished and not self.at_mark:
            self.step(1)

    def drain(self):
        while not self.finished:
            self.step(64)
        self.th.join()
        if self.err is not None:
            raise self.err


def build_program():
    global SKIP_SAME_RAW
    SKIP_SAME_RAW = {0: (), 1: ("dve",), 2: ("dve", "act"), 3: ("dve", "act", "pool")}[SKIPRAW]
    nc = bass.Bass("TRN2", target_bir_lowering=False)
    C = Ctx(nc)
    if DEBUG:
        C.log = []
        DBG["C"] = C
    cnt = [0]

    def DI(name, shape, dt=F32):
        return nc.dram_tensor(name, list(shape), dt, kind="ExternalInput")

    def DO(name, shape, dt=F32):
        return nc.dram_tensor(name, list(shape), dt, kind="ExternalOutput")

    xp = DI("xp", [2048, 1024]); xs = DI("xs", [32, 1024])
    ck = DI("ck", [2, 128, 128]); cv = DI("cv", [2, 128, 128])
    v512 = DI("v512", [NROW, 512])
    ln1 = DI("ln1", [1, 1024]); ln2 = DI("ln2", [1, 1024])
    qg = DI("qg", [1, 64]); kg = DI("kg", [1, 64]); snk = DI("snk", [1, 8])
    w_in = DI("w_in", [1024, 1792]); w_out = DI("w_out", [1024, 1024]); w_q = DI("w_q", [1024, 2048])
    wa = DI("wa", [8, 64, 64]); wi = DI("wi", [8, 64, 64])
    sk1 = DI("sk1", [128, 128]); sk2 = DI("sk2", [128, 128])
    pu = DI("pu", [16384, 1024]); pv = DI("pv", [16384, 1024])
    yp = DO("yp", [2048, 1024]); ys = DO("ys", [32, 1024])
    nkp = DO("nkp", [128, 128]); nvp = DO("nvp", [128, 128]); ncp = DO("ncp", [3, 512]); nhp = DO("nhp", [1, 512])
    nks = DO("nks", [2, 128, 128]); nvs = DO("nvs", [2, 128, 128]); ncs = DO("ncs", [2, 3, 512]); nhs = DO("nhs", [2, 512])
    tab = nc.dram_tensor("tab", [16384, 2048], BF16, kind="Internal")

    def dap(t, off, dims):
        return bass.AP(t, off, [list(d) for d in dims])

    def B(name, shape, dt, stack=None):
        cnt[0] += 1
        return Buf(C, f"{name}_{cnt[0]}", shape, dt, stack=stack)

    def barrier():
        C.barrier()

    P = [PS(C, f"P{i}") for i in range(4)]
    identf = B("identf", [128, 128], F32); identb = B("identb", [128, 128], BF16); ones_f = B("ones_f", [128, 128], F32)
    wi_bf = B("wi_bf", [128, 8, 1792], BF16); wo_bf = B("wo_bf", [128, 8, 1024], BF16); wq_bf = B("wq_bf", [128, 8, 2048], BF16)
    wi_r = [Res(f"wi{k}") for k in range(8)]; wo_r = [Res(f"wo{k}") for k in range(8)]; wq_r = [Res(f"wq{k}") for k in range(8)]
    wa_bd = B("wa_bd", [128, 4, 128], BF16); wi_bd = B("wi_bd", [128, 4, 128], BF16)
    skT = [B("skT0", [128, 128], BF16), B("skT1", [128, 128], BF16)]
    vecT = B("vecT", [128, 4, NROW], F32); gT8 = B("gT8", [128, 2, 8], F32)
    clam = B("clam", [128, 4], F32); nclam = B("nclam", [128, 4], F32)
    qgrep = B("qgrep", [128, 64], F32); kgrep = B("kgrep", [128, 64], F32); esink = B("esink", [128, 8], F32)
    uext = B("uext", [128, 4, 131], F32); hstate = B("hstate", [128, 4], F32)
    kTb = [B("kT0", [64, 2, 128], BF16), B("kT1", [64, 2, 128], BF16)]
    Vaug = [B("Va0", [128, 2, 65], BF16), B("Va1", [128, 2, 65], BF16)]
    PT = {(g, nm): B(f"PT{g}{nm}", [128, 4, 128], BF16) for g in range(2) for nm in ("prev", "own")}
    Zc = B("Zc", [128, 256], BF16)
    iota16 = B("iota16", [128, 16], F32)

    C.op("pool", lambda e: e.memset(ones_f[:], 1.0), w=[ones_f])
    C.op("pool", lambda e: e.affine_select(out=identf[:], in_=ones_f[:], pattern=[[-1, 128]], compare_op=ALU.is_equal,
                                           fill=0.0, base=0, channel_multiplier=1), r=[ones_f], w=[identf])
    C.op("dve", lambda e: e.tensor_copy(out=identb[:], in_=identf[:]), r=[identf], w=[identb])
    for b_ in Vaug:
        C.op("pool", lambda e: e.memset(b_[:], 1.0), w=[b_])
    for b_ in PT.values():
        C.op("pool", lambda e: e.memset(b_[:], 0.0), w=[b_])
    C.op("pool", lambda e: e.memset(uext[:], 0.0), w=[uext])
    C.op("pool", lambda e: e.memset(hstate[:], 0.0), w=[hstate])
    C.op("pool", lambda e: e.memset(Zc[:], 0.0), w=[Zc])
    C.op("pool", lambda e: e.memset(Zc[:, 127:128], 1.0), w=[Zc])
    C.op("pool", lambda e: e.iota(iota16[:], pattern=[[1, 16]], base=0, channel_multiplier=0, allow_small_or_imprecise_dtypes=True), w=[iota16])

    with contextlib.ExitStack() as st:
        v_sb = B("v_sb", [32, 512], F32, st)
        C.dma("sp", v_sb[0:NROW, :], v512.ap(), w=[v_sb], sbuf_side=v_sb)
        for ct in range(4):
            C.op("pe", lambda e: e.transpose(P[0].f(0, 128, 512 + ct * 32, [[1, NROW]]), v_sb[0:NROW, ct * 128:(ct + 1) * 128], identf[0:NROW, 0:NROW]),
                 r=[v_sb, identf], w=[P[0].rb])
        C.op("act", lambda e: e.activation(out=vecT[:], in_=P[0].f(0, 128, 512, [[32, 4], [1, NROW]]), func=AF.Copy), r=[P[0].rb], w=[vecT])
        g_sb = B("g_sb", [16, 128], F32, st)
        C.dma("sp", g_sb[0:8, :], dap(ln1, 0, [[128, 8], [1, 128]]), w=[g_sb], sbuf_side=g_sb)
        C.dma("sp", g_sb[8:16, :], dap(ln2, 0, [[128, 8], [1, 128]]), w=[g_sb], sbuf_side=g_sb)
        C.op("pe", lambda e: e.transpose(P[0].f(0, 128, 0, [[1, 16]]), g_sb[0:16, :], identf[0:16, 0:16]), r=[g_sb, identf], w=[P[0].ra])
        C.op("act", lambda e: e.activation(out=gT8[:], in_=P[0].f(0, 128, 0, [[8, 2], [1, 8]]), func=AF.Copy), r=[P[0].ra], w=[gT8])
        stg = [B("stg0", [128, 2048], F32, st), B("stg1", [128, 2048], F32, st), B("stg2", [128, 2048], F32, st)]
        jobs = []
        for kc in range(8):
            jobs.append((dap(w_in, kc * 128 * 1792, [[1792, 128], [1, 1792]]), wi_bf[:, kc, :], 1792, wi_r[kc], gT8[:, 0, kc:kc + 1], [gT8]))
        for kc in range(8):
            sc_ = vecT[:, kc, R_GL:R_GL + 1] if kc < 4 else vecT[:, kc - 4, R_GA:R_GA + 1]
            jobs.append((dap(w_out, kc * 128 * 1024, [[1024, 128], [1, 1024]]), wo_bf[:, kc, :], 1024, wo_r[kc], sc_, [vecT]))
        for kc in range(8):
            jobs.append((dap(w_q, kc * 128 * 2048, [[2048, 128], [1, 2048]]), wq_bf[:, kc, :], 2048, wq_r[kc], gT8[:, 1, kc:kc + 1], [gT8]))
        for j, (src, dst, n, rr, scl, sr) in enumerate(jobs):
            s_ = stg[j % 3]
            C.dma("sp", s_[:, 0:n], src, w=[s_], sbuf_side=s_)
            en = ("act", "act", "dve")[j % 3]
            if en == "act":
                C.op("act", lambda e: e.activation(out=dst, in_=s_[:, 0:n], func=AF.Copy, scale=scl), r=[s_] + sr, w=[rr])
            else:
                C.op(en, lambda e: e.tensor_scalar(out=dst, in0=s_[:, 0:n], scalar1=scl, scalar2=None, op0=ALU.mult), r=[s_] + sr, w=[rr])
        for (src_t, dstb) in ((wa, wa_bd), (wi, wi_bd)):
            sw = B("stgw", [128, 4, 128], F32, st)
            C.op("pool", lambda e: e.memset(sw[:], 0.0), w=[sw])
            C.dma("sp", sw.pap(0, 64, 0, [[128, 4], [1, 64]]), dap(src_t, 0, [[64, 64], [8192, 4], [1, 64]]), w=[sw], sbuf_side=sw)
            C.dma("sp", sw.pap(64, 64, 64, [[128, 4], [1, 64]]), dap(src_t, 4096, [[64, 64], [8192, 4], [1, 64]]), w=[sw], sbuf_side=sw)
            C.op("dve", lambda e: e.tensor_copy(out=dstb[:], in_=sw[:]), r=[sw], w=[dstb])
        for i_, src_t in enumerate((sk1, sk2)):
            sk_sb = B("sk_sb", [128, 128], F32, st)
            C.dma("sp", sk_sb[:], src_t.ap(), w=[sk_sb], sbuf_side=sk_sb)
            C.op("pe", lambda e: e.transpose(P[0].f(0, 128, i_ * 128, [[1, 128]]), sk_sb[:], identf[:]), r=[sk_sb, identf], w=[P[0].ra])
            C.op("act", lambda e: e.activation(out=skT[i_][:], in_=P[0].f(0, 128, i_ * 128, [[1, 128]]), func=AF.Copy), r=[P[0].ra], w=[skT[i_]])
        e1 = B("e1", [128, 4], F32, st)
        C.op("act", lambda e: e.activation(out=e1[:], in_=vecT[:, :, R_LAM], func=AF.Exp, scale=-1.0), r=[vecT], w=[e1])
        C.op("act", lambda e: e.activation(out=e1[:], in_=e1[:], func=AF.Ln, bias=1.0), r=[e1], w=[e1])
        C.op("dve", lambda e: e.tensor_scalar(out=clam[:], in0=e1[:], scalar1=-8.0, scalar2=None, op0=ALU.mult), r=[e1], w=[clam])
        C.op("dve", lambda e: e.tensor_scalar(out=nclam[:], in0=e1[:], scalar1=8.0, scalar2=None, op0=ALU.mult), r=[e1], w=[nclam])
        for (dt_, buf_, n_) in ((qg, qgrep, 64), (kg, kgrep, 64), (snk, esink, 8)):
            C.dma("sp", buf_[:], dap(dt_, 0, [[0, 128], [1, n_]]), w=[buf_], sbuf_side=buf_)
        C.op("act", lambda e: e.activation(out=esink[:], in_=esink[:], func=AF.Exp), r=[esink], w=[esink])
        barrier()
    xtb = [B("xt0", [128, 1024], F32), B("xt1", [128, 1024], F32)]
    h_sb = [B("h_sb0", [128, 1024], F32), B("h_sb1", [128, 1024], F32)]
    xn2_bf = [B("xn2_0", [128, 1024], BF16), B("xn2_1", [128, 1024], BF16)]
    idxT = [B("idxT0", [128, 128], U32), B("idxT1", [128, 128], U32)]
    gT = [B("gT0", [128, 128], F32), B("gT1", [128, 128], F32)]
    NS, DPF, LAG, NW = NSLOT, NDPF, NLAG, NLAG + 3
    assert NS >= DPF + LAG + 1
    uvarena = B("uvarena", [128, NS * 1024], F32)
    UV = [VBuf(uvarena, i * 1024, [128, 2048], BF16, f"UV{i}") for i in range(NS)]
    WDr = [B(f"WD{i}", [128, 128], BF16) for i in range(NW)]
    apre = B("apre", [128, 128], F32); gel = B("gel", [128, 128], F32)
    a_r = [Res(f"ap{i}") for i in range(8)]; g_r = [Res(f"gl{i}") for i in range(8)]

    ARENA = 11008
    arena = B("arena", [128, ARENA], F32)
    A_res, B_res = [], []

    class Alloc:
        def __init__(self, lst):
            self.off = 0
            self.lst = lst

        def __call__(self, name, shape, dt):
            v = VBuf(arena, self.off, shape, dt, name)
            self.off += v.nf32
            assert self.off <= ARENA, (name, self.off)
            self.lst.append(v.res)
            return v

        def res(self, name):
            r_ = Res(name)
            self.lst.append(r_)
            return r_

    VA = Alloc(A_res); VB = Alloc(B_res)
    ss1 = VA("ss1", [128, 1], F32); rs1 = VA("rs1", [128, 1], F32)
    xn_bf = VA("xn_bf", [128, 1024], BF16); xnT = VA("xnT", [128, 8, 128], BF16)
    gate = VA("gate", [128, 4, 128], F32); xc = VA("xc", [128, 4, 128], F32); xc_bf = VA("xc_bf", [128, 4, 128], BF16)
    rr = VA("rr", [128, 4, 128], F32); ig = VA("ig", [128, 4, 128], F32); aa = VA("aa", [128, 4, 128], F32)
    t1 = VA("t1", [128, 4, 128], F32); t2 = VA("t2", [128, 4, 128], F32); hh = VA("hh", [128, 4, 128], F32)
    lo = VA("lo", [128, 4, 128], F32); ril = VA("ril", [128, 128], F32)
    catT = VA("catT", [128, 8, 128], BF16); cat_l = VA.res("cat_l"); cat_a = VA.res("cat_a")
    qkv = VA("qkv", [128, 768], F32); sqq = VA("sqq", [128, 640], F32); tmpq = VA("tmpq", [128, 640], F32)
    ssq = VA("ssq", [128, 10], F32); riq = VA("riq", [128, 10], F32)
    qn_bf = VA("qn_bf", [128, 512], BF16); kn = VA("kn", [128, 128], F32); kn_bf = VA("kn_bf", [128, 128], BF16)
    qT = VA("qT", [64, 8, 128], BF16)
    den = VA("den", [128, 8], F32); o_sb = VA("o_sb", [128, 512], F32)
    ssa = VA("ssa", [128, 1], F32); rsa = VA("rsa", [128, 1], F32); an_bf = VA("an_bf", [128, 512], BF16)
    ck_sb = VA("ck_sb", [128, 128], F32); cv_sb = VA("cv_sb", [128, 128], F32); ck_bf = VA("ck_bf", [128, 128], BF16)
    ss2 = VB("ss2", [128, 1], F32)
    xn2T = VB("xn2T", [128, 8, 128], BF16); pq_bf = VB("pq_bf", [128, 16, 128], BF16)
    S = VB("S", [128, 16, 128], F32); v16 = VB("v16", [128, 16, 16], F32); i16 = VB("i16", [128, 16, 16], U32)
    i16f = VB("i16f", [128, 16, 16], F32); big = VB("big", [128, 2048], F32)
    tmp = [VB("tmpa", [128, 128], F32), VB("tmpb", [128, 128], F32)]
    tmp2 = [VB("tmp2a", [128, 256], F32), VB("tmp2b", [128, 256], F32)]
    ts = VB("ts", [128, 8, 16], F32); tp = VB("tp", [128, 8, 16], U32)
    pA = VB("pA", [128, 128], U32); pB = VB("pB", [128, 128], U32); pAf = VB("pAf", [128, 128], F32); pBf = VB("pBf", [128, 128], F32)
    i1s = VB("i1s", [128, 128], F32); i2s = VB("i2s", [128, 128], F32); idxf = VB("idxf", [128, 128], F32)
    ge = VB("ge", [128, 128], F32); gs = VB("gs", [128, 8], F32)
    pq_r = [VB.res(f"pq{i}") for i in range(4)]; S_r = [VB.res(f"S{i}") for i in range(4)]
    v_r = [VB.res(f"v{i}") for i in range(16)]; i_r = [VB.res(f"i{i}") for i in range(16)]
    t_r = [VB.res(f"t{i}") for i in range(8)]; p_r = [VB.res(f"p{i}") for i in range(8)]

    def inherit(news, olds):
        d = {}
        for o in olds:
            toks = list(o.reads)
            if o.last_w is not None:
                toks.append(o.last_w)
            for t in toks:
                k = id(t[0])
                if k not in d or d[k][1] < t[1]:
                    d[k] = t
        for n_ in news:
            n_.reads = list(n_.reads) + list(d.values())

    tiles = []
    for n in range(16):
        tiles.append(dict(kind="p", n=n, nt=128, x=dap(xp, n * 128 * 1024, [[1024, 128], [1, 1024]]),
                          y=dap(yp, n * 128 * 1024, [[1024, 128], [1, 1024]]), first=(n == 0), last=(n == 15)))
    for s in range(2):
        tiles.append(dict(kind="s", s=s, nt=16, x=dap(xs, s * 16 * 1024, [[1024, 16], [1, 1024]]),
                          y=dap(ys, s * 16 * 1024, [[1024, 16], [1, 1024]]), first=True, last=True))

    F0 = P[0]

    def front(gi):
        T = tiles[gi]
        nt = T["nt"]
        xt = xtb[gi % 2]
        hb = h_sb[gi % 2]; x2 = xn2_bf[gi % 2]; ixT = idxT[gi % 2]; gtT = gT[gi % 2]
        C.dma("sp", xt[0:nt, :], T["x"], w=[xt], sbuf_side=xt)
        DBG["fstart"] = C.eng_time["sp"]
        samp = T["kind"] == "s"
        if samp:
            sp_, so_ = 0, 1
        else:
            so_ = T["n"] % 2
            sp_ = 1 - so_
        inherit(A_res, B_res)
        C.op("dve", lambda e: e.scalar_tensor_tensor(out=xn_bf[0:nt, :], in0=xt[0:nt, :], scalar=1.0, in1=xt[0:nt, :], op0=ALU.mult,
                                                     op1=ALU.mult, accum_out=ss1[0:nt, :]), r=[xt], w=[xn_bf, ss1])
        C.op("act", lambda e: e.activation(out=rs1[0:nt, :], in_=ss1[0:nt, :], func=AF.Sqrt, scale=1.0 / 1024, bias=EPS), r=[ss1], w=[rs1])
        C.op("dve", lambda e: e.reciprocal(out=rs1[0:nt, :], in_=rs1[0:nt, :]), r=[rs1], w=[rs1])
        C.op("act", lambda e: e.activation(out=xn_bf[0:nt, :], in_=xt[0:nt, :], func=AF.Copy, scale=rs1[0:nt, 0:1]),
             r=[xt, rs1], w=[xn_bf])
        for kc in range(8):
            C.op("pe", lambda e: e.transpose(F0.bf(0, 128, kc * 128, [[1, nt]]), xn_bf[0:nt, kc * 128:(kc + 1) * 128], identb[0:nt, 0:nt]),
                 r=[xn_bf, identb], w=[F0.ra])
        C.op("act", lambda e: e.activation(out=xnT[:, :, 0:nt], in_=F0.bf(0, 128, 0, [[128, 8], [1, nt]]), func=AF.Copy), r=[F0.ra], w=[xnT])
        for ct in range(8):
            hf = ct // 4
            for kc in range(8):
                C.op("pe", lambda e: e.matmul(F0.f(0, 128, hf * 512 + (ct % 4) * 128, [[1, nt]]), wi_bf[:, kc, ct * 128:(ct + 1) * 128],
                                              xnT[:, kc, 0:nt], start=(kc == 0), stop=(kc == 7)), r=[wi_r[kc], xnT], w=[F0.r(hf)])
        if samp:
            s = T["s"]
            C.op("pool", lambda e: e.tensor_copy(out=uext[:, :, 0:3], in_=vecT[:, :, R_SC + 3 * s:R_SC + 3 * s + 3]), r=[vecT], w=[uext])
            C.op("pool", lambda e: e.tensor_copy(out=hstate[:], in_=vecT[:, :, R_SL + s]), r=[vecT], w=[hstate])
        C.op("act", lambda e: e.activation(out=uext[:, :, 3:3 + nt], in_=F0.f(0, 128, 0, [[128, 4], [1, nt]]), func=AF.Copy), r=[F0.ra], w=[uext])
        C.op("act", lambda e: e.activation(out=gate[:, :, 0:nt], in_=F0.f(0, 128, 512, [[128, 4], [1, nt]]), func=AF.Copy), r=[F0.rb], w=[gate])
        for hf, (c0, c1) in enumerate(((1024, 1536), (1536, 1792))):
            for kc in range(8):
                C.op("pe", lambda e: e.matmul(F0.f(0, nt, hf * 512, [[1, c1 - c0]]), xnT[:, kc, 0:nt], wi_bf[:, kc, c0:c1],
                                              start=(kc == 0), stop=(kc == 7)), r=[wi_r[kc], xnT], w=[F0.r(hf)])
        C.op("act", lambda e: e.activation(out=qkv[0:nt, :], in_=F0.f(0, nt, 0, [[1, 768]]), func=AF.Copy), r=F0.both, w=[qkv])
        for ct in range(4):
            C.op("dve", lambda e: e.tensor_scalar(out=xc[:, ct, 0:nt], in0=uext[:, ct, 0:nt], scalar1=vecT[:, ct, R_CW:R_CW + 1],
                                                  scalar2=vecT[:, ct, R_CB:R_CB + 1], op0=ALU.mult, op1=ALU.add), r=[uext, vecT], w=[xc])
            for j in range(1, 4):
                C.op("dve", lambda e: e.scalar_tensor_tensor(out=xc[:, ct, 0:nt], in0=uext[:, ct, j:j + nt], scalar=vecT[:, ct, R_CW + j:R_CW + j + 1],
                                                             in1=xc[:, ct, 0:nt], op0=ALU.mult, op1=ALU.add), r=[uext, vecT, xc], w=[xc])
        C.op("act", lambda e: e.activation(out=xc_bf[:, :, 0:nt], in_=xc[:, :, 0:nt], func=AF.Copy), r=[xc], w=[xc_bf])
        if T["last"]:
            dstt, base = (ncp, 0) if not samp else (ncs, T["s"] * 1536)
            for ct in range(4):
                C.dma("sp", None, None, r=[uext], sbuf_side=uext, is_store=True, final=True,
                      fn=lambda e: e.dma_start(out=dap(dstt, base + ct * 128, [[1, 128], [512, 3]]), in_=uext[:, ct, nt:nt + 3],
                                               allow_slow_non_contiguous=True))
        else:
            C.op("pool", lambda e: e.tensor_copy(out=uext[:, :, 0:3], in_=uext[:, :, nt:nt + 3]), r=[uext], w=[uext])
        for ct in range(4):
            C.op("pe", lambda e: e.matmul(F0.f(0, 128, ct * 128, [[1, nt]]), wa_bd[:, ct, :], xc_bf[:, ct, 0:nt], start=True, stop=True),
                 r=[wa_bd, xc_bf], w=[F0.ra])
            C.op("pe", lambda e: e.matmul(F0.f(0, 128, 512 + ct * 128, [[1, nt]]), wi_bd[:, ct, :], xc_bf[:, ct, 0:nt], start=True, stop=True),
                 r=[wi_bd, xc_bf], w=[F0.rb])
        for ct in range(4):
            C.op("act", lambda e: e.activation(out=rr[:, ct, 0:nt], in_=F0.f(0, 128, ct * 128, [[1, nt]]), func=AF.Sigmoid,
                                               bias=vecT[:, ct, R_BA:R_BA + 1]), r=[F0.ra, vecT], w=[rr])
        for ct in range(4):
            C.op("act", lambda e: e.activation(out=ig[:, ct, 0:nt], in_=F0.f(0, 128, 512 + ct * 128, [[1, nt]]), func=AF.Sigmoid,
                                               bias=vecT[:, ct, R_BI:R_BI + 1]), r=[F0.rb, vecT], w=[ig])
        for ct in range(4):
            C.op("act", lambda e: e.activation(out=aa[:, ct, 0:nt], in_=rr[:, ct, 0:nt], func=AF.Exp, scale=clam[:, ct:ct + 1]), r=[rr, clam], w=[aa])
        for ct in range(4):
            C.op("act", lambda e: e.activation(out=t1[:, ct, 0:nt], in_=rr[:, ct, 0:nt], func=AF.Tanh, scale=nclam[:, ct:ct + 1]), r=[rr, nclam], w=[t1])
        C.op("pool", lambda e: e.tensor_tensor(out=t2[:, :, 0:nt], in0=aa[:, :, 0:nt], in1=aa[:, :, 0:nt], op=ALU.mult), r=[aa], w=[t2])
        C.op("dve", lambda e: e.scalar_tensor_tensor(out=t2[:, :, 0:nt], in0=t2[:, :, 0:nt], scalar=1.0, in1=t1[:, :, 0:nt], op0=ALU.add, op1=ALU.mult),
             r=[t2, t1], w=[t2])
        C.op("act", lambda e: e.activation(out=t2[:, :, 0:nt], in_=t2[:, :, 0:nt], func=AF.Sqrt), r=[t2], w=[t2])
        C.op("pool", lambda e: e.tensor_tensor(out=ig[:, :, 0:nt], in0=ig[:, :, 0:nt], in1=xc[:, :, 0:nt], op=ALU.mult), r=[ig, xc], w=[ig])
        C.op("pool", lambda e: e.tensor_tensor(out=ig[:, :, 0:nt], in0=ig[:, :, 0:nt], in1=t2[:, :, 0:nt], op=ALU.mult), r=[ig, t2], w=[ig])
        for ct in range(4):
            C.op("dve", lambda e: e.tensor_tensor_scan(out=hh[:, ct, 0:nt], data0=aa[:, ct, 0:nt], data1=ig[:, ct, 0:nt], initial=hstate[:, ct:ct + 1],
                                                       op0=ALU.mult, op1=ALU.add), r=[aa, ig, hstate], w=[hh])
        C.op("dve", lambda e: e.tensor_copy(out=hstate[:], in_=hh[:, :, nt - 1]), r=[hh], w=[hstate])
        if T["last"]:
            dstt, base = (nhp, 0) if not samp else (nhs, T["s"] * 512)
            C.dma("sp", None, None, r=[hstate], sbuf_side=hstate, is_store=True, final=True,
                  fn=lambda e: e.dma_start(out=dap(dstt, base, [[1, 128], [128, 4]]), in_=hstate[:], allow_slow_non_contiguous=True))
        C.op("act", lambda e: e.activation(out=t1[:, :, 0:nt], in_=gate[:, :, 0:nt], func=AF.Gelu_apprx_tanh), r=[gate], w=[t1])
        C.op("pool", lambda e: e.tensor_tensor(out=lo[:, :, 0:nt], in0=hh[:, :, 0:nt], in1=t1[:, :, 0:nt], op=ALU.mult), r=[hh, t1], w=[lo])
        C.op("pool", lambda e: e.tensor_tensor(out=t1[:, :, 0:nt], in0=lo[:, :, 0:nt], in1=lo[:, :, 0:nt], op=ALU.mult), r=[lo], w=[t1])
        for ct in range(4):
            C.op("pe", lambda e: e.matmul(F0.f(0, 128, 0, [[1, nt]]), ones_f[:], t1[:, ct, 0:nt], start=(ct == 0), stop=(ct == 3)),
                 r=[ones_f, t1], w=[F0.ra])
        C.op("act", lambda e: e.activation(out=ril[:, 0:nt], in_=F0.f(0, 128, 0, [[1, nt]]), func=AF.Sqrt, scale=1.0 / 512, bias=EPS), r=[F0.ra], w=[ril])
        C.op("dve", lambda e: e.reciprocal(out=ril[:, 0:nt], in_=ril[:, 0:nt]), r=[ril], w=[ril])
        C.op("pool", lambda e: e.tensor_tensor(out=catT[:, 0:4, 0:nt], in0=lo[:, :, 0:nt], in1=ril.pap(0, 128, 0, [[0, 4], [1, nt]]), op=ALU.mult),
             r=[lo, ril], w=[cat_l])
        C.op("pool", lambda e: e.tensor_tensor(out=sqq[0:nt, :], in0=qkv[0:nt, 0:640], in1=qkv[0:nt, 0:640], op=ALU.mult), r=[qkv], w=[sqq])
        C.op("dve", lambda e: e.tensor_reduce(out=ssq[0:nt, :], in_=sqq.pap(0, nt, 0, [[64, 10], [1, 64]]), axis=AX.X, op=ALU.add), r=[sqq], w=[ssq])
        C.op("act", lambda e: e.activation(out=riq[0:nt, :], in_=ssq[0:nt, :], func=AF.Sqrt, scale=1.0 / 64, bias=EPS), r=[ssq], w=[riq])
        C.op("dve", lambda e: e.reciprocal(out=riq[0:nt, :], in_=riq[0:nt, :]), r=[riq], w=[riq])
        C.op("pool", lambda e: e.tensor_tensor(out=tmpq.pap(0, nt, 0, [[64, 10], [1, 64]]), in0=qkv.pap(0, nt, 0, [[64, 10], [1, 64]]),
                                               in1=riq.pap(0, nt, 0, [[1, 10], [0, 64]]), op=ALU.mult), r=[qkv, riq], w=[tmpq])
        C.op("pool", lambda e: e.tensor_tensor(out=qn_bf.pap(0, nt, 0, [[64, 8], [1, 64]]), in0=tmpq.pap(0, nt, 0, [[64, 8], [1, 64]]),
                                               in1=qgrep.pap(0, nt, 0, [[0, 8], [1, 64]]), op=ALU.mult), r=[tmpq, qgrep], w=[qn_bf])
        C.op("dve", lambda e: e.tensor_tensor(out=kn.pap(0, nt, 0, [[64, 2], [1, 64]]), in0=tmpq.pap(0, nt, 512, [[64, 2], [1, 64]]),
                                              in1=kgrep.pap(0, nt, 0, [[0, 2], [1, 64]]), op=ALU.mult), r=[tmpq, kgrep], w=[kn])
        C.op("act", lambda e: e.activation(out=kn_bf[0:nt, :], in_=kn[0:nt, :], func=AF.Copy), r=[kn], w=[kn_bf])
        C.op("act", lambda e: e.activation(out=Vaug[so_].pap(0, nt, 0, [[65, 2], [1, 64]]), in_=qkv.pap(0, nt, 640, [[64, 2], [1, 64]]), func=AF.Copy),
             r=[qkv], w=[Vaug[so_]])
        if samp:
            s = T["s"]
            C.dma("sp", ck_sb[:], dap(ck, s * 16384, [[128, 128], [1, 128]]), w=[ck_sb], sbuf_side=ck_sb)
            C.dma("sp", cv_sb[:], dap(cv, s * 16384, [[128, 128], [1, 128]]), w=[cv_sb], sbuf_side=cv_sb)
            C.op("pool", lambda e: e.tensor_copy(out=ck_bf[:], in_=ck_sb[:]), r=[ck_sb], w=[ck_bf])
            C.op("pool", lambda e: e.tensor_copy(out=Vaug[sp_].pap(0, 128, 0, [[65, 2], [1, 64]]), in_=cv_sb.pap(0, 128, 0, [[64, 2], [1, 64]])),
                 r=[cv_sb], w=[Vaug[sp_]])
            for g in range(2):
                C.op("pe", lambda e: e.transpose(F0.bf(0, 64, 1024 + 256 + g * 128, [[1, 128]]), ck_bf[:, g * 64:(g + 1) * 64], identb[:]),
                     r=[ck_bf, identb], w=[F0.rb])
            C.op("dve", lambda e: e.tensor_copy(out=kTb[sp_][:], in_=F0.bf(0, 64, 1024 + 256, [[128, 2], [1, 128]])), r=[F0.rb], w=[kTb[sp_]])
            C.dma("sp", dap(nks, s * 16384, [[128, 112], [1, 128]]), ck_sb[16:128, :], r=[ck_sb], sbuf_side=ck_sb, is_store=True, final=True)
            C.dma("sp", dap(nvs, s * 16384, [[128, 112], [1, 128]]), cv_sb[16:128, :], r=[cv_sb], sbuf_side=cv_sb, is_store=True, final=True)
            C.dma("sp", dap(nks, s * 16384 + 112 * 128, [[128, 16], [1, 128]]), kn[0:16, :], r=[kn], sbuf_side=kn, is_store=True, final=True)
            C.dma("sp", dap(nvs, s * 16384 + 112 * 128, [[128, 16], [1, 128]]), qkv[0:16, 640:768], r=[qkv], sbuf_side=qkv, is_store=True, final=True)
        elif T["last"]:
            C.dma("sp", nkp.ap(), kn[:, :], r=[kn], sbuf_side=kn, is_store=True, final=True)
            C.dma("sp", nvp.ap(), qkv[:, 640:768], r=[qkv], sbuf_side=qkv, is_store=True, final=True)
        for h in range(8):
            C.op("pe", lambda e: e.transpose(F0.bf(0, 64, h * 128, [[1, nt]]), qn_bf[0:nt, h * 64:(h + 1) * 64], identb[0:nt, 0:nt]),
                 r=[qn_bf, identb], w=[F0.ra])
        for g in range(2):
            C.op("pe", lambda e: e.transpose(F0.bf(0, 64, 1024 + g * 128, [[1, nt]]), kn_bf[0:nt, g * 64:(g + 1) * 64], identb[0:nt, 0:nt]),
                 r=[kn_bf, identb], w=[F0.rb])
        C.op("act", lambda e: e.activation(out=qT[0:64, :, 0:nt], in_=F0.bf(0, 64, 0, [[128, 8], [1, nt]]), func=AF.Copy), r=[F0.ra], w=[qT])
        C.op("dve", lambda e: e.tensor_copy(out=kTb[so_][:, :, 0:nt], in_=F0.bf(0, 64, 1024, [[128, 2], [1, nt]])), r=[F0.rb], w=[kTb[so_]])
        if samp:
            blocks = [("prev", sp_, 128, [(0, 128, 0, 16)]), ("own", so_, 16, [(0, 16, 0, 16)])]
        else:
            blocks = []
            if not T["first"]:
                blocks.append(("prev", sp_, 128, [(0, 128, 0, 64), (64, 128, 64, 128)]))
            blocks.append(("own", so_, 128, [(0, 64, 0, 64), (0, 128, 64, 128)]))
        k_ = 0
        for g in range(2):
            for bi, (nm, slot, nk, regions) in enumerate(blocks):
                hf = k_ % 2
                k_ += 1
                C.op("pe", lambda e: e.matmul(F0.f(0, nk, hf * 512, [[128, 4], [1, nt]]), kTb[slot][:, g, 0:nk], qT[0:64, 4 * g:4 * g + 4, 0:nt],
                                              start=True, stop=True), r=[kTb[slot], qT], w=[F0.r(hf)])
                for (k0, k1, q0, q1) in regions:
                    C.op("act", lambda e: e.activation(out=PT[(g, nm)].pap(k0, k1 - k0, q0, [[128, 4], [1, q1 - q0]]),
                                                       in_=F0.f(k0, k1 - k0, hf * 512 + q0, [[128, 4], [1, q1 - q0]]), func=AF.Exp, scale=0.125),
                         r=[F0.r(hf)], w=[PT[(g, nm)]])
        for h in range(8):
            g, h4 = h // 4, h % 4
            for bi, (nm, slot, nk, regions) in enumerate(blocks):
                C.op("pe", lambda e: e.matmul(F0.f(0, nt, g * 512 + h4 * 65, [[1, 65]]), PT[(g, nm)].pap(0, nk, h4 * 128, [[1, nt]]),
                                              Vaug[slot].pap(0, nk, g * 65, [[1, 65]]), start=(bi == 0), stop=(bi == len(blocks) - 1)),
                     r=[PT[(g, nm)], Vaug[slot]], w=[F0.r(g)])
        C.op("dve", lambda e: e.tensor_tensor(out=den.pap(0, nt, 0, [[4, 2], [1, 4]]), in0=F0.f(0, nt, 64, [[512, 2], [65, 4]]),
                                              in1=esink.pap(0, nt, 0, [[4, 2], [1, 4]]), op=ALU.add), r=F0.both + [esink], w=[den])
        C.op("dve", lambda e: e.reciprocal(out=den[0:nt, :], in_=den[0:nt, :]), r=[den], w=[den])
        C.op("dve", lambda e: e.tensor_tensor(out=o_sb.pap(0, nt, 0, [[256, 2], [64, 4], [1, 64]]), in0=F0.f(0, nt, 0, [[512, 2], [65, 4], [1, 64]]),
                                              in1=den.pap(0, nt, 0, [[4, 2], [1, 4], [0, 64]]), op=ALU.mult), r=F0.both + [den], w=[o_sb])
        C.op("dve", lambda e: e.scalar_tensor_tensor(out=an_bf[0:nt, :], in0=o_sb[0:nt, :], scalar=1.0, in1=o_sb[0:nt, :], op0=ALU.mult, op1=ALU.mult,
                                                     accum_out=ssa[0:nt, :]), r=[o_sb], w=[an_bf, ssa])
        C.op("act", lambda e: e.activation(out=rsa[0:nt, :], in_=ssa[0:nt, :], func=AF.Sqrt, scale=1.0 / 512, bias=EPS), r=[ssa], w=[rsa])
        C.op("dve", lambda e: e.reciprocal(out=rsa[0:nt, :], in_=rsa[0:nt, :]), r=[rsa], w=[rsa])
        C.op("act", lambda e: e.activation(out=an_bf[0:nt, :], in_=o_sb[0:nt, :], func=AF.Copy, scale=rsa[0:nt, 0:1]),
             r=[o_sb, rsa], w=[an_bf])
        for j in range(4):
            C.op("pe", lambda e: e.transpose(F0.bf(0, 128, j * 128, [[1, nt]]), an_bf[0:nt, j * 128:(j + 1) * 128], identb[0:nt, 0:nt]),
                 r=[an_bf, identb], w=[F0.ra])
        C.op("act", lambda e: e.activation(out=catT[:, 4:8, 0:nt], in_=F0.bf(0, 128, 0, [[128, 4], [1, nt]]), func=AF.Copy), r=[F0.ra], w=[cat_a])
        for hf in range(2):
            for c in range(8):
                C.op("pe", lambda e: e.matmul(F0.f(0, nt, hf * 512, [[1, 512]]), catT[:, c, 0:nt], wo_bf[:, c, hf * 512:(hf + 1) * 512],
                                              start=(c == 0), stop=(c == 7)), r=[cat_l if c < 4 else cat_a, wo_r[c]], w=[F0.r(hf)])
        C.op("dve", lambda e: e.tensor_tensor(out=hb[0:nt, :], in0=xt[0:nt, :], in1=F0.f(0, nt, 0, [[1, 1024]]), op=ALU.add), r=[xt] + F0.both, w=[hb])
        inherit(B_res, A_res)
        C.op("dve", lambda e: e.scalar_tensor_tensor(out=x2[0:nt, :], in0=hb[0:nt, :], scalar=1.0, in1=hb[0:nt, :], op0=ALU.mult, op1=ALU.mult,
                                                     accum_out=ss2[0:nt, :]), r=[hb], w=[x2, ss2])
        C.op("act", lambda e: e.activation(out=ss2[0:nt, :], in_=ss2[0:nt, :], func=AF.Sqrt, scale=1.0 / 1024, bias=EPS), r=[ss2], w=[ss2])
        C.op("dve", lambda e: e.reciprocal(out=ss2[0:nt, :], in_=ss2[0:nt, :]), r=[ss2], w=[ss2])
        C.op("act", lambda e: e.activation(out=x2[0:nt, :], in_=hb[0:nt, :], func=AF.Copy, scale=ss2[0:nt, 0:1]),
             r=[hb, ss2], w=[x2])
        for kc in range(8):
            C.op("pe", lambda e: e.transpose(F0.bf(0, 128, kc * 128, [[1, nt]]), x2[0:nt, kc * 128:(kc + 1) * 128], identb[0:nt, 0:nt]),
                 r=[x2, identb], w=[F0.ra])
        C.op("act", lambda e: e.activation(out=xn2T[:, :, 0:nt], in_=F0.bf(0, 128, 0, [[128, 8], [1, nt]]), func=AF.Copy), r=[F0.ra], w=[xn2T])
        for b4 in range(4):
            hf = b4 % 2
            for g4 in range(4):
                grp = b4 * 4 + g4
                for kc in range(8):
                    C.op("pe", lambda e: e.matmul(F0.f(0, 128, hf * 512 + g4 * 128, [[1, nt]]), wq_bf[:, kc, grp * 128:(grp + 1) * 128], xn2T[:, kc, 0:nt],
                                                  start=(kc == 0), stop=(kc == 7)), r=[wq_r[kc], xn2T], w=[F0.r(hf)])
            if b4 % 2 == 0:
                C.op("act", lambda e: e.activation(out=pq_bf[:, 4 * b4:4 * b4 + 4, 0:nt], in_=F0.f(0, 128, hf * 512, [[128, 4], [1, nt]]), func=AF.Copy),
                     r=[F0.r(hf)], w=[pq_r[b4]])
            else:
                C.op("dve", lambda e: e.tensor_copy(out=pq_bf[:, 4 * b4:4 * b4 + 4, 0:nt], in_=F0.f(0, 128, hf * 512, [[128, 4], [1, nt]])),
                     r=[F0.r(hf)], w=[pq_r[b4]])
        for b4 in range(4):
            hf = b4 % 2
            for g4 in range(4):
                grp = b4 * 4 + g4
                C.op("pe", lambda e: e.matmul(F0.f(0, nt, hf * 512 + g4 * 128, [[1, 128]]), pq_bf[:, grp, 0:nt], skT[grp % 2][:], start=True, stop=True),
                     r=[pq_r[b4], skT[grp % 2]], w=[F0.r(hf)])
            if b4 % 2 == 0:
                C.op("act", lambda e: e.activation(out=S.pap(0, nt, b4 * 512, [[1, 512]]), in_=F0.f(0, nt, hf * 512, [[1, 512]]), func=AF.Copy), r=[F0.r(hf)], w=[S_r[b4]])
            else:
                C.op("dve", lambda e: e.tensor_copy(out=S.pap(0, nt, b4 * 512, [[1, 512]]), in_=F0.f(0, nt, hf * 512, [[1, 512]])), r=[F0.r(hf)], w=[S_r[b4]])
        C.mark()
        for grp in range(16):
            tb_ = tmp[grp % 2]; sr = S_r[grp // 4]
            C.op("dve", lambda e: e.max(out=v16[0:nt, grp, 0:8], in_=S[0:nt, grp, :]), r=[sr], w=[v_r[grp]])
            C.op("dve", lambda e: e.max_index(out=i16[0:nt, grp, 0:8], in_max=v16[0:nt, grp, 0:8], in_values=S[0:nt, grp, :]), r=[sr, v_r[grp]], w=[i_r[grp]])
            C.op("dve", lambda e: e.match_replace(out=tb_[0:nt, :], in_to_replace=v16[0:nt, grp, 0:8], in_values=S[0:nt, grp, :], imm_value=NEG), r=[sr, v_r[grp]], w=[tb_])
            C.op("dve", lambda e: e.max(out=v16[0:nt, grp, 8:16], in_=tb_[0:nt, :]), r=[tb_], w=[v_r[grp]])
            C.op("dve", lambda e: e.max_index(out=i16[0:nt, grp, 8:16], in_max=v16[0:nt, grp, 8:16], in_values=tb_[0:nt, :]), r=[tb_, v_r[grp]], w=[i_r[grp]])
        C.op("act", lambda e: e.activation(out=i16f[0:nt, :, :], in_=i16[0:nt, :, :], func=AF.Copy), r=i_r, w=[i16f])
        C.op("dve", lambda e: e.tensor_tensor(out=big.pap(0, nt, 0, [[256, 8], [16, 16], [1, 16]]), in0=v16.pap(0, nt, 0, [[32, 8], [1, 16], [0, 16]]),
                                               in1=v16.pap(0, nt, 16, [[32, 8], [0, 16], [1, 16]]), op=ALU.add), r=v_r, w=[big])
        for h in range(8):
            tb_ = tmp2[h % 2]
            cand = big[0:nt, h * 256:(h + 1) * 256]
            C.op("dve", lambda e: e.max(out=ts[0:nt, h, 0:8], in_=cand), r=[big], w=[t_r[h]])
            C.op("dve", lambda e: e.max_index(out=tp[0:nt, h, 0:8], in_max=ts[0:nt, h, 0:8], in_values=cand), r=[big, t_r[h]], w=[p_r[h]])
            C.op("dve", lambda e: e.match_replace(out=tb_[0:nt, :], in_to_replace=ts[0:nt, h, 0:8], in_values=cand, imm_value=NEG), r=[big, t_r[h]], w=[tb_])
            C.op("dve", lambda e: e.max(out=ts[0:nt, h, 8:16], in_=tb_[0:nt, :]), r=[tb_], w=[t_r[h]])
            C.op("dve", lambda e: e.max_index(out=tp[0:nt, h, 8:16], in_max=ts[0:nt, h, 8:16], in_values=tb_[0:nt, :]), r=[tb_, t_r[h]], w=[p_r[h]])
        C.op("dve", lambda e: e.tensor_scalar(out=pA[0:nt, :], in0=tp.pap(0, nt, 0, [[1, 128]]), scalar1=4, scalar2=None, op0=ALU.logical_shift_right), r=p_r, w=[pA])
        C.op("dve", lambda e: e.tensor_scalar(out=pB[0:nt, :], in0=tp.pap(0, nt, 0, [[1, 128]]), scalar1=15, scalar2=None, op0=ALU.bitwise_and), r=p_r, w=[pB])
        C.op("act", lambda e: e.activation(out=pAf[0:nt, :], in_=pA[0:nt, :], func=AF.Copy), r=[pA], w=[pAf])
        C.op("act", lambda e: e.activation(out=pBf[0:nt, :], in_=pB[0:nt, :], func=AF.Copy), r=[pB], w=[pBf])
        for (pf, ioff, dst) in ((pAf, 0, i1s), (pBf, 16, i2s)):
            C.op("dve", lambda e: e.tensor_tensor(out=big.pap(0, nt, 0, [[256, 8], [16, 16], [1, 16]]), in0=iota16.pap(0, nt, 0, [[0, 8], [0, 16], [1, 16]]),
                                                  in1=pf.pap(0, nt, 0, [[16, 8], [1, 16], [0, 16]]), op=ALU.is_equal), r=[iota16, pf], w=[big])
            C.op("dve", lambda e: e.tensor_tensor(out=big.pap(0, nt, 0, [[256, 8], [16, 16], [1, 16]]), in0=big.pap(0, nt, 0, [[256, 8], [16, 16], [1, 16]]),
                                                   in1=i16f.pap(0, nt, ioff, [[32, 8], [0, 16], [1, 16]]), op=ALU.mult), r=[big, i16f], w=[big])
            C.op("dve", lambda e: e.tensor_reduce(out=dst[0:nt, :], in_=big.pap(0, nt, 0, [[16, 128], [1, 16]]), axis=AX.X, op=ALU.add), r=[big], w=[dst])
        C.op("dve", lambda e: e.scalar_tensor_tensor(out=idxf[0:nt, :], in0=i1s[0:nt, :], scalar=128.0, in1=i2s[0:nt, :], op0=ALU.mult, op1=ALU.add),
             r=[i1s, i2s], w=[idxf])
        C.op("dve", lambda e: e.tensor_tensor(out=ge.pap(0, nt, 0, [[16, 8], [1, 16]]), in0=ts.pap(0, nt, 0, [[16, 8], [1, 16]]),
                                               in1=ts.pap(0, nt, 0, [[16, 8], [0, 16]]), op=ALU.subtract), r=t_r, w=[ge])
        C.op("act", lambda e: e.activation(out=ge[0:nt, :], in_=ge[0:nt, :], func=AF.Exp), r=[ge], w=[ge])
        C.op("dve", lambda e: e.tensor_reduce(out=gs[0:nt, :], in_=ge.pap(0, nt, 0, [[16, 8], [1, 16]]), axis=AX.X, op=ALU.add), r=[ge], w=[gs])
        C.op("dve", lambda e: e.reciprocal(out=gs[0:nt, :], in_=gs[0:nt, :]), r=[gs], w=[gs])
        C.op("dve", lambda e: e.tensor_tensor(out=ge.pap(0, nt, 0, [[16, 8], [1, 16]]), in0=ge.pap(0, nt, 0, [[16, 8], [1, 16]]),
                                               in1=gs.pap(0, nt, 0, [[1, 8], [0, 16]]), op=ALU.mult), r=[ge, gs], w=[ge])
        C.op("pe", lambda e: e.transpose(F0.f(0, 128, 0, [[1, nt]]), idxf[0:nt, :], identf[0:nt, 0:nt]), r=[idxf, identf], w=[F0.ra])
        C.op("pe", lambda e: e.transpose(F0.f(0, 128, 512, [[1, nt]]), ge[0:nt, :], identf[0:nt, 0:nt]), r=[ge, identf], w=[F0.rb])
        C.op("dve", lambda e: e.tensor_copy(out=ixT[:, 0:nt], in_=F0.f(0, 128, 0, [[1, nt]])), r=[F0.ra], w=[ixT])
        C.op("act", lambda e: e.activation(out=gtT[:, 0:nt], in_=F0.f(0, 128, 512, [[1, nt]]), func=AF.Copy), r=[F0.rb], w=[gtT])
        DBG["fend"] = C.eng_time["act"]

    bcP = [P[1], P[2]]

    def back(gi, stepper, kstep):
        T = tiles[gi]
        nt = T["nt"]
        xt = xtb[gi % 2]
        hb = h_sb[gi % 2]; x2 = xn2_bf[gi % 2]; ixT = idxT[gi % 2]; gtT = gT[gi % 2]

        def gather(t):
            uv = UV[t % NS]
            C.dma("pool", None, None, r=[ixT], w=[uv], sbuf_side=uv,
                  fn=lambda e: e.indirect_dma_start(out=uv[:], out_offset=None, in_=tab.ap(),
                                                    in_offset=bass.IndirectOffsetOnAxis(ap=ixT[:, t:t + 1], axis=0)))

        credit = {k_: 0.0 for k_ in BUDGET}
        tcur = [0]

        def pump():
            if stepper is None:
                return
            while not stepper.finished:
                ne = stepper.next_e
                if ne in C.eng_time:
                    if credit.get(ne, 1.0) <= 0.0:
                        break
                    r_, w_ = stepper.next_rw
                    ready = C.pred_ready(ne, r_, w_)
                    if ready > max(C.eng_time[ne], C.front_time) + SLACK:
                        break
                    t_before = max(C.eng_time[ne], ready)
                    stepper.step(1)
                    if ne in credit:
                        credit[ne] -= max(0.05, C.eng_time[ne] - t_before)
                else:
                    stepper.step(1)

        f_base = C.ninst_f

        def refill():
            mult = 1.0
            if stepper is not None and NF[0] > 0:
                exp_frac = min(1.0, (tcur[0] + 1) / (FIN_FRAC * nt))
                act_frac = (C.ninst_f - f_base) / float(NF[0])
                if act_frac < exp_frac:
                    mult = BOOST
            for e_ in credit:
                credit[e_] = min(credit[e_] + mult * BUDGET[e_], 3.0 * mult * BUDGET[e_])

        C.pump = pump if stepper is not None else None
        for t in range(min(DPF, nt)):
            gather(t)
        for t in range(nt + LAG):
            if t < nt:
                if t + DPF < nt:
                    gather(t + DPF)
                pp = bcP[t % 2]; uv = UV[t % NS]
                for hf in range(2):
                    C.op("pe", lambda e: e.matmul(pp.f(0, 128, hf * 512, [[1, 512]]), identb.pap(0, nt, t, [[0, 128]]), x2[0:nt, hf * 512:(hf + 1) * 512],
                                                  start=True, stop=True), r=[identb, x2], w=[pp.r(hf)])
                C.op("dve", lambda e: e.scalar_tensor_tensor(out=uv[:, 0:1024], in0=uv[:, 0:1024], scalar=1.0, in1=pp.f(0, 128, 0, [[1, 1024]]), op0=ALU.mult, op1=ALU.mult,
                                                             accum_out=apre[:, t:t + 1]), r=[uv] + pp.both, w=[uv, a_r[t % 8]])
                C.op("act", lambda e: e.activation(out=gel[:, t:t + 1], in_=apre[:, t:t + 1], func=AF.Gelu_apprx_tanh), r=[a_r[t % 8]], w=[g_r[t % 8]])
                wd = WDr[t % NW]
                if WD_ON_ACT:
                    C.op("act", lambda e: e.activation(out=gel[:, t:t + 1], in_=gel[:, t:t + 1], func=AF.Copy, scale=gtT[:, t:t + 1]), r=[g_r[t % 8], gtT], w=[g_r[t % 8]])
                    C.op("act", lambda e: e.activation(out=wd[:, 0:nt], in_=Zc[:, 127 - t:127 - t + nt], func=AF.Copy, scale=gel[:, t:t + 1]), r=[Zc, g_r[t % 8]], w=[wd])
                else:
                    C.op("dve", lambda e: e.tensor_scalar(out=wd[:, 0:nt], in0=Zc[:, 127 - t:127 - t + nt], scalar1=gel[:, t:t + 1], scalar2=gtT[:, t:t + 1],
                                                          op0=ALU.mult, op1=ALU.mult), r=[Zc, g_r[t % 8], gtT], w=[wd])
            tcur[0] = t
            refill()
            if DEBUG and gi == 2 and t % 8 == 0:
                print("      tok", t, {k_: round(v_, 1) for k_, v_ in C.eng_time.items()}, "front", round(C.front_time, 1), "dma_free", round(C.dma_free, 1), "F done" if (stepper is None or stepper.finished) else "F next " + str(stepper.next_e))
            tv = t - LAG
            if 0 <= tv < nt:
                uv = UV[tv % NS]; wd = WDr[tv % NW]
                for hf in range(2):
                    C.op("pe", lambda e: e.matmul(P[3].f(0, nt, hf * 512, [[1, 512]]), wd[:, 0:nt], uv[:, 1024 + hf * 512:1024 + (hf + 1) * 512],
                                                  start=(tv == 0), stop=(tv == nt - 1)), r=[wd, uv], w=[P[3].r(hf)])
        C.op("dve", lambda e: e.tensor_tensor(out=xt[0:nt, :], in0=hb[0:nt, :], in1=P[3].f(0, nt, 0, [[1, 1024]]), op=ALU.add), r=[hb] + P[3].both, w=[xt])
        C.dma("sp", T["y"], xt[0:nt, :], r=[xt], sbuf_side=xt, is_store=True, final=True)
        C.pump = None

    NF = [0]
    if True:
        NSL = 4
        assert NS >= 12
        g2rep = h_sb[1]
        C.dma("sp", g2rep[:], dap(ln2, 0, [[0, 128], [1, 1024]]), w=[g2rep], sbuf_side=g2rep)
        su = [VBuf(uvarena, s_ * 3072, [128, 1024], F32, f"su{s_}") for s_ in range(NSL)]
        sv = [VBuf(uvarena, s_ * 3072 + 1024, [128, 1024], F32, f"sv{s_}") for s_ in range(NSL)]
        tb = [VBuf(uvarena, s_ * 3072 + 2048, [128, 2048], BF16, f"tb{s_}") for s_ in range(NSL)]
        tbu = [Res("tbu") for _ in range(NSL)]; tbv = [Res("tbv") for _ in range(NSL)]
        stp0 = Stepper(C, lambda: front(0))
        for c in range(128 + 2):
            if c < 128:
                s_ = c % NSL
                C.dma("sp", su[s_][:], dap(pu, c * 128 * 1024, [[1024, 128], [1, 1024]]), w=[su[s_]], sbuf_side=su[s_])
                C.dma("sp", sv[s_][:], dap(pv, c * 128 * 1024, [[1024, 128], [1, 1024]]), w=[sv[s_]], sbuf_side=sv[s_])
                C.op("act", lambda e: e.activation(out=tb[s_][:, 1024:2048], in_=sv[s_][:], func=AF.Copy), r=[sv[s_]], w=[tbv[s_]])
                en = "dve" if c % 2 == 0 else "pool"
                C.op(en, lambda e: e.tensor_tensor(out=tb[s_][:, 0:1024], in0=su[s_][:], in1=g2rep[:], op=ALU.mult), r=[su[s_], g2rep], w=[tbu[s_]])
            cs = c - 2
            if cs >= 0:
                s2 = cs % NSL
                C.dma("sp", dap(tab, cs * 128 * 2048, [[2048, 128], [1, 2048]]), tb[s2][:], r=[tbu[s2], tbv[s2]], sbuf_side=tb[s2], is_store=True, temp_store=True)
            if c >= 4:
                stp0.step(K0STEP)
        stp0.drain()
        NF[0] = C.ninst_f
        barrier()

    for gi in range(len(tiles)):
        stp = None
        if gi + 1 < len(tiles):
            stp = Stepper(C, lambda: front(gi + 1))
            if MARKMODE:
                stp.run_to_mark()
        n0 = C.ninst
        back(gi, stp, KSTEP)
        n1 = C.ninst
        if stp is not None:
            stp.drain()
        if DEBUG:
            print("   F(gi+1) ended at model t=%.1f ; F started at %.1f" % (DBG.get("fend", 0), DBG.get("fstart", 0)))
            print("tile", gi, "model time", {k_: round(v_, 1) for k_, v_ in C.eng_time.items()}, "instr in back(incl F)", n1 - n0, "drained", C.ninst - n1)
    C.finish()
    return nc


KSTEP = 7
MARKMODE = 0
NSLOT, NDPF, NLAG = 12, 6, 5
WD_ON_ACT = 1
K0STEP = 7
FIN_FRAC = 0.8
BOOST = 2.0
DEBUG = 0
DBG = {}
SLACK = 0.15
BUDGET = {"dve": 0.7, "pe": 0.6, "act": 1.0, "pool": 0.5, "sp": 1.0}
COST = {"dve": 0.33, "pe": 0.15, "act": 0.4, "pool": 1.0}
SKIPRAW = 0
_CACHE = {}


def kernel(x_prompt, x_sample, cache_attn_k, cache_attn_v, state_conv, state_lru,
           ln1_g, w_in, conv_w, conv_b, lru_wa, lru_ba, lru_wi, lru_bi, lru_lambda,
           q_norm_g, k_norm_g, attn_sinks, g_lru_out, g_attn_out, w_out, ln2_g,
           peer_w_query, peer_sub_keys1, peer_sub_keys2, peer_u, peer_v):
    f = lambda a: np.ascontiguousarray(np.asarray(a, dtype=np.float32))
    x_prompt = f(x_prompt); x_sample = f(x_sample)
    if "nc" not in _CACHE:
        _CACHE["nc"] = build_program()
    nc = _CACHE["nc"]
    shared = dict(
        ln1=f(ln1_g).reshape(1, 1024), ln2=f(ln2_g).reshape(1, 1024),
        qg=f(q_norm_g).reshape(1, 64), kg=f(k_norm_g).reshape(1, 64), snk=f(attn_sinks).reshape(1, 8),
        w_in=f(w_in).reshape(1024, 1792), w_out=f(w_out).reshape(1024, 1024), w_q=f(peer_w_query).reshape(1024, 2048),
        wa=f(lru_wa).reshape(8, 64, 64), wi=f(lru_wi).reshape(8, 64, 64),
        sk1=f(peer_sub_keys1).reshape(128, 128), sk2=f(peer_sub_keys2).reshape(128, 128),
        pu=f(peer_u).reshape(16384, 1024), pv=f(peer_v).reshape(16384, 1024),
    )
    common_rows = [f(conv_w).reshape(4, 512), f(conv_b).reshape(1, 512), f(lru_ba).reshape(1, 512), f(lru_bi).reshape(1, 512),
                   f(lru_lambda).reshape(1, 512), f(g_lru_out).reshape(1, 512)]
    ga_row = f(g_attn_out).reshape(1, 512)
    sc = f(state_conv).reshape(16, 3, 512); sl = f(state_lru).reshape(16, 512)
    ckf = f(cache_attn_k).reshape(16, 128, 128); cvf = f(cache_attn_v).reshape(16, 128, 128)
    in_maps = []
    for c in range(8):
        s0, s1 = 2 * c, 2 * c + 1
        v512 = np.concatenate(common_rows + [sc[s0], sc[s1], sl[s0:s0 + 1], sl[s1:s1 + 1], ga_row], axis=0)
        m = dict(shared)
        m.update(xp=x_prompt[c], xs=np.ascontiguousarray(x_sample[s0:s1 + 1].reshape(32, 1024)),
                 ck=np.ascontiguousarray(ckf[s0:s1 + 1]), cv=np.ascontiguousarray(cvf[s0:s1 + 1]), v512=np.ascontiguousarray(v512))
        in_maps.append(m)
    res = run_bass_kernel_spmd(nc, in_maps, core_ids=list(range(8)))
    R = res.results
    y_p = np.stack([R[c]["yp"] for c in range(8)], 0).reshape(8, 2048, 1024)
    y_s = np.concatenate([R[c]["ys"] for c in range(8)], 0).reshape(16, 16, 1024)
    nk_p = np.stack([R[c]["nkp"] for c in range(8)], 0).reshape(1, 8, 128, 2, 64)
    nv_p = np.stack([R[c]["nvp"] for c in range(8)], 0).reshape(1, 8, 128, 2, 64)
    nc_p = np.stack([R[c]["ncp"] for c in range(8)], 0).reshape(1, 8, 3, 512)
    nh_p = np.stack([R[c]["nhp"] for c in range(8)], 0).reshape(1, 8, 512)
    nk_s = np.concatenate([R[c]["nks"] for c in range(8)], 0).reshape(1, 16, 128, 2, 64)
    nv_s = np.concatenate([R[c]["nvs"] for c in range(8)], 0).reshape(1, 16, 128, 2, 64)
    nc_s = np.concatenate([R[c]["ncs"] for c in range(8)], 0).reshape(1, 16, 3, 512)
    nh_s = np.concatenate([R[c]["nhs"] for c in range(8)], 0).reshape(1, 16, 512)
    return (y_p, y_s, nk_p, nv_p, nc_p, nh_p, nk_s, nv_s, nc_s, nh_s)
```
